# Optimizing a Trainium2 kernel written in Bass

```python
import jax, jax.numpy as jnp
from jax import lax
import numpy as np

D_MODEL = 2048
BATCH = 8
SEQ = 2048
DEPTH = 2

N_A_LAYERS = DEPTH // 2
N_B_LAYERS = DEPTH - N_A_LAYERS

RWKV_HEAD_DIM = 64
RWKV_HEADS = D_MODEL // RWKV_HEAD_DIM
DECAY_LORA = max(32, int(round(1.8 * D_MODEL ** 0.5 / 32)) * 32)
AAA_LORA = max(32, int(round(1.8 * D_MODEL ** 0.5 / 32)) * 32)
GATE_LORA = max(32, int(round(0.6 * D_MODEL ** 0.8 / 32)) * 32)
N_MIX = 6
GN_EPS = 64e-5
L2_EPS = 1e-12

HEAD_DIM = 64
N_Q_HEADS = D_MODEL // HEAD_DIM
N_KV_HEADS = max(1, N_Q_HEADS // 8)
Q_PER_KV = N_Q_HEADS // N_KV_HEADS
WINDOW = 128
BLOCK = 128
ROPE_THETA = 10000.0

D_FF = 4 * D_MODEL
RMS_EPS = 1e-6

kernel_name = "yoco_rwkv7_swa_sink_hybrid"


def rms_norm(x, gain):
    xf = x.astype(jnp.float32)
    y = xf * lax.rsqrt(jnp.mean(xf * xf, axis=-1, keepdims=True) + RMS_EPS)
    return (y * gain.astype(jnp.float32)).astype(x.dtype)


def rope(x, pos):
    half = x.shape[-1] // 2
    inv_freq = jnp.power(ROPE_THETA, -jnp.arange(half, dtype=jnp.float32) / half)
    ang = pos.astype(jnp.float32)[:, None] * inv_freq[None, :]
    cos = jnp.cos(ang)[None, :, None, :]
    sin = jnp.sin(ang)[None, :, None, :]
    xf = x.astype(jnp.float32)
    x1, x2 = xf[..., :half], xf[..., half:]
    return jnp.concatenate([x1 * cos - x2 * sin, x2 * cos + x1 * sin], axis=-1).astype(x.dtype)


def squared_relu_mlp(xn, w_up, w_down):
    return jnp.square(jax.nn.relu(xn @ w_up)) @ w_down


def wkv7_scan(r, decay, k, v, a, b):
    bsz, _, h, n = r.shape

    def step(S, inp):
        r_t, w_t, k_t, v_t, a_t, b_t = inp
        sa = jnp.einsum('bhvk,bhk->bhv', S, a_t)
        S = S * w_t[:, :, None, :] + sa[..., None] * b_t[:, :, None, :] + v_t[..., None] * k_t[:, :, None, :]
        y_t = jnp.einsum('bhvk,bhk->bhv', S, r_t)
        return S, y_t

    xs = (jnp.moveaxis(r, 1, 0), jnp.moveaxis(decay, 1, 0), jnp.moveaxis(k, 1, 0),
          jnp.moveaxis(v, 1, 0), jnp.moveaxis(a, 1, 0), jnp.moveaxis(b, 1, 0))
    S0 = jnp.zeros((bsz, h, n, n), jnp.float32)
    _, y = lax.scan(step, S0, xs)
    return jnp.moveaxis(y, 0, 1)


def rwkv7_time_mix(xn, mix, w_rkv, w0, w1, w2, a0, a1, a2, g1, g2, k_k, k_a, r_k, ln_w, ln_b, w_out):
    bsz, t, c = xn.shape
    h, n = RWKV_HEADS, RWKV_HEAD_DIM
    f32 = jnp.float32
    xx = jnp.pad(xn[:, :-1], ((0, 0), (1, 0), (0, 0))) - xn
    lerp = lambda i: xn + xx * mix[i]
    r = lerp(0) @ w_rkv[0]
    k = lerp(1) @ w_rkv[1]
    v = lerp(2) @ w_rkv[2]
    w_log = -jax.nn.softplus(-(w0 + jnp.tanh(lerp(3) @ w1) @ w2).astype(f32)) - 0.5
    decay = jnp.exp(-jnp.exp(w_log))
    a = jax.nn.sigmoid((a0 + (lerp(4) @ a1) @ a2).astype(f32))
    g = jax.nn.sigmoid(lerp(5) @ g1) @ g2
    kk = (k * k_k).astype(f32).reshape(bsz, t, h, n)
    kk = kk / jnp.maximum(jnp.sqrt(jnp.sum(kk * kk, axis=-1, keepdims=True)), L2_EPS)
    k = k.astype(f32) * (1.0 + (a - 1.0) * k_a.astype(f32))
    rh = r.astype(f32).reshape(bsz, t, h, n)
    kh = k.reshape(bsz, t, h, n)
    vh = v.astype(f32).reshape(bsz, t, h, n)
    ah = a.reshape(bsz, t, h, n)
    y = wkv7_scan(rh, decay.reshape(bsz, t, h, n), kh, vh, -kk, kk * ah)
    mu = jnp.mean(y, axis=-1, keepdims=True)
    var = jnp.mean(jnp.square(y - mu), axis=-1, keepdims=True)
    y = (y - mu) * lax.rsqrt(var + GN_EPS)
    y = y * ln_w.astype(f32).reshape(h, n) + ln_b.astype(f32).reshape(h, n)
    y = y + jnp.sum(rh * kh * r_k.astype(f32), axis=-1, keepdims=True) * vh
    y = y.reshape(bsz, t, c).astype(xn.dtype)
    return (y * g) @ w_out


def banded_sink_attention(q, k, v, sinks):
    bsz, t, _, d = q.shape
    nb = t // BLOCK
    qb = q.reshape(bsz, nb, BLOCK, N_KV_HEADS, Q_PER_KV, d)
    kb = k.reshape(bsz, nb, BLOCK, N_KV_HEADS, d)
    vb = v.reshape(bsz, nb, BLOCK, N_KV_HEADS, d)
    padw = ((0, 0), (1, 0), (0, 0), (0, 0), (0, 0))
    k_band = jnp.concatenate([jnp.pad(kb[:, :-1], padw), kb], axis=2)
    v_band = jnp.concatenate([jnp.pad(vb[:, :-1], padw), vb], axis=2)
    s = jnp.einsum('bnqkgd,bnskd->bnkgqs', qb, k_band).astype(jnp.float32) * (d ** -0.5)
    qpos = jnp.arange(BLOCK)[:, None] + BLOCK
    kpos = jnp.arange(2 * BLOCK)[None, :]
    diff = qpos - kpos
    in_window = (diff >= 0) & (diff < WINDOW)
    not_pad = (jnp.arange(nb)[:, None, None] > 0) | (kpos[None] >= BLOCK)
    valid = in_window[None] & not_pad
    s = jnp.where(valid[None, :, None, None], s, -jnp.inf)
    sink = jnp.broadcast_to(
        sinks.astype(jnp.float32).reshape(N_KV_HEADS, Q_PER_KV)[None, None, :, :, None, None],
        s.shape[:-1] + (1,))
    p = jax.nn.softmax(jnp.concatenate([s, sink], axis=-1), axis=-1)[..., :-1]
    o = jnp.einsum('bnkgqs,bnskd->bnqkgd', p.astype(v.dtype), v_band)
    return o.reshape(bsz, t, N_Q_HEADS, d)


def swa_cross_layer(xn, k_shared, v_shared, pos, w_q, q_norm, sinks, w_o):
    bsz, t, _ = xn.shape
    q = (xn @ w_q).reshape(bsz, t, N_Q_HEADS, HEAD_DIM)
    q = rope(rms_norm(q, q_norm), pos)
    o = banded_sink_attention(q, k_shared, v_shared, sinks)
    return o.reshape(bsz, t, N_Q_HEADS * HEAD_DIM) @ w_o


def setup_inputs(seed: int = 0) -> dict:
    key = jax.random.key(seed)
    ks = list(jax.random.split(key, 40))
    f32 = jnp.float32
    nxt = lambda: ks.pop()
    nrm = lambda shape, scale: jax.random.normal(nxt(), shape, f32) * scale
    D, nA, nB = D_MODEL, N_A_LAYERS, N_B_LAYERS
    H, N = RWKV_HEADS, RWKV_HEAD_DIM
    return {
        "x": nrm((BATCH, SEQ, D), 1.0),
        "a_norm": 1.0 + nrm((nA, D), 0.02),
        "a_mix": jax.random.uniform(nxt(), (nA, N_MIX, D), f32),
        "a_w_rkv": nrm((nA, 3, D, D), D ** -0.5),
        "a_w0": jax.random.uniform(nxt(), (nA, D), f32, -6.0, 0.0),
        "a_w1": nrm((nA, D, DECAY_LORA), D ** -0.5),
        "a_w2": nrm((nA, DECAY_LORA, D), 0.1 * DECAY_LORA ** -0.5),
        "a_a0": nrm((nA, D), 0.5),
        "a_a1": nrm((nA, D, AAA_LORA), D ** -0.5),
        "a_a2": nrm((nA, AAA_LORA, D), 0.1 * AAA_LORA ** -0.5),
        "a_g1": nrm((nA, D, GATE_LORA), D ** -0.5),
        "a_g2": nrm((nA, GATE_LORA, D), GATE_LORA ** -0.5),
        "a_k_k": 0.85 + nrm((nA, D), 0.05),
        "a_k_a": 1.0 + nrm((nA, D), 0.05),
        "a_r_k": nrm((nA, H, N), 0.1),
        "a_ln_x_w": 1.0 + nrm((nA, D), 0.02),
        "a_ln_x_b": nrm((nA, D), 0.02),
        "a_w_out": nrm((nA, D, D), D ** -0.5),
        "mlp_norm": 1.0 + nrm((DEPTH, D), 0.02),
        "mlp_w_up": nrm((DEPTH, D, D_FF), D ** -0.5),
        "mlp_w_down": nrm((DEPTH, D_FF, D), D_FF ** -0.5),
        "kv_norm": 1.0 + nrm((D,), 0.02),
        "w_kv": nrm((D, 2 * N_KV_HEADS * HEAD_DIM), D ** -0.5),
        "k_norm": 1.0 + nrm((HEAD_DIM,), 0.02),
        "b_norm": 1.0 + nrm((nB, D), 0.02),
        "b_w_q": nrm((nB, D, N_Q_HEADS * HEAD_DIM), D ** -0.5),
        "b_q_norm": 1.0 + nrm((nB, HEAD_DIM), 0.02),
        "b_sinks": nrm((nB, N_Q_HEADS), 0.5),
        "b_w_o": nrm((nB, N_Q_HEADS * HEAD_DIM, D), (N_Q_HEADS * HEAD_DIM) ** -0.5),
    }


def reference(x, a_norm, a_mix, a_w_rkv, a_w0, a_w1, a_w2, a_a0, a_a1, a_a2, a_g1, a_g2,
              a_k_k, a_k_a, a_r_k, a_ln_x_w, a_ln_x_b, a_w_out,
              mlp_norm, mlp_w_up, mlp_w_down,
              kv_norm, w_kv, k_norm,
              b_norm, b_w_q, b_q_norm, b_sinks, b_w_o):
    bsz, t, _ = x.shape
    pos = jnp.arange(t, dtype=jnp.int32)
    h = x
    k_shared = None
    v_shared = None
    for layer in range(DEPTH):
        if layer < N_A_LAYERS:
            i = layer
            h = h + rwkv7_time_mix(rms_norm(h, a_norm[i]), a_mix[i], a_w_rkv[i], a_w0[i], a_w1[i], a_w2[i],
                                   a_a0[i], a_a1[i], a_a2[i], a_g1[i], a_g2[i], a_k_k[i], a_k_a[i],
                                   a_r_k[i], a_ln_x_w[i], a_ln_x_b[i], a_w_out[i])
        else:
            if layer == N_A_LAYERS:
                kv = rms_norm(h, kv_norm) @ w_kv
                k_s, v_s = jnp.split(kv, 2, axis=-1)
                k_shared = rope(rms_norm(k_s.reshape(bsz, t, N_KV_HEADS, HEAD_DIM), k_norm), pos)
                v_shared = v_s.reshape(bsz, t, N_KV_HEADS, HEAD_DIM)
            j = layer - N_A_LAYERS
            h = h + swa_cross_layer(rms_norm(h, b_norm[j]), k_shared, v_shared, pos,
                                    b_w_q[j], b_q_norm[j], b_sinks[j], b_w_o[j])
        h = h + squared_relu_mlp(rms_norm(h, mlp_norm[layer]), mlp_w_up[layer], mlp_w_down[layer])
    return h
```

```python
import math
import os
from contextlib import ExitStack
import numpy as np
import ml_dtypes
import concourse.bass as bass
import concourse.mybir as mybir
from concourse.bass_utils import run_bass_kernel_spmd

F32 = mybir.dt.float32
BF16 = mybir.dt.bfloat16
AF = mybir.ActivationFunctionType
ALU = mybir.AluOpType
AX = mybir.AxisListType

D = 2048
T = 2048
NCH = 16
FF = 8192
NCORES = 8
ENGS = ["tensor", "vector", "scalar", "gpsimd", "sync"]
ARENA_WORDS = 45056


class Op:
    __slots__ = ("eng", "fn", "reads", "writes", "sig", "dma", "semkey", "csem", "cval", "waits", "preads")

    def __init__(self, eng, fn, reads, writes, sig, dma, semkey):
        self.eng = eng
        self.fn = fn
        self.reads = reads
        self.writes = writes
        self.sig = sig
        self.dma = dma
        self.semkey = semkey
        self.csem = None
        self.cval = None
        self.waits = []


class Prog:
    def __init__(self, nc):
        self.nc = nc
        self.ops = []
        self.psq = {}

    def add(self, eng, fn, reads=(), writes=(), sig=True, dma=False, semkey=None):
        ispk = lambda b: isinstance(b, tuple) and len(b) == 2 and b[0] == "ps"
        pr = frozenset(b for b in reads if ispk(b) and b not in writes)
        rd = tuple(b for b in reads if not ispk(b))
        wr = tuple(writes) + tuple(pr)
        op = Op(eng, fn, rd, wr, sig, dma, semkey)
        op.preads = pr
        self.ops.append(op)
        return op

    def barrier(self):
        self.ops.append(None)

    def mm(self, out, lhsT, rhs, start=True, stop=True, reads=(), writes=(), sig=True):
        return self.add("tensor", lambda e: e.matmul(out, lhsT, rhs, start=start, stop=stop,
                                                     skip_group_check=True),
                        reads, writes, sig=sig)

    def tr(self, out, in_, ident, reads=(), writes=(), sig=True):
        return self.add("tensor", lambda e: e.transpose(out, in_, ident), reads, writes, sig=sig)

    def act(self, out, in_, func, reads=(), writes=(), **kw):
        return self.add("scalar", lambda e: e.activation(out, in_, func, **kw), reads, writes)

    def dma(self, eng, out, in_, reads=(), writes=(), semkey=None):
        assert semkey is not None
        return self.add(eng, lambda e: e.dma_start(out, in_), reads, writes, dma=True, semkey=semkey)

    def tt(self, eng, out, a, b, op, reads=(), writes=()):
        return self.add(eng, lambda e: e.tensor_tensor(out, a, b, op), reads, writes)

    def ts(self, eng, out, a, s1, s2, op0, op1=None, reads=(), writes=()):
        if op1 is None:
            return self.add(eng, lambda e: e.tensor_scalar(out, a, s1, None, op0), reads, writes)
        return self.add(eng, lambda e: e.tensor_scalar(out, a, s1, s2, op0, op1), reads, writes)

    def stt(self, eng, out, a, s, b, op0, op1, reads=(), writes=()):
        eng = "vector"
        return self.add(eng, lambda e: e.scalar_tensor_tensor(out, a, s, b, op0, op1), reads, writes)

    def cp(self, eng, out, a, reads=(), writes=()):
        if eng == "scalar":
            return self.act(out, a, AF.Copy, reads, writes)
        return self.add(eng, lambda e: e.tensor_copy(out, a), reads, writes)

    def emit(self, stack):
        nc = self.nc
        ops = self.ops
        eng_cnt = {e: 0 for e in ENGS}
        dma_cnt = {}
        pending = {e: [] for e in ENGS}
        lastop = {}
        for op in ops:
            if op is not None:
                lastop[op.eng] = op
        for op in lastop.values():
            op.sig = True
        for op in ops:
            if op is None:
                continue
            if op.dma:
                n = dma_cnt.get(op.semkey, 0) + 1
                dma_cnt[op.semkey] = n
                op.csem = ("d", op.semkey)
                op.cval = 16 * n
            elif op.sig:
                eng_cnt[op.eng] += 1
                op.csem = ("e", op.eng)
                op.cval = eng_cnt[op.eng]
                for q in pending[op.eng]:
                    q.csem = op.csem
                    q.cval = op.cval
                pending[op.eng] = []
            else:
                pending[op.eng].append(op)
        for e in ENGS:
            assert not pending[e], e
        last_w = {}
        readers = {}
        waited = {e: {} for e in ENGS}
        all_prev = {}
        barrier_req = {e: None for e in ENGS}
        for op in ops:
            if op is None:
                snap = dict(all_prev)
                for e in ENGS:
                    barrier_req[e] = snap
                continue
            need = {}
            if barrier_req[op.eng] is not None:
                for s, v in barrier_req[op.eng].items():
                    need[s] = max(need.get(s, 0), v)
                barrier_req[op.eng] = None
            for b in op.reads:
                o = last_w.get(b)
                if o is not None and o is not op:
                    if not (o.eng == "tensor" and op.eng == "tensor"):
                        assert o.cval is not None
                        need[o.csem] = max(need.get(o.csem, 0), o.cval)
            for b in op.writes:
                o = last_w.get(b)
                lst = list(readers.get(b, ()))
                if o is not None:
                    lst.append(o)
                for o in lst:
                    if o is op:
                        continue
                    if o.eng == "tensor" and op.eng == "tensor":
                        continue
                    if o.eng == op.eng and b in op.preads and b in o.preads:
                        continue
                    need[o.csem] = max(need.get(o.csem, 0), o.cval)
            for b in op.reads:
                readers.setdefault(b, []).append(op)
            for b in op.writes:
                last_w[b] = op
                readers[b] = []
            w = waited[op.eng]
            for s, v in need.items():
                if s == op.csem and v >= op.cval:
                    raise RuntimeError("self-wait deadlock")
                if w.get(s, 0) < v:
                    w[s] = v
                    op.waits.append((s, v))
            all_prev[op.csem] = max(all_prev.get(op.csem, 0), op.cval)
        final = dict(all_prev)
        sems = {}
        for op in ops:
            if op is None:
                continue
            if op.csem not in sems:
                sems[op.csem] = stack.enter_context(nc.semaphore("s%d" % len(sems)))
        self.n_sems = len(sems)
        block = stack.enter_context(nc.Block())
        per_eng = {e: [o for o in ops if o is not None and o.eng == e] for e in ENGS}

        def body(engname):
            def f(eng):
                for op in per_eng[engname]:
                    for s, v in op.waits:
                        eng.wait_ge(sems[s], v)
                    ins = op.fn(eng)
                    if op.dma:
                        ins.then_inc(sems[op.csem], 16)
                    elif op.sig:
                        ins.then_inc(sems[op.csem], 1)
                if engname == "sync":
                    for s, v in final.items():
                        eng.wait_ge(sems[s], v)
            return f

        block.tensor(body("tensor"))
        block.vector(body("vector"))
        block.scalar(body("scalar"))
        block.gpsimd(body("gpsimd"))
        block.sync(body("sync"))
        return {e: len(per_eng[e]) for e in ENGS}, dict(eng_cnt)


class Arena:
    def __init__(self, ap):
        self.ap = ap
        self.off = 0
        self.cap = ap.shape[1]

    def tile(self, shape, dt):
        n = int(np.prod(shape))
        nb = n * (4 if dt == F32 else 2)
        nw = (nb + 3) // 4
        nw = (nw + 7) // 8 * 8
        a = self.ap[:, self.off:self.off + nw]
        self.off += nw
        assert self.off <= self.cap, ("arena overflow", self.off, self.cap)
        if dt != F32:
            a = a.bitcast(dt)
        a = a[:, 0:n]
        if len(shape) == 2:
            a = a.rearrange("p (a b) -> p a b", a=shape[0])
        elif len(shape) == 3:
            a = a.rearrange("p (a b c) -> p a b c", a=shape[0], b=shape[1])
        elif len(shape) == 4:
            a = a.rearrange("p (a b c d) -> p a b c d", a=shape[0], b=shape[1], c=shape[2])
        return a


def bc(ap, shape):
    return ap.to_broadcast(list(shape))


PV_ANORM, PV_MIX, PV_W0, PV_A0, PV_KK, PV_KA, PV_RK, PV_LNW, PV_LNB = 0, 1, 7, 8, 9, 10, 11, 12, 13
PV_MLP0, PV_MLP1, PV_KVN, PV_BN, PV_SK = 14, 15, 16, 17, 18
NV = 19
EXPM05 = math.exp(-0.5)


class Ctx:
    pass


class _Stop(Exception):
    pass


def chk(n):
    if os.environ.get('KSTOP') == str(n):
        raise _Stop()


def build(phases, debug_outs=()):
    nc = bass.Bass("TRN2", target_bir_lowering=False)
    C = Ctx()
    C.nc = nc
    dr = {}

    def din(name, shape, dt=F32):
        dr[name] = nc.dram_tensor(name, list(shape), dt, kind="ExternalInput").ap()

    def dscr(name, shape, dt=F32):
        kind = "ExternalOutput" if name in debug_outs else "Internal"
        dr[name] = nc.dram_tensor(name, list(shape), dt, kind=kind).ap()

    din("x", [T, D])
    for w in ["w_r", "w_k", "w_v", "w_out", "w_q", "w_o"]:
        din(w, [D, D])
    din("w1", [D, 96]); din("w2", [96, D]); din("a1", [D, 96]); din("a2", [96, D])
    din("g1", [D, 256]); din("g2", [256, D])
    din("up0", [D, FF]); din("up1", [D, FF]); din("dn0", [FF, D]); din("dn1", [FF, D])
    din("w_kv", [D, 512])
    din("pv", [128, NV * 16])
    din("pvb", [128, 128])
    din("c_ident", [128, 128]); din("c_bones", [128, 128])
    din("c_m64", [128, 3 * 64])
    din("c_cs", [128, 16 * 64])
    din("c_am", [128, 256])
    dr["out"] = nc.dram_tensor("out", [T, D], F32, kind="ExternalOutput").ap()
    dscr("h", [T, D])
    dscr("ARt", [D, 32, 2, 64], BF16); dscr("BTt", [D, T], BF16); dscr("KTt", [D, T], BF16)
    dscr("Bs", [16, 128, 32, 64], BF16); dscr("Ks", [16, 128, 32, 64], BF16); dscr("Vs", [16, 128, 32, 64], BF16)
    dscr("GCt", [D, 32]); dscr("BON", [D, T]); dscr("Gt", [D, T], BF16)
    dscr("ygT", [D, T], BF16)
    dscr("KTs", [4, 64, T], BF16); dscr("Vt", [T, 256], BF16)
    C.dr = dr

    with ExitStack() as st:
        arena_ap = st.enter_context(nc.sbuf_tensor("arena", [128, ARENA_WORDS], F32))
        ps = st.enter_context(nc.psum_tensor("ps", [128, 8, 512], F32))
        C.ps = ps
        P = Prog(nc)
        C.P = P
        A = Arena(arena_ap)
        C.A = A
        C.ident = A.tile([128], F32)
        C.identb = A.tile([128], BF16)
        C.bones = A.tile([128], F32)
        C.m64 = A.tile([3, 64], F32)
        C.pv = A.tile([NV, 16], F32)
        C.pvb = A.tile([128], F32)
        C.cs = A.tile([16, 64], F32)
        C.am = A.tile([2, 128], F32)
        C.esk = A.tile([16], F32)
        P.dma("sync", C.ident, dr["c_ident"], writes=["ident"], semkey="c0")
        P.dma("sync", C.bones, dr["c_bones"], writes=["bones"], semkey="c1")
        P.dma("sync", C.m64, dr["c_m64"].rearrange("p (a b) -> p a b", a=3), writes=["m64"], semkey="c2")
        P.dma("sync", C.pv, dr["pv"].rearrange("p (a b) -> p a b", a=NV), writes=["pv"], semkey="c3")
        P.dma("sync", C.pvb, dr["pvb"], writes=["pvb"], semkey="c4")
        P.dma("sync", C.cs, dr["c_cs"].rearrange("p (a b) -> p a b", a=16), writes=["cs"], semkey="c5")
        P.dma("sync", C.am, dr["c_am"].rearrange("p (a b) -> p a b", a=2), writes=["am"], semkey="c6")
        P.cp("vector", C.identb, C.ident, reads=["ident"], writes=["identb"])
        P.act(C.esk, C.pv[:, PV_SK, :], AF.Exp, reads=["pv"], writes=["esk"])
        C.base = A.off
        for ph in phases:
            A.off = C.base
            P.barrier()
            try:
                ph(C)
            except _Stop:
                break
        P.barrier()
        info = P.emit(st)
        C.info = info
    return nc, C


def pk(bank, lo=0, hi=512):
    return [("ps", bank)]


def rsqrt(P, out, in_, scale, eps, rk, wk):
    P.ts("vector", out, in_, scale, eps, ALU.mult, ALU.add, reads=rk, writes=wk)
    P.act(out, out, AF.Sqrt, reads=wk, writes=wk)
    P.add("vector", lambda e: e.reciprocal(out, out), reads=wk, writes=wk)


def norm_T(C, src, tok0, gain_idx, dst, dst_key, col0, xt, xtk, junk, ss, psbanks=(6, 7), junk_keys=("junk",)):
    P = C.P
    ps = C.ps
    P.dma("sync", xt, src[tok0:tok0 + 128, :], writes=[xtk], semkey=xtk)
    chk('n0')
    P.act(junk, xt, AF.Square, reads=[xtk], writes=list(junk_keys) + ["ss0"], accum_out=ss[:, 0:1])
    chk('n1')
    rsqrt(P, ss[:, 2:3], ss[:, 0:1], 1.0 / D, 1e-6, ["ss0"], ["ss2"])
    chk('n2')
    P.ts("vector", xt, xt, ss[:, 2:3], None, ALU.mult, reads=[xtk, "ss2"], writes=[xtk])
    chk('n3')
    for gq in range(NCH // 4):
        b = psbanks[gq % 2]
        k_ = pk(b)
        for q in range(4):
            c = gq * 4 + q
            P.tr(ps[:, b, q * 128:(q + 1) * 128], xt[:, c * 128:(c + 1) * 128], C.ident, reads=[xtk, "ident"], writes=k_,
                 sig=(q == 3))
        for q in range(4):
            c = gq * 4 + q
            o = dst[:, c, col0:col0 + 128]
            g = C.pv[:, gain_idx, c:c + 1]
            if gq % 2 == 0:
                P.ts("vector", o, ps[:, b, q * 128:(q + 1) * 128], g, None, ALU.mult, reads=k_ + ["pv"], writes=[(dst_key, c)])
            else:
                P.act(o, ps[:, b, q * 128:(q + 1) * 128], AF.Copy, reads=k_ + ["pv"], writes=[(dst_key, c)], scale=g)


def load_panel(C, wp, wpk, w, r0, c0, rows=2048, cols=512):
    kc = rows // 128
    C.P.dma("gpsimd", wp[:, 0:kc, 0:cols], w[r0:r0 + rows, c0:c0 + cols].rearrange("(k p) n -> p k n", p=128),
            writes=[wpk], semkey=wpk)


class Deferred:
    def __init__(self):
        self.calls = []

    def __getattr__(self, name):
        def f(*a, **k):
            self.calls.append((name, a, k))
        return f


def replay_interleaved(P, lists, weights=None):
    ptr = [0] * len(lists)
    if weights is None:
        weights = [1] * len(lists)
    while any(ptr[i] < len(l) for i, l in enumerate(lists)):
        for i, l in enumerate(lists):
            for _ in range(weights[i]):
                while ptr[i] < len(l):
                    name, a, k = l[ptr[i]]
                    ptr[i] += 1
                    getattr(P, name)(*a, **k)
                    if k.get("sig", True):
                        break


class WStream:
    def __init__(self, C, plan, tiles, keys):
        self.C, self.plan, self.tiles, self.keys = C, plan, tiles, keys
        self.n = len(tiles)
        self.issued = 0
        self.ndone = 0
        self.nget = 0

    def _pump(self):
        while self.issued < len(self.plan) and self.issued - self.n < self.ndone:
            j = self.issued
            w, r0, c0 = self.plan[j]
            load_panel(self.C, self.tiles[j % self.n], self.keys[j % self.n], w, r0, c0)
            self.issued += 1

    def get(self):
        self._pump()
        j = self.nget
        assert j < self.issued
        self.nget += 1
        return self.tiles[j % self.n], self.keys[j % self.n]

    def done(self):
        self.ndone += 1
        self._pump()


def proj_tm(C, lhs, lhs_key, KC, w, nsub, WS, res_src, res_dst, tok0, rtiles, name, cnt):
    P = C.P
    ps = C.ps
    npan = KC // 16
    for n in range(4):
        for kp in range(npan):
            wp_, wpk_ = WS.get()
            for k in range(16):
                kk = kp * 16 + k
                for s in range(nsub):
                    bank = 4 + s
                    P.mm(ps[:, bank, :], lhs[:, kk, s * 128:(s + 1) * 128], wp_[:, k, :],
                         start=(kk == 0), stop=(kk == KC - 1),
                         reads=[wpk_, (lhs_key, kk)], writes=pk(bank), sig=(k == 15 and s == nsub - 1))
            WS.done()
        for s in range(nsub):
            bank = 4 + s
            ri = cnt[1] % len(rtiles)
            cnt[1] += 1
            rt, rk = rtiles[ri]
            t0 = tok0 + s * 128
            P.dma("sync", rt, res_src[t0:t0 + 128, n * 512:(n + 1) * 512], writes=[rk], semkey=rk)
            P.tt("vector", rt, rt, ps[:, bank, :], ALU.add, reads=[rk] + pk(bank), writes=[rk])
            P.dma("sync", res_dst[t0:t0 + 128, n * 512:(n + 1) * 512], rt, reads=[rk],
                  writes=[("dram", name, t0 // 128, n)], semkey=rk + "o")


def phase_A1(C):
    P, A, ps, dr, pv = C.P, C.A, C.ps, C.dr, C.pv
    TT = 256
    NQ = TT // 64
    xt = A.tile([2048], F32)
    junk_raw = A.tile([1024], F32)
    junk = junk_raw.bitcast(BF16)
    ss = A.tile([4], F32)
    xnT = A.tile([16, TT], BF16)
    xxT = A.tile([16, TT], BF16)
    xlast = A.tile([16, 1], BF16)
    xl = [A.tile([16, TT], BF16) for _ in range(3)]
    xs_raw = A.tile([2048], F32)
    xs = xs_raw.bitcast(BF16).rearrange("p (k n) -> p k n", k=16)
    stage32 = A.tile([4096], F32)
    wps = [stage32.bitcast(BF16).rearrange("p (k n) -> p k n", k=16), A.tile([16, 512], BF16)]
    wpks = ["wpA0", "wpA1"]
    w1b = A.tile([16, 96], BF16)
    a1b = A.tile([16, 96], BF16)
    g1b = A.tile([16, 256], BF16)
    w2b = A.tile([2048], BF16)
    a2b = A.tile([2048], BF16)
    g2b = A.tile([2, 2048], BF16)
    twT = A.tile([TT], BF16)
    taT = A.tile([TT], BF16)
    sgT = A.tile([2, TT], BF16)

    def ld(dst, src_ap, view, key):
        P.dma("sync", view, src_ap, writes=["wpA0"], semkey="ldsmall")
        P.cp("vector", dst, view, reads=["wpA0"], writes=[key])

    ld(w1b, dr["w1"].rearrange("(k p) n -> p k n", p=128), stage32[:, 0:16 * 96].rearrange("p (k n) -> p k n", k=16), "w1b")
    ld(a1b, dr["a1"].rearrange("(k p) n -> p k n", p=128), stage32[:, 0:16 * 96].rearrange("p (k n) -> p k n", k=16), "a1b")
    ld(g1b, dr["g1"].rearrange("(k p) n -> p k n", p=128), stage32[:, 0:16 * 256].rearrange("p (k n) -> p k n", k=16), "g1b")
    ld(w2b[0:96, :], dr["w2"], stage32[0:96, 0:2048], "w2b")
    ld(a2b[0:96, :], dr["a2"], stage32[0:96, 0:2048], "a2b")
    ld(g2b, dr["g2"].rearrange("(k p) n -> p k n", p=128), stage32[:, 0:4096].rearrange("p (k n) -> p k n", k=2), "g2b")
    P.add("vector", lambda e: e.memset(xlast, 0.0), writes=["xlast"])

    def stg(n, dt=F32, shape=(TT,)):
        return [A.tile(list(shape), dt) for _ in range(n)]

    r_s, k_s, a_s, d_s = stg(4), stg(4), stg(4), stg(4)
    v_s = stg(2)
    g_sb = stg(2, BF16)
    negka = A.tile([16], F32)
    P.ts("vector", negka, pv[:, PV_KA, :], -1.0, None, ALU.mult, reads=["pv"], writes=["negka"])
    TN = ["kk0", "sq", "nkk", "bb", "km", "t1", "rk", "e1", "e2"]
    TMP = {}
    TKY = {}
    for j, nm in enumerate(TN):
        if j < 8:
            alt = xs_raw[:, j * 256:(j + 1) * 256]
            altk = [("xs", 2 * j), ("xs", 2 * j + 1)]
        else:
            alt = stg(1)[0]
            altk = [(nm, 1)]
        TMP[nm] = [stg(1)[0], alt]
        TKY[nm] = [[(nm, 0)], altk]
    cA = [stg(1, F32, (NQ, 96))[0], junk_raw[:, 0:384].rearrange("p (a b) -> p a b", a=NQ)]
    cB = [stg(1, F32, (NQ, 96))[0], junk_raw[:, 384:768].rearrange("p (a b) -> p a b", a=NQ)]
    bon = [stg(1)[0], junk_raw[:, 768:1024]]
    CK = [[("cAB", 0)], ["junk"]]
    BK = [[("bon", 0)], ["junk"]]
    P.add("vector", lambda e: e.memset(cA[0], 0.0), writes=CK[0])
    P.add("vector", lambda e: e.memset(cB[0], 0.0), writes=CK[0])
    gc = stg(2, F32, (NQ,))
    AR = stg(2, BF16, (NQ, 2, 64))
    btil, ktil = stg(2, BF16), stg(2, BF16)
    bhat = stg(2, BF16, (NQ, 64))
    khat = stg(2, BF16, (NQ, 64))
    vb = stg(2, BF16, (NQ, 64))
    stk = stg(2, BF16, (3, NQ, 64))
    print("A1 arena words", A.off)
    v3 = lambda a: a.rearrange("p (a b) -> p a b", a=NQ)

    def post(P, c, tt, tok0):
        s4 = c % 4
        si = c % 2
        hb = (c % 2) * 3
        K4 = lambda nm: (nm, s4)
        K2 = lambda nm: (nm, si)
        rs, ks, as_, ds, vs = r_s[s4], k_s[s4], a_s[s4], d_s[s4], v_s[si]
        kk0, sq, nkk, bb, km, t1, rk, e1, e2 = [TMP[n][si] for n in TN]
        kk0k, sqk, nkkk, bbk, kmk, t1k, rkk, e1k, e2k = [TKY[n][si] for n in TN]
        ck = CK[si]
        bk = BK[si]
        P.act(kk0, ks, AF.Copy, reads=[K4("k_s"), "pv"], writes=kk0k, scale=pv[:, PV_KK, c:c + 1])
        P.act(sq, kk0, AF.Square, reads=kk0k, writes=sqk)
        pA = ps[:, hb + 2, 256:256 + TT]
        pAk = pk(hb + 2)
        P.mm(pA, C.bones, sq, reads=["bones"] + sqk, writes=pAk)
        P.act(sq, pA, AF.Sqrt, reads=pAk, writes=sqk)
        P.ts("vector", sq, sq, 1e-12, None, ALU.max, reads=sqk, writes=sqk)
        P.add("vector", lambda e: e.reciprocal(sq, sq), reads=sqk, writes=sqk)
        P.stt("vector", nkk, kk0, -1.0, sq, ALU.mult, ALU.mult, reads=kk0k + sqk, writes=nkkk)
        P.stt("vector", bb, nkk, -1.0, as_, ALU.mult, ALU.mult, reads=nkkk + [K4("a_s")], writes=bbk)
        P.act(t1, as_, AF.Identity, reads=[K4("a_s"), "pv", "negka"], writes=t1k, scale=pv[:, PV_KA, c:c + 1], bias=negka[:, c:c + 1])
        P.stt("vector", km, t1, 1.0, ks, ALU.add, ALU.mult, reads=t1k + [K4("k_s")], writes=kmk)
        P.stt("vector", rk, rs, pv[:, PV_RK, c:c + 1], km, ALU.mult, ALU.mult, reads=[K4("r_s"), "pv"] + kmk, writes=rkk)
        pB = ps[:, hb + 2, 0:TT]
        P.mm(pB, C.bones, rk, reads=["bones"] + rkk, writes=pAk)
        P.tt("vector", bon[si], pB, vs, ALU.mult, reads=pAk + [K2("v_s")], writes=bk)
        P.dma("sync", dr["BON"][c * 128:(c + 1) * 128, tok0:tok0 + TT], bon[si], reads=bk,
              writes=[("dram", "BON", c, tt)], semkey="oBON%d" % si)
        ca, cb = cA[si], cB[si]
        P.cp("vector", ca[:, :, 32:96], v3(ds), reads=[K4("d_s")], writes=ck)
        src, dst = ca, cb
        for sh in [1, 2, 4, 8, 16, 32]:
            P.tt("vector", dst[:, :, 32:96], src[:, :, 32:96], src[:, :, 32 - sh:96 - sh], ALU.add, reads=ck, writes=ck)
            src, dst = dst, src
        Lp = src[:, :, 32:96]
        P.act(v3(e1), Lp, AF.Exp, reads=ck, writes=e1k, scale=-EXPM05)
        P.act(v3(e2), Lp, AF.Exp, reads=ck, writes=e2k, scale=EXPM05)
        P.tt("gpsimd", v3(t1), Lp, v3(ds), ALU.subtract, reads=ck + [K4("d_s")], writes=t1k)
        P.act(t1, t1, AF.Exp, reads=t1k, writes=t1k, scale=-EXPM05)
        P.tt("gpsimd", v3(kk0), bc(src[:, :, 95:96], [128, NQ, 64]), Lp, ALU.subtract, reads=ck, writes=kk0k)
        P.act(kk0, kk0, AF.Exp, reads=kk0k, writes=kk0k, scale=-EXPM05)
        P.cp("gpsimd", gc[si], v3(e1)[:, :, 63], reads=e1k, writes=[K2("gc")])
        P.dma("sync", dr["GCt"][c * 128:(c + 1) * 128, tt * NQ:(tt + 1) * NQ], gc[si], reads=[K2("gc")],
              writes=[("dram", "GCt", c, tt)], semkey="oGC%d" % si)
        ar = AR[si]
        P.tt("vector", ar[:, :, 0, :], v3(nkk), v3(t1), ALU.mult, reads=nkkk + t1k, writes=[K2("AR0")])
        P.tt("gpsimd", ar[:, :, 1, :], v3(rs), v3(e1), ALU.mult, reads=[K4("r_s")] + e1k, writes=[K2("AR1")])
        P.dma("sync", dr["ARt"][c * 128:(c + 1) * 128, tt * NQ:(tt + 1) * NQ, :, :], ar, reads=[K2("AR0"), K2("AR1")],
              writes=[("dram", "ARt", c, tt)], semkey="oAR%d" % si)
        P.tt("vector", btil[si], bb, e2, ALU.mult, reads=bbk + e2k, writes=[K2("btil")])
        P.dma("sync", dr["BTt"][c * 128:(c + 1) * 128, tok0:tok0 + TT], btil[si], reads=[K2("btil")],
              writes=[("dram", "BTt", c, tt)], semkey="oBT%d" % si)
        P.tt("gpsimd", ktil[si], km, e2, ALU.mult, reads=kmk + e2k, writes=[K2("ktil")])
        P.dma("sync", dr["KTt"][c * 128:(c + 1) * 128, tok0:tok0 + TT], ktil[si], reads=[K2("ktil")],
              writes=[("dram", "KTt", c, tt)], semkey="oKT%d" % si)
        P.tt("vector", bhat[si], v3(bb), v3(kk0), ALU.mult, reads=bbk + kk0k, writes=[K2("bhat")])
        P.tt("gpsimd", khat[si], v3(km), v3(kk0), ALU.mult, reads=kmk + kk0k, writes=[K2("khat")])
        pst = ps[:, hb, 256:512].bitcast(BF16).rearrange("p (a b c) -> p a b c", a=2, b=NQ)
        pst2 = ps[:, hb + 1, 256:384].bitcast(BF16).rearrange("p (b c) -> p b c", b=NQ)
        for oi, (srct, sk_) in enumerate([(bhat[si], "bhat"), (khat[si], "khat"), (vb[si], "vb")]):
            for q in range(NQ):
                for h in range(2):
                    hp = slice(h * 64, (h + 1) * 64)
                    dstp = pst[hp, oi, q, :] if oi < 2 else pst2[hp, q, :]
                    dk_ = pk(hb) if oi < 2 else pk(hb + 1)
                    last = (q == NQ - 1 and h == 1)
                    P.tr(dstp, srct[hp, q, :], C.identb[hp, hp], reads=[K2(sk_), "identb"], writes=dk_, sig=last)
        P.cp("scalar", stk[si][:, 0:2, :, :], pst, reads=pk(hb), writes=[K2("stk01")])
        P.cp("scalar", stk[si][:, 2, :, :], pst2, reads=pk(hb + 1), writes=[K2("stk2")])
        for oi, nm in enumerate(["Bs", "Ks", "Vs"]):
            P.dma("sync", dr[nm][c, :, tt * NQ:(tt + 1) * NQ, :], stk[si][:, oi, :, :],
                  reads=[K2("stk01") if oi < 2 else K2("stk2")], writes=[("dram", nm, c, tt)], semkey="o%s%d" % (nm, si))

    chk('a')
    xx_junk = xxT.rearrange("p a b -> p (a b)")[:, 0:2048]
    nxt = {"calls": [], "ptr": 0}

    def drain(n):
        L = nxt["calls"]
        done = 0
        while nxt["ptr"] < len(L) and done < n:
            name, a, k = L[nxt["ptr"]]
            nxt["ptr"] += 1
            getattr(P, name)(*a, **k)
            if k.get("sig", True):
                done += 1

    for tt in range(T // TT):
        tok0 = tt * TT
        if tt == 0:
            for s in range(TT // 128):
                norm_T(C, dr["x"], tok0 + s * 128, PV_ANORM, xnT, "xnT", s * 128, xt, "xtA", junk, ss)
        chk(1)
        xn_all = [("xnT", c) for c in range(NCH)]
        P.tt("vector", xxT[:, :, 1:TT], xnT[:, :, 0:TT - 1], xnT[:, :, 1:TT], ALU.subtract, reads=xn_all, writes=["xxT_a"])
        P.tt("vector", xxT[:, :, 0:1], xlast, xnT[:, :, 0:1], ALU.subtract, reads=xn_all + ["xlast"], writes=["xxT_b"])
        P.cp("vector", xlast, xnT[:, :, TT - 1:TT], reads=xn_all + ["xxT_b"], writes=["xlast"])
        xx_all = ["xxT_a", "xxT_b"]

        def lerp(m, dst, dkey, eng0):
            for c in range(NCH):
                eng = "vector" if (c + eng0) % 2 == 0 else "gpsimd"
                P.stt(eng, dst[:, c, :], xxT[:, c, :], pv[:, PV_MIX + m, c:c + 1], xnT[:, c, :], ALU.mult, ALU.add,
                      reads=xn_all + xx_all + ["pv"], writes=[(dkey, c)])

        lerp(3, xs, "xs", 0)
        for k in range(NCH):
            P.mm(ps[0:96, 0, 0:TT], w1b[:, k, :], xs[:, k, :], start=(k == 0), stop=(k == 15),
                 reads=["w1b", ("xs", k)], writes=pk(0, 0, 256), sig=(k == 15))
        P.act(twT[0:96, :], ps[0:96, 0, 0:TT], AF.Tanh, reads=pk(0, 0, 256), writes=["twT"])
        lerp(4, xs, "xs", 1)
        for k in range(NCH):
            P.mm(ps[0:96, 1, 0:TT], a1b[:, k, :], xs[:, k, :], start=(k == 0), stop=(k == 15),
                 reads=["a1b", ("xs", k)], writes=pk(1, 0, 256), sig=(k == 15))
        P.cp("vector", taT[0:96, :], ps[0:96, 1, 0:TT], reads=pk(1, 0, 256), writes=["taT"])
        lerp(5, xs, "xs", 0)
        for j in range(2):
            for k in range(NCH):
                P.mm(ps[:, 2 + j, 0:TT], g1b[:, k, j * 128:(j + 1) * 128], xs[:, k, :], start=(k == 0), stop=(k == 15),
                     reads=["g1b", ("xs", k)], writes=pk(2 + j, 0, 256), sig=(k == 15))
            P.act(sgT[:, j, :], ps[:, 2 + j, 0:TT], AF.Sigmoid, reads=pk(2 + j, 0, 256), writes=[("sgT", j)])
        chk(2)
        for m in range(3):
            lerp(m, xl[m], "xl%d" % m, m)
        chk(3)
        P.add("vector", lambda e: e.memset(cA[1], 0.0), writes=CK[1])
        P.add("vector", lambda e: e.memset(cB[1], 0.0), writes=CK[1])
        nxt["calls"], nxt["ptr"] = [], 0
        if tt + 1 < T // TT:
            C.P = dq = Deferred()
            for s in range(TT // 128):
                norm_T(C, dr["x"], tok0 + TT + s * 128, PV_ANORM, xnT, "xnT", s * 128, xt, "xtA", xx_junk, ss,
                       junk_keys=("xxT_a", "xxT_b"))
            C.P = P
            nxt["calls"] = dq.calls
        for pn in range(4):
            load_panel(C, wps[0], wpks[0], dr["w_r"], 0, pn * 512)
            load_panel(C, wps[1], wpks[1], dr["w_k"], 0, pn * 512)
            for c4 in range(4):
                c = pn * 4 + c4
                s4 = c % 4
                si = c % 2
                hb = (c % 2) * 3

                def half(i):
                    b_ = hb + i // 2
                    lo = (i % 2) * 256
                    return ps[:, b_, lo:lo + TT], pk(b_, lo, lo + 256)

                for m in range(2):
                    o, ok = half(m)
                    for k in range(NCH):
                        P.mm(o, wps[m][:, k, c4 * 128:(c4 + 1) * 128], xl[m][:, k, :], start=(k == 0), stop=(k == 15),
                             reads=[wpks[m], ("xl%d" % m, k)], writes=ok, sig=(k == 15))
                o, ok = half(3)
                P.mm(o, a2b[0:96, c * 128:(c + 1) * 128], taT[0:96, :], reads=["a2b", "taT"], writes=ok)
                o, ok = half(4)
                P.mm(o, w2b[0:96, c * 128:(c + 1) * 128], twT[0:96, :], reads=["w2b", "twT"], writes=ok)
                o, ok = half(5)
                for j in range(2):
                    P.mm(o, g2b[:, j, c * 128:(c + 1) * 128], sgT[:, j, :], start=(j == 0), stop=(j == 1),
                         reads=["g2b", ("sgT", 0), ("sgT", 1)], writes=ok, sig=(j == 1))
                o, ok = half(0)
                P.cp("scalar", r_s[s4], o, reads=ok, writes=[("r_s", s4)])
                o, ok = half(1)
                P.cp("vector", k_s[s4], o, reads=ok, writes=[("k_s", s4)])
                o, ok = half(3)
                P.act(a_s[s4], o, AF.Sigmoid, reads=ok + ["pv"], writes=[("a_s", s4)], bias=pv[:, PV_A0, c:c + 1])
                o, ok = half(4)
                P.act(d_s[s4], o, AF.Sigmoid, reads=ok + ["pv"], writes=[("d_s", s4)], bias=pv[:, PV_W0, c:c + 1])
                o, ok = half(5)
                P.cp("vector", g_sb[si], o, reads=ok, writes=[("g_sb", si)])
                P.dma("sync", dr["Gt"][c * 128:(c + 1) * 128, tok0:tok0 + TT], g_sb[si], reads=[("g_sb", si)],
                      writes=[("dram", "Gt", c, tt)], semkey="oG%d" % si)
            chk(4)
            load_panel(C, wps[0], wpks[0], dr["w_v"], 0, pn * 512)
            for c2 in range(2):
                dfs = []
                for c4 in (2 * c2, 2 * c2 + 1):
                    c = pn * 4 + c4
                    si = c % 2
                    hb = (c % 2) * 3
                    o = ps[:, hb + 1, 0:TT]
                    ok = pk(hb + 1)
                    for k in range(NCH):
                        P.mm(o, wps[0][:, k, c4 * 128:(c4 + 1) * 128], xl[2][:, k, :], start=(k == 0), stop=(k == 15),
                             reads=[wpks[0], ("xl2", k)], writes=ok, sig=(k == 15))
                    P.cp("scalar", v_s[si], o, reads=ok, writes=[("v_s", si)])
                    P.cp("scalar", vb[si].rearrange("p a b -> p (a b)"), o, reads=ok, writes=[("vb", si)])
                for c4 in (2 * c2, 2 * c2 + 1):
                    c = pn * 4 + c4
                    df = Deferred()
                    post(df, c, tt, tok0)
                    dfs.append(df.calls)
                replay_interleaved(P, dfs)
                drain(8)
        drain(10 ** 6)


def phase_A2(C):
    P, A, ps, dr, pv = C.P, C.A, C.ps, C.dr, C.pv
    G = 4
    AR = A.tile([G, 32, 128], BF16)
    BT = A.tile([G, T], BF16)
    KT = A.tile([G, T], BF16)
    Bs = A.tile([G, 32, 64], BF16)
    Ks = A.tile([G, 32, 64], BF16)
    Vs = A.tile([G, 32, 64], BF16)
    GC = A.tile([G, 32], F32)
    NTbs = [A.tile([G, 128], BF16) for _ in range(2)]
    Nbs = [A.tile([G, 128], BF16) for _ in range(2)]
    AKb2 = [A.tile([G, 128], BF16) for _ in range(3)]
    XTF = [A.tile([G, 128], BF16) for _ in range(3)]
    Pms = [[A.tile([G, 128], BF16) for _ in range(2)] for _ in range(2)]
    PTms = [[A.tile([G, 128], BF16) for _ in range(2)] for _ in range(2)]
    XTs = [[A.tile([G, 128], BF16) for _ in range(2)] for _ in range(2)]
    ARB2 = [A.tile([G, 64], BF16) for _ in range(3)]
    ARK2 = [A.tile([G, 64], BF16) for _ in range(3)]
    W1s = A.tile([G, 64], BF16)
    Us = A.tile([G, 64], BF16)
    Sf = A.tile([G, 64], F32)
    Stmp = A.tile([G, 64], F32)
    Sb = A.tile([G, 64], BF16)
    yb = [A.tile([G, 256], F32)] * 2
    ysq = A.tile([G, 256], F32)
    mean = A.tile([G, 256], F32)
    m2 = A.tile([G, 256], F32)
    var = ysq
    bonb = A.tile([G, 256], F32)
    gb = A.tile([G, 256], BF16)
    ygb = [A.tile([G, 256], BF16)] * 2
    print("A2 arena words", A.off)
    m_uts = C.m64[:, 0, :]
    m_uti = C.m64[:, 1, :]
    m_lts = C.m64[:, 2, :]
    for t_, k_ in [(NTbs[0], ("NTb", 0)), (NTbs[1], ("NTb", 1)), (Nbs[0], ("Nb", 0)), (Nbs[1], ("Nb", 1)),
                   (AKb2[0], ("AKb", 0)), (AKb2[1], ("AKb", 1)), (AKb2[2], ("AKb", 2))]:
        P.add("vector", lambda e, t_=t_: e.memset(t_, 0.0), writes=[k_])
    identG = bc(C.identb.unsqueeze(1), [128, G, 128])
    for grp in range(16 // G):
        p0 = grp * G
        f0 = p0 * 128
        fs = slice(f0, f0 + G * 128)
        P.dma("sync", AR, dr["ARt"][fs].rearrange("(g p) c a t -> p g c (a t)", p=128), writes=["AR"], semkey="lAR")
        P.dma("sync", BT, dr["BTt"][fs].rearrange("(g p) t -> p g t", p=128), writes=["BT"], semkey="lBT")
        P.dma("sync", KT, dr["KTt"][fs].rearrange("(g p) t -> p g t", p=128), writes=["KT"], semkey="lKT")
        P.dma("sync", Bs, dr["Bs"][p0:p0 + G].rearrange("g p c k -> p g c k"), writes=["Bs"], semkey="lBs")
        P.dma("sync", Ks, dr["Ks"][p0:p0 + G].rearrange("g p c k -> p g c k"), writes=["Ks"], semkey="lKs")
        P.dma("sync", Vs, dr["Vs"][p0:p0 + G].rearrange("g p c k -> p g c k"), writes=["Vs"], semkey="lVs")
        P.dma("sync", GC, dr["GCt"][fs].rearrange("(g p) c -> p g c", p=128), writes=["GC"], semkey="lGC")
        P.add("vector", lambda e: e.memset(Sf, 0.0), writes=["Sf"])
        P.add("vector", lambda e: e.memset(Sb, 0.0), writes=["Sb"])
        def front(P, c):
            st = c % 2
            par = c % 3
            b0, b1, b2 = 3 * st, 3 * st + 1, 3 * st + 2
            NTb, Nb, Pm, PTm, XT = NTbs[st], Nbs[st], Pms[st], PTms[st], XTs[st]
            kNT, kN = ("NTb", st), ("Nb", st)
            ts_ = slice(c * 64, (c + 1) * 64)
            for g in range(G):
                for h in range(2):
                    hp = slice(h * 64, (h + 1) * 64)
                    st_ = (g == 0)
                    P.mm(ps[hp, b0, g * 128:(g + 1) * 128], BT[hp, g, ts_], AR[hp, g, c, :], start=st_, stop=True,
                         reads=["BT", "AR"], writes=pk(b0), sig=False)
                    P.mm(ps[hp, b1, g * 128:(g + 1) * 128], KT[hp, g, ts_], AR[hp, g, c, :], start=st_, stop=True,
                         reads=["KT", "AR"], writes=pk(b1), sig=False)
                    P.mm(ps[hp, b2, g * 64:(g + 1) * 64], AR[hp, g, c, 0:64], BT[hp, g, ts_], start=st_, stop=True,
                         reads=["BT", "AR"], writes=pk(b2), sig=(g == G - 1 and h == 1))
            psAB = ps[:, b0, :].rearrange("p (g n) -> p g n", g=G)
            psAK = ps[:, b1, :].rearrange("p (g n) -> p g n", g=G)
            psN = ps[:, b2, 0:256].rearrange("p (g n) -> p g n", g=G)
            for h in range(2):
                hp = slice(h * 64, (h + 1) * 64)
                cs_ = slice(h * 64, (h + 1) * 64)
                P.tt("vector", NTb[hp, :, cs_], psAB[hp, :, 0:64], bc(m_uts[hp].unsqueeze(1), [64, G, 64]), ALU.mult,
                     reads=pk(b0) + ["m64"], writes=[kNT])
                P.tt("vector", AKb2[par][hp, :, cs_], psAK[hp, :, 0:64], bc(m_uts[hp].unsqueeze(1), [64, G, 64]), ALU.mult,
                     reads=pk(b1) + ["m64"], writes=[("AKb", par)])
                P.tt("vector", Nb[hp, :, cs_], psN[hp, :, :], bc(m_lts[hp].unsqueeze(1), [64, G, 64]), ALU.mult,
                     reads=pk(b2) + ["m64"], writes=[kN])
            P.tt("vector", ARB2[par], psAB[:, :, 64:128], bc(m_uti.unsqueeze(1), [128, G, 64]), ALU.mult,
                 reads=pk(b0) + ["m64"], writes=[("ARB", par)])
            P.tt("vector", ARK2[par], psAK[:, :, 64:128], bc(m_uti.unsqueeze(1), [128, G, 64]), ALU.mult,
                 reads=pk(b1) + ["m64"], writes=[("ARK", par)])
            xi = 0
            P.tt("gpsimd", XT[0], NTb, identG, ALU.add, reads=[kNT, "identb"], writes=[("XT", st, 0)])
            Pc, PTc, Pk, PTk = Nb, NTb, kN, kNT
            for j in range(1, 6):
                pi = j % 2
                for g in range(G):
                    P.mm(ps[:, b0, g * 128:(g + 1) * 128], PTc[:, g, :], Pc[:, g, :], start=(g == 0), stop=True,
                         reads=[Pk, PTk], writes=pk(b0), sig=(g == G - 1))
                P.cp("scalar", Pm[pi].rearrange("p g n -> p (g n)"), ps[:, b0, :], reads=pk(b0), writes=[("Pm", st, pi)])
                if j < 5:
                    for g in range(G):
                        P.mm(ps[:, b1, g * 128:(g + 1) * 128], Pc[:, g, :], PTc[:, g, :], start=(g == 0), stop=True,
                             reads=[Pk, PTk], writes=pk(b1), sig=(g == G - 1))
                    P.cp("scalar", PTm[pi].rearrange("p g n -> p (g n)"), ps[:, b1, :], reads=pk(b1), writes=[("PTm", st, pi)])
                for g in range(G):
                    P.mm(ps[:, b2, g * 128:(g + 1) * 128], Pm[pi][:, g, :], XT[xi][:, g, :], start=(g == 0), stop=True,
                         reads=[("Pm", st, pi), ("XT", st, xi)], writes=pk(b2), sig=(g == G - 1))
                xdst, xdk = (XTF[par], ("XTF", par)) if j == 5 else (XT[1 - xi], ("XT", st, 1 - xi))
                P.tt("vector", xdst.rearrange("p g n -> p (g n)"), XT[xi].rearrange("p g n -> p (g n)"), ps[:, b2, :],
                     ALU.add, reads=pk(b2) + [("XT", st, xi)], writes=[xdk])
                xi = 1 - xi
                Pc, PTc, Pk, PTk = Pm[pi], PTm[pi], ("Pm", st, pi), ("PTm", st, pi)

        def back(P, c):
            par = c % 3
            ts_ = slice(c * 64, (c + 1) * 64)
            XTf, XTk = XTF[par], ("XTF", par)
            psW = ps[:, 6, 0:256].rearrange("p (g n) -> p g n", g=G)
            psU = ps[:, 6, 256:512].rearrange("p (g n) -> p g n", g=G)
            psY = ps[:, 7, 0:256].rearrange("p (g n) -> p g n", g=G)
            psS = ps[:, 7, 256:512].rearrange("p (g n) -> p g n", g=G)
            kW, kU, kY, kS = pk(6, 0, 256), pk(6, 256, 512), pk(7, 0, 256), pk(7, 256, 512)
            fl = [True, True]
            for g in range(G):
                for h in range(2):
                    hp = slice(h * 64, (h + 1) * 64)
                    P.mm(psW[hp, g, :], AR[hp, g, c, 0:64], Sb[hp, g, :], start=fl[h], stop=True,
                         reads=["AR", "Sb"], writes=kW, sig=False)
                    fl[h] = False
            for g in range(G):
                P.mm(psW[:, g, :], AKb2[par][:, g, :], Vs[:, g, c, :], start=False, stop=True,
                     reads=[("AKb", par), "Vs"], writes=kW, sig=(g == G - 1))
            P.cp("scalar", W1s, psW, reads=kW, writes=["W1s"])
            for g in range(G):
                P.mm(psU[:, g, :], XTf[:, g, :], W1s[:, g, :], start=False, stop=True,
                     reads=[XTk, "W1s"], writes=kU, sig=(g == G - 1))
            P.cp("vector", Us, psU, reads=kU, writes=["Us"])
            fl = [True, True]
            for g in range(G):
                for h in range(2):
                    hp = slice(h * 64, (h + 1) * 64)
                    P.mm(psY[hp, g, :], Sb[hp, g, :], AR[hp, g, c, 64:128], start=fl[h], stop=False,
                         reads=["AR", "Sb"], writes=kY, sig=False)
                    fl[h] = False
                    P.mm(psY[hp, g, :], Us[hp, g, :], ARB2[par][hp, g, :], start=False, stop=False,
                         reads=["Us", ("ARB", par)], writes=kY, sig=False)
                    P.mm(psY[hp, g, :], Vs[hp, g, c, :], ARK2[par][hp, g, :], start=False, stop=True,
                         reads=["Vs", ("ARK", par)], writes=kY, sig=(g == G - 1 and h == 1))
            ybi = (c // 4) % 2
            q = c % 4
            P.cp("scalar", yb[ybi][:, :, q * 64:(q + 1) * 64], psY, reads=kY, writes=[("yb", 0, q)])
            for g in range(G):
                for h in range(2):
                    hp = slice(h * 64, (h + 1) * 64)
                    P.mm(psS[hp, g, :], Bs[hp, g, c, :], Us[hp, g, :], start=False, stop=False,
                         reads=["Bs", "Us"], writes=kS, sig=False)
                    P.mm(psS[hp, g, :], Ks[hp, g, c, :], Vs[hp, g, c, :], start=False, stop=True,
                         reads=["Ks", "Vs"], writes=kS, sig=(g == G - 1 and h == 1))
            P.tt("gpsimd", Stmp, Sf, bc(GC[:, :, c:c + 1], [128, G, 64]), ALU.mult, reads=["Sf", "GC"], writes=["Stmp"])
            P.tt("vector", Sf, Stmp, psS, ALU.add, reads=["Stmp"] + kS, writes=["Sf"])
            P.cp("scalar", Sb, Sf, reads=["Sf"], writes=["Sb"])
            if q == 3:
                blk = c // 4
                tk0 = blk * 256
                ybk = [("yb", 0, qq) for qq in range(4)]
                y = yb[ybi]
                P.dma("sync", bonb, dr["BON"][fs, tk0:tk0 + 256].rearrange("(g p) t -> p g t", p=128), writes=["bonb"], semkey="lbon")
                P.dma("sync", gb, dr["Gt"][fs, tk0:tk0 + 256].rearrange("(g p) t -> p g t", p=128), writes=["gb"], semkey="lgb")
                P.act(ysq, y, AF.Square, reads=ybk, writes=["ysq", ("var", 0), ("var", 1)])
                for hh in range(2):
                    gs = slice(hh * 2, hh * 2 + 2)
                    for g in (2 * hh, 2 * hh + 1):
                        lo = (g % 2) * 256
                        P.mm(ps[:, 6, lo:lo + 256], C.bones, y[:, g, :], reads=ybk + ["bones"], writes=pk(6))
                        P.mm(ps[:, 7, lo:lo + 256], C.bones, ysq[:, g, :], reads=["ysq", "bones"], writes=pk(7))
                    pm_ = ps[:, 6, :].rearrange("p (g n) -> p g n", g=2)
                    pq_ = ps[:, 7, :].rearrange("p (g n) -> p g n", g=2)
                    P.act(mean[:, gs, :], pm_, AF.Copy, reads=pk(6), writes=[("mean", hh)], scale=1.0 / 64)
                    P.tt("gpsimd", m2[:, gs, :], mean[:, gs, :], mean[:, gs, :], ALU.mult, reads=[("mean", hh)], writes=[("m2", hh)])
                    P.stt("vector", var[:, gs, :], pq_, 1.0 / 64, m2[:, gs, :], ALU.mult, ALU.subtract,
                          reads=pk(7) + [("m2", hh)], writes=[("var", hh), "ysq"])
                    rsqrt(P, var[:, gs, :], var[:, gs, :], 1.0, 64e-5, [("var", hh)], [("var", hh)])
                    P.tt("gpsimd", mean[:, gs, :], y[:, gs, :], mean[:, gs, :], ALU.subtract, reads=ybk + [("mean", hh)], writes=[("mean", hh)])
                    P.tt("vector", mean[:, gs, :], mean[:, gs, :], var[:, gs, :], ALU.mult, reads=[("mean", hh), ("var", hh)], writes=[("mean", hh)])
                for g in range(G):
                    pc = p0 + g
                    P.act(m2[:, g, :], mean[:, g, :], AF.Identity, reads=[("mean", g // 2), "pv"], writes=[("m2", g // 2)],
                          scale=pv[:, PV_LNW, pc:pc + 1], bias=pv[:, PV_LNB, pc:pc + 1])
                P.tt("vector", m2, m2, bonb, ALU.add, reads=[("m2", 0), ("m2", 1), "bonb"], writes=[("m2", 0), ("m2", 1)])
                yi = blk % 2
                P.tt("gpsimd", ygb[yi], m2, gb, ALU.mult, reads=[("m2", 0), ("m2", 1), "gb"], writes=[("ygb", 0)])
                P.dma("sync", dr["ygT"][fs, tk0:tk0 + 256].rearrange("(g p) t -> p g t", p=128), ygb[yi], reads=[("ygb", 0)],
                      writes=[("dram", "ygT", grp, blk)], semkey="oyg")

        def halves(c):
            d = Deferred()
            front(d, c)
            L = d.calls
            m = len(L) // 2
            while m < len(L) and not L[m - 1][2].get("sig", True):
                m += 1
            return L[:m], L[m:]

        fr = {}
        fr[0] = halves(0)
        fr[1] = halves(1)
        replay_interleaved(P, [fr[0][0]])
        replay_interleaved(P, [fr[0][1], fr[1][0]])
        for c in range(32):
            lists = []
            db = Deferred()
            back(db, c)
            lists.append(db.calls)
            if c + 1 < 32:
                lists.append(fr[c + 1][1])
            if c + 2 < 32:
                fr[c + 2] = halves(c + 2)
                lists.append(fr[c + 2][0])
            replay_interleaved(P, lists)


def phase_A3(C):
    P, A, dr = C.P, C.A, C.dr
    TT = 512
    lhs = [A.tile([16, TT], BF16) for _ in range(2)]
    wps = [A.tile([16, 512], BF16) for _ in range(3)]
    wpks = ["wp0", "wp1", "wp2"]
    rt = [(A.tile([512], F32), "rt%d" % i) for i in range(3)]
    cnt = [0, 0]
    plan = [(dr["w_out"], 0, n * 512) for tt in range(T // TT) for n in range(4)]
    WS = WStream(C, plan, wps, wpks)
    for tt in range(T // TT):
        li = tt % 2
        P.dma("sync", lhs[li], dr["ygT"][:, tt * TT:(tt + 1) * TT].rearrange("(c p) t -> p c t", p=128),
              writes=[("lhsA3%d" % li, k) for k in range(16)], semkey="lyg%d" % li)
        proj_tm(C, lhs[li], "lhsA3%d" % li, 16, dr["w_out"], 4, WS, dr["x"], dr["h"], tt * TT, rt, "h", cnt)


def make_mlp(layer, dst_name):
    def phase(C):
        P, A, ps, dr = C.P, C.A, C.ps, C.dr
        TT = 512
        xt = A.tile([2048], F32)
        junk = A.tile([2048], BF16)
        ss = A.tile([4], F32)
        xnTs = [A.tile([16, TT], BF16) for _ in range(2)]
        h1T = A.tile([64, TT], BF16)
        wps = [A.tile([16, 512], BF16) for _ in range(3)]
        wpks = ["wp0", "wp1", "wp2"]
        rl = [A.tile([512], F32) for _ in range(2)]
        rt = [(A.tile([512], F32), "rt%d" % i) for i in range(3)]
        print("MLP arena words", A.off)
        cnt = [0, 0]
        wu = dr["up%d" % layer]
        wd = dr["dn%d" % layer]
        gi = PV_MLP0 + layer
        ec = 0
        plan = []
        for tt in range(T // TT):
            plan += [(wu, 0, fp * 512) for fp in range(16)]
            plan += [(wd, kp * 2048, n * 512) for n in range(4) for kp in range(4)]
        WS = WStream(C, plan, wps, wpks)

        def norm_tile(tt):
            for s in range(4):
                norm_T(C, dr["h"], tt * TT + s * 128, gi, xnTs[tt % 2], ("xnT", tt % 2), s * 128, xt, "xtM", junk, ss,
                       psbanks=(2, 3))

        norm_tile(0)
        for tt in range(T // TT):
            tok0 = tt * TT
            xnT = xnTs[tt % 2]
            xk = ("xnT", tt % 2)
            for fp in range(16):
                wp_, wpk_ = WS.get()
                for j in range(4):
                    fc = fp * 4 + j
                    bank = fc % 2
                    for k in range(16):
                        P.mm(ps[:, bank, :], wp_[:, k, j * 128:(j + 1) * 128], xnT[:, k, :], start=(k == 0), stop=(k == 15),
                             reads=[wpk_, (xk, k)], writes=pk(bank), sig=(k == 15))
                    ri = ec % 2
                    ec += 1
                    P.act(rl[ri], ps[:, bank, :], AF.Relu, reads=pk(bank), writes=[("rl", ri)])
                    P.tt("vector", h1T[:, fc, :], rl[ri], rl[ri], ALU.mult, reads=[("rl", ri)], writes=[("h1T", fc)])
                WS.done()
            streams = []
            wts = []
            C.P = dn = Deferred()
            proj_tm(C, h1T, "h1T", 64, wd, 4, WS, dr["h"], dr[dst_name], tok0, rt, dst_name, cnt)
            C.P = P
            streams.append(dn.calls)
            wts.append(1)
            if tt + 1 < T // TT:
                C.P = dq = Deferred()
                norm_tile(tt + 1)
                C.P = P
                streams.append(dq.calls)
                wts.append(8)
            replay_interleaved(P, streams, wts)
    return phase


def phase_K(C):
    P, A, ps, dr = C.P, C.A, C.ps, C.dr
    TT = 512
    xt = A.tile([2048], F32)
    junk = A.tile([2048], BF16)
    ss = A.tile([4], F32)
    xnT = A.tile([16, TT], BF16)
    wkv = A.tile([16, 512], BF16)
    kv = [A.tile([512], F32) for _ in range(2)]
    sqk = A.tile([4, 64], F32)
    ssum = A.tile([8], F32)
    kn = A.tile([4, 64], F32)
    ta = A.tile([4, 32], F32)
    tb = A.tile([4, 32], F32)
    kr = [A.tile([4, 64], BF16) for _ in range(2)]
    kTs = [A.tile([2, 128], BF16) for _ in range(2)]
    vsb = [A.tile([256], BF16) for _ in range(2)]
    load_panel(C, wkv, "wkv", dr["w_kv"], 0, 0)
    knb = bc(C.pvb[:, 64:128].unsqueeze(1), [128, 4, 64])
    for tt in range(T // TT):
        tok0 = tt * TT
        for s in range(4):
            norm_T(C, dr["h"], tok0 + s * 128, PV_KVN, xnT, "xnT", s * 128, xt, "xtK", junk, ss, psbanks=(6, 7))
        for s in range(4):
            gt = tt * 4 + s
            i2 = s % 2
            bank = s % 2
            for k in range(16):
                P.mm(ps[:, bank, :], xnT[:, k, s * 128:(s + 1) * 128], wkv[:, k, :], start=(k == 0), stop=(k == 15),
                     reads=["wkv", ("xnT", k)], writes=pk(bank), sig=(k == 15))
            P.cp("scalar", kv[i2], ps[:, bank, :], reads=pk(bank), writes=[("kv", i2)])
            k3 = kv[i2][:, 0:256].rearrange("p (h d) -> p h d", h=4)
            P.act(sqk, k3, AF.Square, reads=[("kv", i2)], writes=["sqk"])
            P.add("vector", lambda e: e.tensor_reduce(ssum[:, 0:4], sqk, AX.X, ALU.add), reads=["sqk"], writes=["ssum0"])
            rsqrt(P, ssum[:, 4:8], ssum[:, 0:4], 1.0 / 64, 1e-6, ["ssum0"], ["ssum1"])
            P.tt("vector", kn, k3, bc(ssum[:, 4:8].unsqueeze(2), [128, 4, 64]), ALU.mult, reads=[("kv", i2), "ssum1"], writes=["kn"])
            P.tt("gpsimd", kn, kn, knb, ALU.mult, reads=["kn", "pvb"], writes=["kn"])
            cosb = bc(C.cs[:, gt, 0:32].unsqueeze(1), [128, 4, 32])
            sinb = bc(C.cs[:, gt, 32:64].unsqueeze(1), [128, 4, 32])
            x1 = kn[:, :, 0:32]
            x2 = kn[:, :, 32:64]
            P.tt("vector", ta, x1, cosb, ALU.mult, reads=["kn", "cs"], writes=["ta"])
            P.tt("gpsimd", tb, x2, sinb, ALU.mult, reads=["kn", "cs"], writes=["tb"])
            P.tt("vector", kr[i2][:, :, 0:32], ta, tb, ALU.subtract, reads=["ta", "tb"], writes=[("kr", i2, 0)])
            P.tt("vector", ta, x2, cosb, ALU.mult, reads=["kn", "cs", ("kr", i2, 0)], writes=["ta"])
            P.tt("gpsimd", tb, x1, sinb, ALU.mult, reads=["kn", "cs", ("kr", i2, 0)], writes=["tb"])
            P.tt("vector", kr[i2][:, :, 32:64], ta, tb, ALU.add, reads=["ta", "tb"], writes=[("kr", i2, 1)])
            pT = ps[:, 2 + i2, 0:128].bitcast(BF16).rearrange("p (j t) -> p j t", j=2)
            for j in range(2):
                P.tr(pT[:, j, :], kr[i2][:, 2 * j:2 * j + 2, :].rearrange("p a b -> p (a b)"), C.identb,
                     reads=[("kr", i2, 0), ("kr", i2, 1), "identb"], writes=pk(2 + i2, 0, 128), sig=(j == 1))
            P.cp("vector", kTs[i2], pT, reads=pk(2 + i2, 0, 128), writes=[("kTs", i2)])
            P.dma("sync", dr["KTs"].rearrange("(j a) d t -> (a d) j t", a=2)[:, :, gt * 128:(gt + 1) * 128], kTs[i2],
                  reads=[("kTs", i2)], writes=[("dram", "KTs", gt)], semkey="oKTs%d" % i2)
            P.cp("scalar", vsb[i2], kv[i2][:, 256:512], reads=[("kv", i2)], writes=[("vsb", i2)])
            P.dma("sync", dr["Vt"][gt * 128:(gt + 1) * 128, :], vsb[i2], reads=[("vsb", i2)],
                  writes=[("dram", "Vt", gt)], semkey="oVt%d" % i2)


def phase_B(C):
    P, A, ps, dr = C.P, C.A, C.ps, C.dr
    TT = 256
    NS = TT // 128
    KTsb = A.tile([4, T], BF16)
    Vsb = A.tile([16, 256], BF16)
    onesb = A.tile([64], BF16)
    xt = A.tile([2048], F32)
    junk = A.tile([2048], BF16)
    ss = A.tile([4], F32)
    xnTs = [A.tile([16, TT], BF16) for _ in range(2)]
    q_tm = [A.tile([32, 64], F32) for _ in range(NS)]
    tmpq = A.tile([32, 64], F32)
    ssum = A.tile([64], F32)
    ta = A.tile([32, 32], F32)
    tb = A.tile([32, 32], F32)
    qr = A.tile([32, 64], BF16)
    QT = A.tile([16, 128], BF16)
    pf = [A.tile([512], F32) for _ in range(2)]
    PT = [A.tile([4, 128], BF16) for _ in range(4)]
    den = A.tile([4, 128], F32)
    oT = A.tile([16, TT], BF16)
    wps = [A.tile([16, 512], BF16) for _ in range(3)]
    wpks = ["wp0", "wp1", "wp2"]
    rt = [(A.tile([512], F32), "rt%d" % i) for i in range(2)]
    print("B arena words", A.off)
    plan = []
    for tt in range(T // TT):
        plan += [(dr["w_q"], 0, n * 512) for n in range(4)]
        plan += [(dr["w_o"], 0, n * 512) for n in range(4)]
    WS = WStream(C, plan, wps, wpks)
    for dup in range(2):
        P.dma("sync", KTsb[dup * 64:(dup + 1) * 64], dr["KTs"].rearrange("g d t -> d g t"), writes=["KTsb%d" % dup], semkey="lKTs%d" % dup)
    P.dma("sync", Vsb, dr["Vt"].rearrange("(b p) f -> p b f", p=128), writes=["Vsb"], semkey="lVsb")
    P.add("vector", lambda e: e.memset(onesb, 1.0), writes=["onesb"])
    qnb = bc(C.pvb[:, 0:64].unsqueeze(1), [128, 32, 64])
    cnt = [0, 0]
    pcnt = 0
    def norm_tile(tt):
        for s in range(NS):
            norm_T(C, dr["h"], tt * TT + s * 128, PV_BN, xnTs[tt % 2], ("xnT", tt % 2), s * 128, xt, "xtB", junk, ss,
                   psbanks=(6, 7))

    norm_tile(0)
    for tt in range(T // TT):
        tok0 = tt * TT
        xnT = xnTs[tt % 2]
        xk = ("xnT", tt % 2)
        for n in range(4):
            wp_, wpk_ = WS.get()
            for s in range(NS):
                bank = s
                for k in range(16):
                    P.mm(ps[:, bank, :], xnT[:, k, s * 128:(s + 1) * 128], wp_[:, k, :], start=(k == 0), stop=(k == 15),
                         reads=[wpk_, (xk, k)], writes=pk(bank), sig=(k == 15))
                dstq = q_tm[s].rearrange("p h d -> p (h d)")[:, n * 512:(n + 1) * 512]
                if (n + s) % 2 == 0:
                    P.cp("scalar", dstq, ps[:, bank, :], reads=pk(bank), writes=[("q_tm", s, n)])
                else:
                    P.cp("vector", dstq, ps[:, bank, :], reads=pk(bank), writes=[("q_tm", s, n)])
            WS.done()
        for s in range(NS):
            nb = tt * NS + s
            qk = [("q_tm", s, n) for n in range(4)]
            q3 = q_tm[s]
            P.act(tmpq, q3, AF.Square, reads=qk, writes=["tmpq"])
            P.add("vector", lambda e: e.tensor_reduce(ssum[:, 0:32], tmpq, AX.X, ALU.add), reads=["tmpq"], writes=["ssum0"])
            rsqrt(P, ssum[:, 32:64], ssum[:, 0:32], 1.0 / 64, 1e-6, ["ssum0"], ["ssum1"])
            P.tt("vector", tmpq, q3, bc(ssum[:, 32:64].unsqueeze(2), [128, 32, 64]), ALU.mult, reads=qk + ["ssum1", "tmpq"], writes=["tmpq"])
            P.tt("gpsimd", tmpq, tmpq, qnb, ALU.mult, reads=["tmpq", "pvb"], writes=["tmpq"])
            cosb = bc(C.cs[:, nb, 0:32].unsqueeze(1), [128, 32, 32])
            sinb = bc(C.cs[:, nb, 32:64].unsqueeze(1), [128, 32, 32])
            x1 = tmpq[:, :, 0:32]
            x2 = tmpq[:, :, 32:64]
            P.tt("vector", ta, x1, cosb, ALU.mult, reads=["tmpq", "cs"], writes=["ta"])
            P.tt("gpsimd", tb, x2, sinb, ALU.mult, reads=["tmpq", "cs"], writes=["tb"])
            P.tt("vector", qr[:, :, 0:32], ta, tb, ALU.subtract, reads=["ta", "tb"], writes=["qr0"])
            P.tt("vector", ta, x2, cosb, ALU.mult, reads=["tmpq", "cs", "qr0"], writes=["ta"])
            P.tt("gpsimd", tb, x1, sinb, ALU.mult, reads=["tmpq", "cs", "qr0"], writes=["tb"])
            P.tt("vector", qr[:, :, 32:64], ta, tb, ALU.add, reads=["ta", "tb"], writes=["qr1"])
            for hb_ in range(2):
                pT = ps[:, 6 + hb_, :].bitcast(BF16).rearrange("p (c t) -> p c t", c=8)
                for cc in range(8):
                    c = hb_ * 8 + cc
                    P.tr(pT[:, cc, :], qr[:, 2 * c:2 * c + 2, :].rearrange("p a b -> p (a b)"), C.identb,
                         reads=["qr0", "qr1", "identb"], writes=pk(6 + hb_), sig=(cc == 7))
                if hb_ == 0:
                    P.cp("scalar", QT[:, 0:8, :], pT, reads=pk(6), writes=[("QT", 0)])
                else:
                    P.cp("vector", QT[:, 8:16, :], pT, reads=pk(7), writes=[("QT", 1)])
            for g in range(4):
                js = [1] if nb == 0 else [0, 1]
                pts = {}
                for hp_ in range(2):
                    hp = slice(hp_ * 64, (hp_ + 1) * 64)
                    for j in js:
                        kb = nb - 1 + j
                        bank = pcnt % 4
                        pcnt += 1
                        P.mm(ps[:, bank, :], KTsb[hp, g, kb * 128:(kb + 1) * 128],
                             QT[hp, 4 * g:4 * g + 4, :].rearrange("p c t -> p (c t)"),
                             reads=["KTsb%d" % hp_, ("QT", g // 2)], writes=pk(bank))
                        pfi = pcnt % 2
                        P.act(pf[pfi], ps[:, bank, :], AF.Exp, reads=pk(bank), writes=[("pf", pfi)], scale=0.125)
                        pti = hp_ * 2 + j
                        msk = C.am[:, 0, :] if j == 1 else C.am[:, 1, :]
                        eng = "vector" if pcnt % 2 == 0 else "gpsimd"
                        P.tt(eng, PT[pti], pf[pfi].rearrange("p (c t) -> p c t", c=4), bc(msk.unsqueeze(1), [128, 4, 128]),
                             ALU.mult, reads=[("pf", pfi), "am"], writes=[("PT", pti)])
                        pts[(hp_, j)] = pti
                psO = ps[:, 4, :].rearrange("p (c t) -> p c t", c=4)
                psD = ps[:, 5, :].rearrange("p (c t) -> p c t", c=4)
                for hp_ in range(2):
                    hp = slice(hp_ * 64, (hp_ + 1) * 64)
                    fo = True
                    for j in js:
                        kb = nb - 1 + j
                        pti = pts[(hp_, j)]
                        for ci in range(4):
                            lastmm = (hp_ == 1 and j == js[-1] and ci == 3)
                            P.mm(psO[hp, ci, :], Vsb[:, kb, g * 64:(g + 1) * 64], PT[pti][:, ci, :], start=fo, stop=True,
                                 reads=["Vsb", ("PT", pti)], writes=pk(4), sig=False)
                            P.mm(psD[hp, ci, :], onesb, PT[pti][:, ci, :], start=fo, stop=True,
                                 reads=["onesb", ("PT", pti)], writes=pk(5), sig=lastmm)
                            fo = False
                P.tt("vector", den, psD, bc(C.esk[:, 4 * g:4 * g + 4].unsqueeze(2), [128, 4, 128]), ALU.add,
                     reads=pk(5) + ["esk"], writes=["den"])
                P.add("vector", lambda e: e.reciprocal(den, den), reads=["den"], writes=["den"])
                P.tt("vector", oT[:, 4 * g:4 * g + 4, s * 128:(s + 1) * 128], psO, den, ALU.mult,
                     reads=pk(4) + ["den"], writes=[("oT", 4 * g + i) for i in range(4)])
        streams = []
        wts = []
        C.P = dn = Deferred()
        proj_tm(C, oT, "oT", 16, dr["w_o"], NS, WS, dr["h"], dr["out"], tok0, rt, "out", cnt)
        C.P = P
        streams.append(dn.calls)
        wts.append(1)
        if tt + 1 < T // TT:
            C.P = dq = Deferred()
            norm_tile(tt + 1)
            C.P = P
            streams.append(dq.calls)
            wts.append(10)
        replay_interleaved(P, streams, wts)


ALL_PHASES = None


def get_phases():
    return [phase_A1, phase_A2, phase_A3, make_mlp(0, "h"), phase_K, phase_B_pre, make_mlp(1, "out")]


def phase_B_pre(C):
    dr = C.dr
    o = dr["out"]
    dr["out"] = dr["h"]
    phase_B(C)
    dr["out"] = o


def host_consts():
    f = np.float32
    ident = np.eye(128, dtype=f)
    bones = np.zeros((128, 128), f)
    bones[:64, :64] = 1
    bones[64:, 64:] = 1
    p = np.arange(128)[:, None] % 64
    t = np.arange(64)[None, :]
    m64 = np.concatenate([(t > p), (t >= p), (t < p)], axis=1).astype(f)
    pos = np.arange(T, dtype=f)
    half = 32
    inv = np.power(f(10000.0), -np.arange(half, dtype=f) / f(half)).astype(f)
    ang = (pos[:, None] * inv[None, :]).astype(f)
    cs = np.concatenate([np.cos(ang), np.sin(ang)], axis=1).astype(f)
    cs = cs.reshape(16, 128, 64).transpose(1, 0, 2).reshape(128, 16 * 64)
    s = np.arange(128)[:, None]
    q = np.arange(128)[None, :]
    am = np.concatenate([(s <= q), (s > q)], axis=1).astype(f)
    return dict(c_ident=ident, c_bones=bones, c_m64=np.ascontiguousarray(m64), c_cs=np.ascontiguousarray(cs), c_am=np.ascontiguousarray(am))


def host_inputs(inp):
    f = np.float32
    g = lambda k: np.asarray(inp[k], dtype=f)
    vecs = [g("a_norm")[0]] + [g("a_mix")[0, i] for i in range(6)] + [g("a_w0")[0], g("a_a0")[0], g("a_k_k")[0], g("a_k_a")[0],
            g("a_r_k")[0].reshape(-1), g("a_ln_x_w")[0], g("a_ln_x_b")[0], g("mlp_norm")[0], g("mlp_norm")[1], g("kv_norm"),
            g("b_norm")[0]]
    pv = np.zeros((128, NV, 16), f)
    for i, v in enumerate(vecs):
        pv[:, i, :] = v.reshape(16, 128).T
    sk = g("b_sinks")[0]
    for c in range(16):
        pv[:64, PV_SK, c] = sk[2 * c]
        pv[64:, PV_SK, c] = sk[2 * c + 1]
    pvb = np.zeros((128, 128), f)
    pvb[:, 0:64] = g("b_q_norm")[0][None, :]
    pvb[:, 64:128] = g("k_norm")[None, :]
    wr = g("a_w_rkv")[0]
    shared = dict(
        w_r=wr[0], w_k=wr[1], w_v=wr[2], w_out=g("a_w_out")[0], w_q=g("b_w_q")[0], w_o=g("b_w_o")[0],
        w1=g("a_w1")[0], w2=g("a_w2")[0], a1=g("a_a1")[0], a2=g("a_a2")[0], g1=g("a_g1")[0], g2=g("a_g2")[0],
        up0=g("mlp_w_up")[0], up1=g("mlp_w_up")[1], dn0=g("mlp_w_down")[0], dn1=g("mlp_w_down")[1],
        w_kv=g("w_kv"), pv=pv.reshape(128, NV * 16), pvb=pvb)
    shared.update(host_consts())
    shared = {k: np.ascontiguousarray(v) for k, v in shared.items()}
    return shared


_CACHE = {}


def kernel(**inputs):
    shared = host_inputs(inputs)
    x = np.asarray(inputs["x"], dtype=np.float32)
    if "nc" not in _CACHE:
        _CACHE["nc"] = build(get_phases())
    nc, C = _CACHE["nc"]
    in_maps = []
    for b in range(NCORES):
        m = dict(shared)
        m["x"] = np.ascontiguousarray(x[b])
        in_maps.append(m)
    res = run_bass_kernel_spmd(nc, in_maps, core_ids=list(range(NCORES)))
    return np.stack([res.results[b]["out"] for b in range(NCORES)], axis=0).astype(np.float32)
```

```python
import math
import os
from contextlib import ExitStack
import numpy as np
import ml_dtypes
import concourse.bass as bass
import concourse.mybir as mybir
from concourse.bass_utils import run_bass_kernel_spmd

F32 = mybir.dt.float32
BF16 = mybir.dt.bfloat16
AF = mybir.ActivationFunctionType
ALU = mybir.AluOpType
AX = mybir.AxisListType

D = 2048
T = 2048
NCH = 16
FF = 8192
NCORES = 8
ENGS = ["tensor", "vector", "scalar", "gpsimd", "sync"]
ARENA_WORDS = 45056


class Op:
    __slots__ = ("eng", "fn", "reads", "writes", "sig", "dma", "semkey", "csem", "cval", "waits", "preads")

    def __init__(self, eng, fn, reads, writes, sig, dma, semkey):
        self.eng = eng
        self.fn = fn
        self.reads = reads
        self.writes = writes
        self.sig = sig
        self.dma = dma
        self.semkey = semkey
        self.csem = None
        self.cval = None
        self.waits = []


class Prog:
    def __init__(self, nc):
        self.nc = nc
        self.ops = []
        self.psq = {}

    def add(self, eng, fn, reads=(), writes=(), sig=True, dma=False, semkey=None):
        ispk = lambda b: isinstance(b, tuple) and len(b) == 2 and b[0] == "ps"
        pr = frozenset(b for b in reads if ispk(b) and b not in writes)
        rd = tuple(b for b in reads if not ispk(b))
        wr = tuple(writes) + tuple(pr)
        op = Op(eng, fn, rd, wr, sig, dma, semkey)
        op.preads = pr
        self.ops.append(op)
        return op

    def barrier(self):
        self.ops.append(None)

    def mm(self, out, lhsT, rhs, start=True, stop=True, reads=(), writes=(), sig=True):
        return self.add("tensor", lambda e: e.matmul(out, lhsT, rhs, start=start, stop=stop,
                                                     skip_group_check=True),
                        reads, writes, sig=sig)

    def tr(self, out, in_, ident, reads=(), writes=(), sig=True):
        return self.add("tensor", lambda e: e.transpose(out, in_, ident), reads, writes, sig=sig)

    def act(self, out, in_, func, reads=(), writes=(), **kw):
        return self.add("scalar", lambda e: e.activation(out, in_, func, **kw), reads, writes)

    def dma(self, eng, out, in_, reads=(), writes=(), semkey=None):
        assert semkey is not None
        return self.add(eng, lambda e: e.dma_start(out, in_), reads, writes, dma=True, semkey=semkey)

    def tt(self, eng, out, a, b, op, reads=(), writes=()):
        return self.add(eng, lambda e: e.tensor_tensor(out, a, b, op), reads, writes)

    def ts(self, eng, out, a, s1, s2, op0, op1=None, reads=(), writes=()):
        if op1 is None:
            return self.add(eng, lambda e: e.tensor_scalar(out, a, s1, None, op0), reads, writes)
        return self.add(eng, lambda e: e.tensor_scalar(out, a, s1, s2, op0, op1), reads, writes)

    def stt(self, eng, out, a, s, b, op0, op1, reads=(), writes=()):
        eng = "vector"
        return self.add(eng, lambda e: e.scalar_tensor_tensor(out, a, s, b, op0, op1), reads, writes)

    def cp(self, eng, out, a, reads=(), writes=()):
        if eng == "scalar":
            return self.act(out, a, AF.Copy, reads, writes)
        return self.add(eng, lambda e: e.tensor_copy(out, a), reads, writes)

    def emit(self, stack):
        nc = self.nc
        ops = self.ops
        eng_cnt = {e: 0 for e in ENGS}
        dma_cnt = {}
        pending = {e: [] for e in ENGS}
        lastop = {}
        for op in ops:
            if op is not None:
                lastop[op.eng] = op
        for op in lastop.values():
            op.sig = True
        for op in ops:
            if op is None:
                continue
            if op.dma:
                n = dma_cnt.get(op.semkey, 0) + 1
                dma_cnt[op.semkey] = n
                op.csem = ("d", op.semkey)
                op.cval = 16 * n
            elif op.sig:
                eng_cnt[op.eng] += 1
                op.csem = ("e", op.eng)
                op.cval = eng_cnt[op.eng]
                for q in pending[op.eng]:
                    q.csem = op.csem
                    q.cval = op.cval
                pending[op.eng] = []
            else:
                pending[op.eng].append(op)
        for e in ENGS:
            assert not pending[e], e
        last_w = {}
        readers = {}
        waited = {e: {} for e in ENGS}
        all_prev = {}
        barrier_req = {e: None for e in ENGS}
        for op in ops:
            if op is None:
                snap = dict(all_prev)
                for e in ENGS:
                    barrier_req[e] = snap
                continue
            need = {}
            if barrier_req[op.eng] is not None:
                for s, v in barrier_req[op.eng].items():
                    need[s] = max(need.get(s, 0), v)
                barrier_req[op.eng] = None
            for b in op.reads:
                o = last_w.get(b)
                if o is not None and o is not op:
                    if not (o.eng == "tensor" and op.eng == "tensor"):
                        assert o.cval is not None
                        need[o.csem] = max(need.get(o.csem, 0), o.cval)
            for b in op.writes:
                o = last_w.get(b)
                lst = list(readers.get(b, ()))
                if o is not None:
                    lst.append(o)
                for o in lst:
                    if o is op:
                        continue
                    if o.eng == "tensor" and op.eng == "tensor":
                        continue
                    if o.eng == op.eng and b in op.preads and b in o.preads:
                        continue
                    need[o.csem] = max(need.get(o.csem, 0), o.cval)
            for b in op.reads:
                readers.setdefault(b, []).append(op)
            for b in op.writes:
                last_w[b] = op
                readers[b] = []
            w = waited[op.eng]
            for s, v in need.items():
                if s == op.csem and v >= op.cval:
                    raise RuntimeError("self-wait deadlock")
                if w.get(s, 0) < v:
                    w[s] = v
                    op.waits.append((s, v))
            all_prev[op.csem] = max(all_prev.get(op.csem, 0), op.cval)
        final = dict(all_prev)
        sems = {}
        for op in ops:
            if op is None:
                continue
            if op.csem not in sems:
                sems[op.csem] = stack.enter_context(nc.semaphore("s%d" % len(sems)))
        self.n_sems = len(sems)
        block = stack.enter_context(nc.Block())
        per_eng = {e: [o for o in ops if o is not None and o.eng == e] for e in ENGS}

        def body(engname):
            def f(eng):
                for op in per_eng[engname]:
                    for s, v in op.waits:
                        eng.wait_ge(sems[s], v)
                    ins = op.fn(eng)
                    if op.dma:
                        ins.then_inc(sems[op.csem], 16)
                    elif op.sig:
                        ins.then_inc(sems[op.csem], 1)
                if engname == "sync":
                    for s, v in final.items():
                        eng.wait_ge(sems[s], v)
            return f

        block.tensor(body("tensor"))
        block.vector(body("vector"))
        block.scalar(body("scalar"))
        block.gpsimd(body("gpsimd"))
        block.sync(body("sync"))
        return {e: len(per_eng[e]) for e in ENGS}, dict(eng_cnt)


class Arena:
    def __init__(self, ap):
        self.ap = ap
        self.off = 0
        self.cap = ap.shape[1]

    def tile(self, shape, dt):
        n = int(np.prod(shape))
        nb = n * (4 if dt == F32 else 2)
        nw = (nb + 3) // 4
        nw = (nw + 7) // 8 * 8
        a = self.ap[:, self.off:self.off + nw]
        self.off += nw
        assert self.off <= self.cap, ("arena overflow", self.off, self.cap)
        if dt != F32:
            a = a.bitcast(dt)
        a = a[:, 0:n]
        if len(shape) == 2:
            a = a.rearrange("p (a b) -> p a b", a=shape[0])
        elif len(shape) == 3:
            a = a.rearrange("p (a b c) -> p a b c", a=shape[0], b=shape[1])
        elif len(shape) == 4:
            a = a.rearrange("p (a b c d) -> p a b c d", a=shape[0], b=shape[1], c=shape[2])
        return a


def bc(ap, shape):
    return ap.to_broadcast(list(shape))


PV_ANORM, PV_MIX, PV_W0, PV_A0, PV_KK, PV_KA, PV_RK, PV_LNW, PV_LNB = 0, 1, 7, 8, 9, 10, 11, 12, 13
PV_MLP0, PV_MLP1, PV_KVN, PV_BN, PV_SK = 14, 15, 16, 17, 18
NV = 19
EXPM05 = math.exp(-0.5)


class Ctx:
    pass


class _Stop(Exception):
    pass


def chk(n):
    if os.environ.get('KSTOP') == str(n):
        raise _Stop()


def build(phases, debug_outs=()):
    nc = bass.Bass("TRN2", target_bir_lowering=False)
    C = Ctx()
    C.nc = nc
    dr = {}

    def din(name, shape, dt=F32):
        dr[name] = nc.dram_tensor(name, list(shape), dt, kind="ExternalInput").ap()

    def dscr(name, shape, dt=F32):
        kind = "ExternalOutput" if name in debug_outs else "Internal"
        dr[name] = nc.dram_tensor(name, list(shape), dt, kind=kind).ap()

    din("x", [T, D])
    for w in ["w_r", "w_k", "w_v", "w_out", "w_q", "w_o"]:
        din(w, [D, D])
    din("w1", [D, 96]); din("w2", [96, D]); din("a1", [D, 96]); din("a2", [96, D])
    din("g1", [D, 256]); din("g2", [256, D])
    din("up0", [D, FF]); din("up1", [D, FF]); din("dn0", [FF, D]); din("dn1", [FF, D])
    din("w_kv", [D, 512])
    din("pv", [128, NV * 16])
    din("pvb", [128, 128])
    din("c_ident", [128, 128]); din("c_bones", [128, 128])
    din("c_m64", [128, 3 * 64])
    din("c_cs", [128, 16 * 64])
    din("c_am", [128, 256])
    dr["out"] = nc.dram_tensor("out", [T, D], F32, kind="ExternalOutput").ap()
    dscr("h", [T, D])
    dscr("ARt", [D, 32, 2, 64], BF16); dscr("BTt", [D, T], BF16); dscr("KTt", [D, T], BF16)
    dscr("Bs", [16, 128, 32, 64], BF16); dscr("Ks", [16, 128, 32, 64], BF16); dscr("Vs", [16, 128, 32, 64], BF16)
    dscr("GCt", [D, 32]); dscr("BON", [D, T]); dscr("Gt", [D, T], BF16)
    dscr("ygT", [D, T], BF16)
    dscr("KTs", [4, 64, T], BF16); dscr("Vt", [T, 256], BF16)
    C.dr = dr

    with ExitStack() as st:
        arena_ap = st.enter_context(nc.sbuf_tensor("arena", [128, ARENA_WORDS], F32))
        ps = st.enter_context(nc.psum_tensor("ps", [128, 8, 512], F32))
        C.ps = ps
        P = Prog(nc)
        C.P = P
        A = Arena(arena_ap)
        C.A = A
        C.ident = A.tile([128], F32)
        C.identb = A.tile([128], BF16)
        C.bones = A.tile([128], F32)
        C.m64 = A.tile([3, 64], F32)
        C.pv = A.tile([NV, 16], F32)
        C.pvb = A.tile([128], F32)
        C.cs = A.tile([16, 64], F32)
        C.am = A.tile([2, 128], F32)
        C.esk = A.tile([16], F32)
        P.dma("sync", C.ident, dr["c_ident"], writes=["ident"], semkey="c0")
        P.dma("sync", C.bones, dr["c_bones"], writes=["bones"], semkey="c1")
        P.dma("sync", C.m64, dr["c_m64"].rearrange("p (a b) -> p a b", a=3), writes=["m64"], semkey="c2")
        P.dma("sync", C.pv, dr["pv"].rearrange("p (a b) -> p a b", a=NV), writes=["pv"], semkey="c3")
        P.dma("sync", C.pvb, dr["pvb"], writes=["pvb"], semkey="c4")
        P.dma("sync", C.cs, dr["c_cs"].rearrange("p (a b) -> p a b", a=16), writes=["cs"], semkey="c5")
        P.dma("sync", C.am, dr["c_am"].rearrange("p (a b) -> p a b", a=2), writes=["am"], semkey="c6")
        P.cp("vector", C.identb, C.ident, reads=["ident"], writes=["identb"])
        P.act(C.esk, C.pv[:, PV_SK, :], AF.Exp, reads=["pv"], writes=["esk"])
        C.base = A.off
        for ph in phases:
            A.off = C.base
            P.barrier()
            try:
                ph(C)
            except _Stop:
                break
        P.barrier()
        info = P.emit(st)
        C.info = info
    return nc, C


def pk(bank, lo=0, hi=512):
    return [("ps", bank)]


def rsqrt(P, out, in_, scale, eps, rk, wk):
    P.ts("vector", out, in_, scale, eps, ALU.mult, ALU.add, reads=rk, writes=wk)
    P.act(out, out, AF.Sqrt, reads=wk, writes=wk)
    P.add("vector", lambda e: e.reciprocal(out, out), reads=wk, writes=wk)


def norm_T(C, src, tok0, gain_idx, dst, dst_key, col0, xt, xtk, junk, ss, psbanks=(6, 7), junk_keys=("junk",)):
    P = C.P
    ps = C.ps
    P.dma("sync", xt, src[tok0:tok0 + 128, :], writes=[xtk], semkey=xtk)
    chk('n0')
    P.act(junk, xt, AF.Square, reads=[xtk], writes=list(junk_keys) + ["ss0"], accum_out=ss[:, 0:1])
    chk('n1')
    rsqrt(P, ss[:, 2:3], ss[:, 0:1], 1.0 / D, 1e-6, ["ss0"], ["ss2"])
    chk('n2')
    P.ts("vector", xt, xt, ss[:, 2:3], None, ALU.mult, reads=[xtk, "ss2"], writes=[xtk])
    chk('n3')
    for gq in range(NCH // 4):
        b = psbanks[gq % 2]
        k_ = pk(b)
        for q in range(4):
            c = gq * 4 + q
            P.tr(ps[:, b, q * 128:(q + 1) * 128], xt[:, c * 128:(c + 1) * 128], C.ident, reads=[xtk, "ident"], writes=k_,
                 sig=(q == 3))
        for q in range(4):
            c = gq * 4 + q
            o = dst[:, c, col0:col0 + 128]
            g = C.pv[:, gain_idx, c:c + 1]
            if gq % 2 == 0:
                P.ts("vector", o, ps[:, b, q * 128:(q + 1) * 128], g, None, ALU.mult, reads=k_ + ["pv"], writes=[(dst_key, c)])
            else:
                P.act(o, ps[:, b, q * 128:(q + 1) * 128], AF.Copy, reads=k_ + ["pv"], writes=[(dst_key, c)], scale=g)


def load_panel(C, wp, wpk, w, r0, c0, rows=2048, cols=512):
    kc = rows // 128
    C.P.dma("gpsimd", wp[:, 0:kc, 0:cols], w[r0:r0 + rows, c0:c0 + cols].rearrange("(k p) n -> p k n", p=128),
            writes=[wpk], semkey=wpk)


class Deferred:
    def __init__(self):
        self.calls = []

    def __getattr__(self, name):
        def f(*a, **k):
            self.calls.append((name, a, k))
        return f


def replay_interleaved(P, lists, weights=None):
    ptr = [0] * len(lists)
    if weights is None:
        weights = [1] * len(lists)
    while any(ptr[i] < len(l) for i, l in enumerate(lists)):
        for i, l in enumerate(lists):
            for _ in range(weights[i]):
                while ptr[i] < len(l):
                    name, a, k = l[ptr[i]]
                    ptr[i] += 1
                    getattr(P, name)(*a, **k)
                    if k.get("sig", True):
                        break


class WStream:
    def __init__(self, C, plan, tiles, keys):
        self.C, self.plan, self.tiles, self.keys = C, plan, tiles, keys
        self.n = len(tiles)
        self.issued = 0
        self.ndone = 0
        self.nget = 0

    def _pump(self):
        while self.issued < len(self.plan) and self.issued - self.n < self.ndone:
            j = self.issued
            w, r0, c0 = self.plan[j]
            load_panel(self.C, self.tiles[j % self.n], self.keys[j % self.n], w, r0, c0)
            self.issued += 1

    def get(self):
        self._pump()
        j = self.nget
        assert j < self.issued
        self.nget += 1
        return self.tiles[j % self.n], self.keys[j % self.n]

    def done(self):
        self.ndone += 1
        self._pump()


def proj_tm(C, lhs, lhs_key, KC, w, nsub, WS, res_src, res_dst, tok0, rtiles, name, cnt):
    P = C.P
    ps = C.ps
    npan = KC // 16
    for n in range(4):
        for kp in range(npan):
            wp_, wpk_ = WS.get()
            for k in range(16):
                kk = kp * 16 + k
                for s in range(nsub):
                    bank = 4 + s
                    P.mm(ps[:, bank, :], lhs[:, kk, s * 128:(s + 1) * 128], wp_[:, k, :],
                         start=(kk == 0), stop=(kk == KC - 1),
                         reads=[wpk_, (lhs_key, kk)], writes=pk(bank), sig=(k == 15 and s == nsub - 1))
            WS.done()
        for s in range(nsub):
            bank = 4 + s
            ri = cnt[1] % len(rtiles)
            cnt[1] += 1
            rt, rk = rtiles[ri]
            t0 = tok0 + s * 128
            P.dma("sync", rt, res_src[t0:t0 + 128, n * 512:(n + 1) * 512], writes=[rk], semkey=rk)
            P.tt("vector", rt, rt, ps[:, bank, :], ALU.add, reads=[rk] + pk(bank), writes=[rk])
            P.dma("sync", res_dst[t0:t0 + 128, n * 512:(n + 1) * 512], rt, reads=[rk],
                  writes=[("dram", name, t0 // 128, n)], semkey=rk + "o")


def phase_A1(C):
    P, A, ps, dr, pv = C.P, C.A, C.ps, C.dr, C.pv
    TT = 256
    NQ = TT // 64
    xt = A.tile([2048], F32)
    junk_raw = A.tile([1024], F32)
    junk = junk_raw.bitcast(BF16)
    ss = A.tile([4], F32)
    xnT = A.tile([16, TT], BF16)
    xxT = A.tile([16, TT], BF16)
    xlast = A.tile([16, 1], BF16)
    xl = [A.tile([16, TT], BF16) for _ in range(3)]
    xs_raw = A.tile([2048], F32)
    xs = xs_raw.bitcast(BF16).rearrange("p (k n) -> p k n", k=16)
    stage32 = A.tile([4096], F32)
    wps = [stage32.bitcast(BF16).rearrange("p (k n) -> p k n", k=16), A.tile([16, 512], BF16)]
    wpks = ["wpA0", "wpA1"]
    w1b = A.tile([16, 96], BF16)
    a1b = A.tile([16, 96], BF16)
    g1b = A.tile([16, 256], BF16)
    w2b = A.tile([2048], BF16)
    a2b = A.tile([2048], BF16)
    g2b = A.tile([2, 2048], BF16)
    twT = A.tile([TT], BF16)
    taT = A.tile([TT], BF16)
    sgT = A.tile([2, TT], BF16)

    def ld(dst, src_ap, view, key):
        P.dma("sync", view, src_ap, writes=["wpA0"], semkey="ldsmall")
        P.cp("vector", dst, view, reads=["wpA0"], writes=[key])

    ld(w1b, dr["w1"].rearrange("(k p) n -> p k n", p=128), stage32[:, 0:16 * 96].rearrange("p (k n) -> p k n", k=16), "w1b")
    ld(a1b, dr["a1"].rearrange("(k p) n -> p k n", p=128), stage32[:, 0:16 * 96].rearrange("p (k n) -> p k n", k=16), "a1b")
    ld(g1b, dr["g1"].rearrange("(k p) n -> p k n", p=128), stage32[:, 0:16 * 256].rearrange("p (k n) -> p k n", k=16), "g1b")
    ld(w2b[0:96, :], dr["w2"], stage32[0:96, 0:2048], "w2b")
    ld(a2b[0:96, :], dr["a2"], stage32[0:96, 0:2048], "a2b")
    ld(g2b, dr["g2"].rearrange("(k p) n -> p k n", p=128), stage32[:, 0:4096].rearrange("p (k n) -> p k n", k=2), "g2b")
    P.add("vector", lambda e: e.memset(xlast, 0.0), writes=["xlast"])

    def stg(n, dt=F32, shape=(TT,)):
        return [A.tile(list(shape), dt) for _ in range(n)]

    r_s, k_s, a_s, d_s = stg(4), stg(4), stg(4), stg(4)
    v_s = stg(2)
    g_sb = stg(2, BF16)
    negka = A.tile([16], F32)
    P.ts("vector", negka, pv[:, PV_KA, :], -1.0, None, ALU.mult, reads=["pv"], writes=["negka"])
    TN = ["kk0", "sq", "nkk", "bb", "km", "t1", "rk", "e1", "e2"]
    TMP = {}
    TKY = {}
    for j, nm in enumerate(TN):
        if j < 8:
            alt = xs_raw[:, j * 256:(j + 1) * 256]
            altk = [("xs", 2 * j), ("xs", 2 * j + 1)]
        else:
            alt = stg(1)[0]
            altk = [(nm, 1)]
        TMP[nm] = [stg(1)[0], alt]
        TKY[nm] = [[(nm, 0)], altk]
    cA = [stg(1, F32, (NQ, 96))[0], junk_raw[:, 0:384].rearrange("p (a b) -> p a b", a=NQ)]
    cB = [stg(1, F32, (NQ, 96))[0], junk_raw[:, 384:768].rearrange("p (a b) -> p a b", a=NQ)]
    bon = [stg(1)[0], junk_raw[:, 768:1024]]
    CK = [[("cAB", 0)], ["junk"]]
    BK = [[("bon", 0)], ["junk"]]
    P.add("vector", lambda e: e.memset(cA[0], 0.0), writes=CK[0])
    P.add("vector", lambda e: e.memset(cB[0], 0.0), writes=CK[0])
    gc = stg(2, F32, (NQ,))
    AR = stg(2, BF16, (NQ, 2, 64))
    btil, ktil = stg(2, BF16), stg(2, BF16)
    bhat = stg(2, BF16, (NQ, 64))
    khat = stg(2, BF16, (NQ, 64))
    vb = stg(2, BF16, (NQ, 64))
    stk = stg(2, BF16, (3, NQ, 64))
    print("A1 arena words", A.off)
    v3 = lambda a: a.rearrange("p (a b) -> p a b", a=NQ)

    def post(P, c, tt, tok0):
        s4 = c % 4
        si = c % 2
        hb = (c % 2) * 3
        K4 = lambda nm: (nm, s4)
        K2 = lambda nm: (nm, si)
        rs, ks, as_, ds, vs = r_s[s4], k_s[s4], a_s[s4], d_s[s4], v_s[si]
        kk0, sq, nkk, bb, km, t1, rk, e1, e2 = [TMP[n][si] for n in TN]
        kk0k, sqk, nkkk, bbk, kmk, t1k, rkk, e1k, e2k = [TKY[n][si] for n in TN]
        ck = CK[si]
        bk = BK[si]
        P.act(kk0, ks, AF.Copy, reads=[K4("k_s"), "pv"], writes=kk0k, scale=pv[:, PV_KK, c:c + 1])
        P.act(sq, kk0, AF.Square, reads=kk0k, writes=sqk)
        pA = ps[:, hb + 2, 256:256 + TT]
        pAk = pk(hb + 2)
        P.mm(pA, C.bones, sq, reads=["bones"] + sqk, writes=pAk)
        P.act(sq, pA, AF.Sqrt, reads=pAk, writes=sqk)
        P.ts("vector", sq, sq, 1e-12, None, ALU.max, reads=sqk, writes=sqk)
        P.add("vector", lambda e: e.reciprocal(sq, sq), reads=sqk, writes=sqk)
        P.stt("vector", nkk, kk0, -1.0, sq, ALU.mult, ALU.mult, reads=kk0k + sqk, writes=nkkk)
        P.stt("vector", bb, nkk, -1.0, as_, ALU.mult, ALU.mult, reads=nkkk + [K4("a_s")], writes=bbk)
        P.act(t1, as_, AF.Identity, reads=[K4("a_s"), "pv", "negka"], writes=t1k, scale=pv[:, PV_KA, c:c + 1], bias=negka[:, c:c + 1])
        P.stt("vector", km, t1, 1.0, ks, ALU.add, ALU.mult, reads=t1k + [K4("k_s")], writes=kmk)
        P.stt("vector", rk, rs, pv[:, PV_RK, c:c + 1], km, ALU.mult, ALU.mult, reads=[K4("r_s"), "pv"] + kmk, writes=rkk)
        pB = ps[:, hb + 2, 0:TT]
        P.mm(pB, C.bones, rk, reads=["bones"] + rkk, writes=pAk)
        P.tt("vector", bon[si], pB, vs, ALU.mult, reads=pAk + [K2("v_s")], writes=bk)
        P.dma("sync", dr["BON"][c * 128:(c + 1) * 128, tok0:tok0 + TT], bon[si], reads=bk,
              writes=[("dram", "BON", c, tt)], semkey="oBON%d" % si)
        ca, cb = cA[si], cB[si]
        P.cp("vector", ca[:, :, 32:96], v3(ds), reads=[K4("d_s")], writes=ck)
        src, dst = ca, cb
        for sh in [1, 2, 4, 8, 16, 32]:
            P.tt("vector", dst[:, :, 32:96], src[:, :, 32:96], src[:, :, 32 - sh:96 - sh], ALU.add, reads=ck, writes=ck)
            src, dst = dst, src
        Lp = src[:, :, 32:96]
        P.act(v3(e1), Lp, AF.Exp, reads=ck, writes=e1k, scale=-EXPM05)
        P.act(v3(e2), Lp, AF.Exp, reads=ck, writes=e2k, scale=EXPM05)
        P.tt("gpsimd", v3(t1), Lp, v3(ds), ALU.subtract, reads=ck + [K4("d_s")], writes=t1k)
        P.act(t1, t1, AF.Exp, reads=t1k, writes=t1k, scale=-EXPM05)
        P.tt("gpsimd", v3(kk0), bc(src[:, :, 95:96], [128, NQ, 64]), Lp, ALU.subtract, reads=ck, writes=kk0k)
        P.act(kk0, kk0, AF.Exp, reads=kk0k, writes=kk0k, scale=-EXPM05)
        P.cp("gpsimd", gc[si], v3(e1)[:, :, 63], reads=e1k, writes=[K2("gc")])
        P.dma("sync", dr["GCt"][c * 128:(c + 1) * 128, tt * NQ:(tt + 1) * NQ], gc[si], reads=[K2("gc")],
              writes=[("dram", "GCt", c, tt)], semkey="oGC%d" % si)
        ar = AR[si]
        P.tt("vector", ar[:, :, 0, :], v3(nkk), v3(t1), ALU.mult, reads=nkkk + t1k, writes=[K2("AR0")])
        P.tt("gpsimd", ar[:, :, 1, :], v3(rs), v3(e1), ALU.mult, reads=[K4("r_s")] + e1k, writes=[K2("AR1")])
        P.dma("sync", dr["ARt"][c * 128:(c + 1) * 128, tt * NQ:(tt + 1) * NQ, :, :], ar, reads=[K2("AR0"), K2("AR1")],
              writes=[("dram", "ARt", c, tt)], semkey="oAR%d" % si)
        P.tt("vector", btil[si], bb, e2, ALU.mult, reads=bbk + e2k, writes=[K2("btil")])
        P.dma("sync", dr["BTt"][c * 128:(c + 1) * 128, tok0:tok0 + TT], btil[si], reads=[K2("btil")],
              writes=[("dram", "BTt", c, tt)], semkey="oBT%d" % si)
        P.tt("gpsimd", ktil[si], km, e2, ALU.mult, reads=kmk + e2k, writes=[K2("ktil")])
        P.dma("sync", dr["KTt"][c * 128:(c + 1) * 128, tok0:tok0 + TT], ktil[si], reads=[K2("ktil")],
              writes=[("dram", "KTt", c, tt)], semkey="oKT%d" % si)
        P.tt("vector", bhat[si], v3(bb), v3(kk0), ALU.mult, reads=bbk + kk0k, writes=[K2("bhat")])
        P.tt("gpsimd", khat[si], v3(km), v3(kk0), ALU.mult, reads=kmk + kk0k, writes=[K2("khat")])
        pst = ps[:, hb, 256:512].bitcast(BF16).rearrange("p (a b c) -> p a b c", a=2, b=NQ)
        pst2 = ps[:, hb + 1, 256:384].bitcast(BF16).rearrange("p (b c) -> p b c", b=NQ)
        for oi, (srct, sk_) in enumerate([(bhat[si], "bhat"), (khat[si], "khat"), (vb[si], "vb")]):
            for q in range(NQ):
                for h in range(2):
                    hp = slice(h * 64, (h + 1) * 64)
                    dstp = pst[hp, oi, q, :] if oi < 2 else pst2[hp, q, :]
                    dk_ = pk(hb) if oi < 2 else pk(hb + 1)
                    last = (q == NQ - 1 and h == 1)
                    P.tr(dstp, srct[hp, q, :], C.identb[hp, hp], reads=[K2(sk_), "identb"], writes=dk_, sig=last)
        P.cp("scalar", stk[si][:, 0:2, :, :], pst, reads=pk(hb), writes=[K2("stk01")])
        P.cp("scalar", stk[si][:, 2, :, :], pst2, reads=pk(hb + 1), writes=[K2("stk2")])
        for oi, nm in enumerate(["Bs", "Ks", "Vs"]):
            P.dma("sync", dr[nm][c, :, tt * NQ:(tt + 1) * NQ, :], stk[si][:, oi, :, :],
                  reads=[K2("stk01") if oi < 2 else K2("stk2")], writes=[("dram", nm, c, tt)], semkey="o%s%d" % (nm, si))

    chk('a')
    xx_junk = xxT.rearrange("p a b -> p (a b)")[:, 0:2048]
    nxt = {"calls": [], "ptr": 0}

    def drain(n):
        L = nxt["calls"]
        done = 0
        while nxt["ptr"] < len(L) and done < n:
            name, a, k = L[nxt["ptr"]]
            nxt["ptr"] += 1
            getattr(P, name)(*a, **k)
            if k.get("sig", True):
                done += 1

    for tt in range(T // TT):
        tok0 = tt * TT
        if tt == 0:
            for s in range(TT // 128):
                norm_T(C, dr["x"], tok0 + s * 128, PV_ANORM, xnT, "xnT", s * 128, xt, "xtA", junk, ss)
        chk(1)
        xn_all = [("xnT", c) for c in range(NCH)]
        P.tt("vector", xxT[:, :, 1:TT], xnT[:, :, 0:TT - 1], xnT[:, :, 1:TT], ALU.subtract, reads=xn_all, writes=["xxT_a"])
        P.tt("vector", xxT[:, :, 0:1], xlast, xnT[:, :, 0:1], ALU.subtract, reads=xn_all + ["xlast"], writes=["xxT_b"])
        P.cp("vector", xlast, xnT[:, :, TT - 1:TT], reads=xn_all + ["xxT_b"], writes=["xlast"])
        xx_all = ["xxT_a", "xxT_b"]

        def lerp(m, dst, dkey, eng0):
            for c in range(NCH):
                eng = "vector" if (c + eng0) % 2 == 0 else "gpsimd"
                P.stt(eng, dst[:, c, :], xxT[:, c, :], pv[:, PV_MIX + m, c:c + 1], xnT[:, c, :], ALU.mult, ALU.add,
                      reads=xn_all + xx_all + ["pv"], writes=[(dkey, c)])

        lerp(3, xs, "xs", 0)
        for k in range(NCH):
            P.mm(ps[0:96, 0, 0:TT], w1b[:, k, :], xs[:, k, :], start=(k == 0), stop=(k == 15),
                 reads=["w1b", ("xs", k)], writes=pk(0, 0, 256), sig=(k == 15))
        P.act(twT[0:96, :], ps[0:96, 0, 0:TT], AF.Tanh, reads=pk(0, 0, 256), writes=["twT"])
        lerp(4, xs, "xs", 1)
        for k in range(NCH):
            P.mm(ps[0:96, 1, 0:TT], a1b[:, k, :], xs[:, k, :], start=(k == 0), stop=(k == 15),
                 reads=["a1b", ("xs", k)], writes=pk(1, 0, 256), sig=(k == 15))
        P.cp("vector", taT[0:96, :], ps[0:96, 1, 0:TT], reads=pk(1, 0, 256), writes=["taT"])
        lerp(5, xs, "xs", 0)
        for j in range(2):
            for k in range(NCH):
                P.mm(ps[:, 2 + j, 0:TT], g1b[:, k, j * 128:(j + 1) * 128], xs[:, k, :], start=(k == 0), stop=(k == 15),
                     reads=["g1b", ("xs", k)], writes=pk(2 + j, 0, 256), sig=(k == 15))
            P.act(sgT[:, j, :], ps[:, 2 + j, 0:TT], AF.Sigmoid, reads=pk(2 + j, 0, 256), writes=[("sgT", j)])
        chk(2)
        for m in range(3):
            lerp(m, xl[m], "xl%d" % m, m)
        chk(3)
        P.add("vector", lambda e: e.memset(cA[1], 0.0), writes=CK[1])
        P.add("vector", lambda e: e.memset(cB[1], 0.0), writes=CK[1])
        nxt["calls"], nxt["ptr"] = [], 0
        if tt + 1 < T // TT:
            C.P = dq = Deferred()
            for s in range(TT // 128):
                norm_T(C, dr["x"], tok0 + TT + s * 128, PV_ANORM, xnT, "xnT", s * 128, xt, "xtA", xx_junk, ss,
                       junk_keys=("xxT_a", "xxT_b"))
            C.P = P
            nxt["calls"] = dq.calls
        for pn in range(4):
            if tt == 0 and pn == 0:
                load_panel(C, wps[0], wpks[0], dr["w_r"], 0, pn * 512)
                load_panel(C, wps[1], wpks[1], dr["w_k"], 0, pn * 512)
            is_last = (tt == T // TT - 1 and pn == 3)
            npn = (pn + 1) % 4
            for c4 in range(4):
                c = pn * 4 + c4
                s4 = c % 4
                si = c % 2
                hb = (c % 2) * 3

                def half(i):
                    b_ = hb + i // 2
                    lo = (i % 2) * 256
                    return ps[:, b_, lo:lo + TT], pk(b_, lo, lo + 256)

                for m in range(2):
                    o, ok = half(m)
                    for k in range(NCH):
                        P.mm(o, wps[m][:, k, c4 * 128:(c4 + 1) * 128], xl[m][:, k, :], start=(k == 0), stop=(k == 15),
                             reads=[wpks[m], ("xl%d" % m, k)], writes=ok, sig=(k == 15))
                o, ok = half(3)
                P.mm(o, a2b[0:96, c * 128:(c + 1) * 128], taT[0:96, :], reads=["a2b", "taT"], writes=ok)
                o, ok = half(4)
                P.mm(o, w2b[0:96, c * 128:(c + 1) * 128], twT[0:96, :], reads=["w2b", "twT"], writes=ok)
                o, ok = half(5)
                for j in range(2):
                    P.mm(o, g2b[:, j, c * 128:(c + 1) * 128], sgT[:, j, :], start=(j == 0), stop=(j == 1),
                         reads=["g2b", ("sgT", 0), ("sgT", 1)], writes=ok, sig=(j == 1))
                o, ok = half(0)
                P.cp("scalar", r_s[s4], o, reads=ok, writes=[("r_s", s4)])
                o, ok = half(1)
                P.cp("vector", k_s[s4], o, reads=ok, writes=[("k_s", s4)])
                o, ok = half(3)
                P.act(a_s[s4], o, AF.Sigmoid, reads=ok + ["pv"], writes=[("a_s", s4)], bias=pv[:, PV_A0, c:c + 1])
                o, ok = half(4)
                P.act(d_s[s4], o, AF.Sigmoid, reads=ok + ["pv"], writes=[("d_s", s4)], bias=pv[:, PV_W0, c:c + 1])
                o, ok = half(5)
                P.cp("vector", g_sb[si], o, reads=ok, writes=[("g_sb", si)])
                P.dma("sync", dr["Gt"][c * 128:(c + 1) * 128, tok0:tok0 + TT], g_sb[si], reads=[("g_sb", si)],
                      writes=[("dram", "Gt", c, tt)], semkey="oG%d" % si)
            chk(4)
            load_panel(C, wps[0], wpks[0], dr["w_v"], 0, pn * 512)
            if not is_last:
                load_panel(C, wps[1], wpks[1], dr["w_k"], 0, npn * 512)
            for c2 in range(2):
                dfs = []
                for c4 in (2 * c2, 2 * c2 + 1):
                    c = pn * 4 + c4
                    si = c % 2
                    hb = (c % 2) * 3
                    o = ps[:, hb + 1, 0:TT]
                    ok = pk(hb + 1)
                    for k in range(NCH):
                        P.mm(o, wps[0][:, k, c4 * 128:(c4 + 1) * 128], xl[2][:, k, :], start=(k == 0), stop=(k == 15),
                             reads=[wpks[0], ("xl2", k)], writes=ok, sig=(k == 15))
                    P.cp("scalar", v_s[si], o, reads=ok, writes=[("v_s", si)])
                    P.cp("scalar", vb[si].rearrange("p a b -> p (a b)"), o, reads=ok, writes=[("vb", si)])
                if c2 == 1 and not is_last:
                    load_panel(C, wps[0], wpks[0], dr["w_r"], 0, npn * 512)
                for c4 in (2 * c2, 2 * c2 + 1):
                    c = pn * 4 + c4
                    df = Deferred()
                    post(df, c, tt, tok0)
                    dfs.append(df.calls)
                replay_interleaved(P, dfs)
                drain(8)
        drain(10 ** 6)


def phase_A2(C):
    P, A, ps, dr, pv = C.P, C.A, C.ps, C.dr, C.pv
    G = 4
    AR = A.tile([G, 32, 128], BF16)
    BT = A.tile([G, T], BF16)
    KT = A.tile([G, T], BF16)
    Bs = A.tile([G, 32, 64], BF16)
    Ks = A.tile([G, 32, 64], BF16)
    Vs = A.tile([G, 32, 64], BF16)
    GC = A.tile([G, 32], F32)
    NTbs = [A.tile([G, 128], BF16) for _ in range(2)]
    Nbs = [A.tile([G, 128], BF16) for _ in range(2)]
    AKb2 = [A.tile([G, 128], BF16) for _ in range(3)]
    XTF = [A.tile([G, 128], BF16) for _ in range(3)]
    Pms = [[A.tile([G, 128], BF16) for _ in range(2)] for _ in range(2)]
    PTms = [[A.tile([G, 128], BF16) for _ in range(2)] for _ in range(2)]
    XTs = [[A.tile([G, 128], BF16) for _ in range(2)] for _ in range(2)]
    ARB2 = [A.tile([G, 64], BF16) for _ in range(3)]
    ARK2 = [A.tile([G, 64], BF16) for _ in range(3)]
    W1s = A.tile([G, 64], BF16)
    Us = A.tile([G, 64], BF16)
    Sf = A.tile([G, 64], F32)
    Stmp = A.tile([G, 64], F32)
    Sb = A.tile([G, 64], BF16)
    yb = [A.tile([G, 256], F32)] * 2
    ysq = A.tile([G, 256], F32)
    mean = A.tile([G, 256], F32)
    m2 = A.tile([G, 256], F32)
    var = ysq
    bonb = A.tile([G, 256], F32)
    gb = A.tile([G, 256], BF16)
    ygb = [A.tile([G, 256], BF16)] * 2
    print("A2 arena words", A.off)
    m_uts = C.m64[:, 0, :]
    m_uti = C.m64[:, 1, :]
    m_lts = C.m64[:, 2, :]
    for t_, k_ in [(NTbs[0], ("NTb", 0)), (NTbs[1], ("NTb", 1)), (Nbs[0], ("Nb", 0)), (Nbs[1], ("Nb", 1)),
                   (AKb2[0], ("AKb", 0)), (AKb2[1], ("AKb", 1)), (AKb2[2], ("AKb", 2))]:
        P.add("vector", lambda e, t_=t_: e.memset(t_, 0.0), writes=[k_])
    identG = bc(C.identb.unsqueeze(1), [128, G, 128])
    for grp in range(16 // G):
        p0 = grp * G
        f0 = p0 * 128
        fs = slice(f0, f0 + G * 128)
        P.dma("sync", AR, dr["ARt"][fs].rearrange("(g p) c a t -> p g c (a t)", p=128), writes=["AR"], semkey="lAR")
        P.dma("sync", BT, dr["BTt"][fs].rearrange("(g p) t -> p g t", p=128), writes=["BT"], semkey="lBT")
        P.dma("sync", KT, dr["KTt"][fs].rearrange("(g p) t -> p g t", p=128), writes=["KT"], semkey="lKT")
        P.dma("sync", Bs, dr["Bs"][p0:p0 + G].rearrange("g p c k -> p g c k"), writes=["Bs"], semkey="lBs")
        P.dma("sync", Ks, dr["Ks"][p0:p0 + G].rearrange("g p c k -> p g c k"), writes=["Ks"], semkey="lKs")
        P.dma("sync", Vs, dr["Vs"][p0:p0 + G].rearrange("g p c k -> p g c k"), writes=["Vs"], semkey="lVs")
        P.dma("sync", GC, dr["GCt"][fs].rearrange("(g p) c -> p g c", p=128), writes=["GC"], semkey="lGC")
        P.add("vector", lambda e: e.memset(Sf, 0.0), writes=["Sf"])
        P.add("vector", lambda e: e.memset(Sb, 0.0), writes=["Sb"])
        def front(P, c):
            st = c % 2
            par = c % 3
            b0, b1, b2 = 3 * st, 3 * st + 1, 3 * st + 2
            NTb, Nb, Pm, PTm, XT = NTbs[st], Nbs[st], Pms[st], PTms[st], XTs[st]
            kNT, kN = ("NTb", st), ("Nb", st)
            ts_ = slice(c * 64, (c + 1) * 64)
            for g in range(G):
                for h in range(2):
                    hp = slice(h * 64, (h + 1) * 64)
                    st_ = (g == 0)
                    P.mm(ps[hp, b0, g * 128:(g + 1) * 128], BT[hp, g, ts_], AR[hp, g, c, :], start=st_, stop=True,
                         reads=["BT", "AR"], writes=pk(b0), sig=False)
                    P.mm(ps[hp, b1, g * 128:(g + 1) * 128], KT[hp, g, ts_], AR[hp, g, c, :], start=st_, stop=True,
                         reads=["KT", "AR"], writes=pk(b1), sig=False)
                    P.mm(ps[hp, b2, g * 64:(g + 1) * 64], AR[hp, g, c, 0:64], BT[hp, g, ts_], start=st_, stop=True,
                         reads=["BT", "AR"], writes=pk(b2), sig=(g == G - 1 and h == 1))
            psAB = ps[:, b0, :].rearrange("p (g n) -> p g n", g=G)
            psAK = ps[:, b1, :].rearrange("p (g n) -> p g n", g=G)
            psN = ps[:, b2, 0:256].rearrange("p (g n) -> p g n", g=G)
            for h in range(2):
                hp = slice(h * 64, (h + 1) * 64)
                cs_ = slice(h * 64, (h + 1) * 64)
                P.tt("vector", NTb[hp, :, cs_], psAB[hp, :, 0:64], bc(m_uts[hp].unsqueeze(1), [64, G, 64]), ALU.mult,
                     reads=pk(b0) + ["m64"], writes=[kNT])
                P.tt("vector", AKb2[par][hp, :, cs_], psAK[hp, :, 0:64], bc(m_uts[hp].unsqueeze(1), [64, G, 64]), ALU.mult,
                     reads=pk(b1) + ["m64"], writes=[("AKb", par)])
                P.tt("vector", Nb[hp, :, cs_], psN[hp, :, :], bc(m_lts[hp].unsqueeze(1), [64, G, 64]), ALU.mult,
                     reads=pk(b2) + ["m64"], writes=[kN])
            P.tt("vector", ARB2[par], psAB[:, :, 64:128], bc(m_uti.unsqueeze(1), [128, G, 64]), ALU.mult,
                 reads=pk(b0) + ["m64"], writes=[("ARB", par)])
            P.tt("vector", ARK2[par], psAK[:, :, 64:128], bc(m_uti.unsqueeze(1), [128, G, 64]), ALU.mult,
                 reads=pk(b1) + ["m64"], writes=[("ARK", par)])
            xi = 0
            P.tt("gpsimd", XT[0], NTb, identG, ALU.add, reads=[kNT, "identb"], writes=[("XT", st, 0)])
            Pc, PTc, Pk, PTk = Nb, NTb, kN, kNT
            for j in range(1, 6):
                pi = j % 2
                for g in range(G):
                    P.mm(ps[:, b0, g * 128:(g + 1) * 128], PTc[:, g, :], Pc[:, g, :], start=(g == 0), stop=True,
                         reads=[Pk, PTk], writes=pk(b0), sig=(g == G - 1))
                P.cp("scalar", Pm[pi].rearrange("p g n -> p (g n)"), ps[:, b0, :], reads=pk(b0), writes=[("Pm", st, pi)])
                if j < 5:
                    for g in range(G):
                        P.mm(ps[:, b1, g * 128:(g + 1) * 128], Pc[:, g, :], PTc[:, g, :], start=(g == 0), stop=True,
                             reads=[Pk, PTk], writes=pk(b1), sig=(g == G - 1))
                    P.cp("scalar", PTm[pi].rearrange("p g n -> p (g n)"), ps[:, b1, :], reads=pk(b1), writes=[("PTm", st, pi)])
                for g in range(G):
                    P.mm(ps[:, b2, g * 128:(g + 1) * 128], Pm[pi][:, g, :], XT[xi][:, g, :], start=(g == 0), stop=True,
                         reads=[("Pm", st, pi), ("XT", st, xi)], writes=pk(b2), sig=(g == G - 1))
                xdst, xdk = (XTF[par], ("XTF", par)) if j == 5 else (XT[1 - xi], ("XT", st, 1 - xi))
                P.tt("vector", xdst.rearrange("p g n -> p (g n)"), XT[xi].rearrange("p g n -> p (g n)"), ps[:, b2, :],
                     ALU.add, reads=pk(b2) + [("XT", st, xi)], writes=[xdk])
                xi = 1 - xi
                Pc, PTc, Pk, PTk = Pm[pi], PTm[pi], ("Pm", st, pi), ("PTm", st, pi)

        def back(P, c):
            par = c % 3
            ts_ = slice(c * 64, (c + 1) * 64)
            XTf, XTk = XTF[par], ("XTF", par)
            psW = ps[:, 6, 0:256].rearrange("p (g n) -> p g n", g=G)
            psU = ps[:, 6, 256:512].rearrange("p (g n) -> p g n", g=G)
            psY = ps[:, 7, 0:256].rearrange("p (g n) -> p g n", g=G)
            psS = ps[:, 7, 256:512].rearrange("p (g n) -> p g n", g=G)
            kW, kU, kY, kS = pk(6, 0, 256), pk(6, 256, 512), pk(7, 0, 256), pk(7, 256, 512)
            fl = [True, True]
            for g in range(G):
                for h in range(2):
                    hp = slice(h * 64, (h + 1) * 64)
                    P.mm(psW[hp, g, :], AR[hp, g, c, 0:64], Sb[hp, g, :], start=fl[h], stop=True,
                         reads=["AR", "Sb"], writes=kW, sig=False)
                    fl[h] = False
            for g in range(G):
                P.mm(psW[:, g, :], AKb2[par][:, g, :], Vs[:, g, c, :], start=False, stop=True,
                     reads=[("AKb", par), "Vs"], writes=kW, sig=(g == G - 1))
            P.cp("scalar", W1s, psW, reads=kW, writes=["W1s"])
            for g in range(G):
                P.mm(psU[:, g, :], XTf[:, g, :], W1s[:, g, :], start=False, stop=True,
                     reads=[XTk, "W1s"], writes=kU, sig=(g == G - 1))
            P.cp("vector", Us, psU, reads=kU, writes=["Us"])
            fl = [True, True]
            for g in range(G):
                for h in range(2):
                    hp = slice(h * 64, (h + 1) * 64)
                    P.mm(psY[hp, g, :], Sb[hp, g, :], AR[hp, g, c, 64:128], start=fl[h], stop=False,
                         reads=["AR", "Sb"], writes=kY, sig=False)
                    fl[h] = False
                    P.mm(psY[hp, g, :], Us[hp, g, :], ARB2[par][hp, g, :], start=False, stop=False,
                         reads=["Us", ("ARB", par)], writes=kY, sig=False)
                    P.mm(psY[hp, g, :], Vs[hp, g, c, :], ARK2[par][hp, g, :], start=False, stop=True,
                         reads=["Vs", ("ARK", par)], writes=kY, sig=(g == G - 1 and h == 1))
            ybi = (c // 4) % 2
            q = c % 4
            P.cp("scalar", yb[ybi][:, :, q * 64:(q + 1) * 64], psY, reads=kY, writes=[("yb", 0, q)])
            for g in range(G):
                for h in range(2):
                    hp = slice(h * 64, (h + 1) * 64)
                    P.mm(psS[hp, g, :], Bs[hp, g, c, :], Us[hp, g, :], start=False, stop=False,
                         reads=["Bs", "Us"], writes=kS, sig=False)
                    P.mm(psS[hp, g, :], Ks[hp, g, c, :], Vs[hp, g, c, :], start=False, stop=True,
                         reads=["Ks", "Vs"], writes=kS, sig=(g == G - 1 and h == 1))
            P.tt("gpsimd", Stmp, Sf, bc(GC[:, :, c:c + 1], [128, G, 64]), ALU.mult, reads=["Sf", "GC"], writes=["Stmp"])
            P.tt("vector", Sf, Stmp, psS, ALU.add, reads=["Stmp"] + kS, writes=["Sf"])
            P.cp("scalar", Sb, Sf, reads=["Sf"], writes=["Sb"])
            if q == 3:
                blk = c // 4
                tk0 = blk * 256
                ybk = [("yb", 0, qq) for qq in range(4)]
                y = yb[ybi]
                P.dma("sync", bonb, dr["BON"][fs, tk0:tk0 + 256].rearrange("(g p) t -> p g t", p=128), writes=["bonb"], semkey="lbon")
                P.dma("sync", gb, dr["Gt"][fs, tk0:tk0 + 256].rearrange("(g p) t -> p g t", p=128), writes=["gb"], semkey="lgb")
                P.act(ysq, y, AF.Square, reads=ybk, writes=["ysq", ("var", 0), ("var", 1)])
                for hh in range(2):
                    gs = slice(hh * 2, hh * 2 + 2)
                    for g in (2 * hh, 2 * hh + 1):
                        lo = (g % 2) * 256
                        P.mm(ps[:, 6, lo:lo + 256], C.bones, y[:, g, :], reads=ybk + ["bones"], writes=pk(6))
                        P.mm(ps[:, 7, lo:lo + 256], C.bones, ysq[:, g, :], reads=["ysq", "bones"], writes=pk(7))
                    pm_ = ps[:, 6, :].rearrange("p (g n) -> p g n", g=2)
                    pq_ = ps[:, 7, :].rearrange("p (g n) -> p g n", g=2)
                    P.act(mean[:, gs, :], pm_, AF.Copy, reads=pk(6), writes=[("mean", hh)], scale=1.0 / 64)
                    P.tt("gpsimd", m2[:, gs, :], mean[:, gs, :], mean[:, gs, :], ALU.mult, reads=[("mean", hh)], writes=[("m2", hh)])
                    P.stt("vector", var[:, gs, :], pq_, 1.0 / 64, m2[:, gs, :], ALU.mult, ALU.subtract,
                          reads=pk(7) + [("m2", hh)], writes=[("var", hh), "ysq"])
                    rsqrt(P, var[:, gs, :], var[:, gs, :], 1.0, 64e-5, [("var", hh)], [("var", hh)])
                    P.tt("gpsimd", mean[:, gs, :], y[:, gs, :], mean[:, gs, :], ALU.subtract, reads=ybk + [("mean", hh)], writes=[("mean", hh)])
                    P.tt("vector", mean[:, gs, :], mean[:, gs, :], var[:, gs, :], ALU.mult, reads=[("mean", hh), ("var", hh)], writes=[("mean", hh)])
                for g in range(G):
                    pc = p0 + g
                    P.act(m2[:, g, :], mean[:, g, :], AF.Identity, reads=[("mean", g // 2), "pv"], writes=[("m2", g // 2)],
                          scale=pv[:, PV_LNW, pc:pc + 1], bias=pv[:, PV_LNB, pc:pc + 1])
                P.tt("vector", m2, m2, bonb, ALU.add, reads=[("m2", 0), ("m2", 1), "bonb"], writes=[("m2", 0), ("m2", 1)])
                yi = blk % 2
                P.tt("gpsimd", ygb[yi], m2, gb, ALU.mult, reads=[("m2", 0), ("m2", 1), "gb"], writes=[("ygb", 0)])
                P.dma("sync", dr["ygT"][fs, tk0:tk0 + 256].rearrange("(g p) t -> p g t", p=128), ygb[yi], reads=[("ygb", 0)],
                      writes=[("dram", "ygT", grp, blk)], semkey="oyg")

        def halves(c):
            d = Deferred()
            front(d, c)
            L = d.calls
            m = len(L) // 2
            while m < len(L) and not L[m - 1][2].get("sig", True):
                m += 1
            return L[:m], L[m:]

        fr = {}
        fr[0] = halves(0)
        fr[1] = halves(1)
        replay_interleaved(P, [fr[0][0]])
        replay_interleaved(P, [fr[0][1], fr[1][0]])
        for c in range(32):
            lists = []
            db = Deferred()
            back(db, c)
            lists.append(db.calls)
            if c + 1 < 32:
                lists.append(fr[c + 1][1])
            if c + 2 < 32:
                fr[c + 2] = halves(c + 2)
                lists.append(fr[c + 2][0])
            replay_interleaved(P, lists)


def phase_A3(C):
    P, A, dr = C.P, C.A, C.dr
    TT = 512
    lhs = [A.tile([16, TT], BF16) for _ in range(2)]
    wps = [A.tile([16, 512], BF16) for _ in range(3)]
    wpks = ["wp0", "wp1", "wp2"]
    rt = [(A.tile([512], F32), "rt%d" % i) for i in range(3)]
    cnt = [0, 0]
    plan = [(dr["w_out"], 0, n * 512) for tt in range(T // TT) for n in range(4)]
    WS = WStream(C, plan, wps, wpks)
    for tt in range(T // TT):
        li = tt % 2
        P.dma("sync", lhs[li], dr["ygT"][:, tt * TT:(tt + 1) * TT].rearrange("(c p) t -> p c t", p=128),
              writes=[("lhsA3%d" % li, k) for k in range(16)], semkey="lyg%d" % li)
        proj_tm(C, lhs[li], "lhsA3%d" % li, 16, dr["w_out"], 4, WS, dr["x"], dr["h"], tt * TT, rt, "h", cnt)


def make_mlp(layer, dst_name):
    def phase(C):
        P, A, ps, dr = C.P, C.A, C.ps, C.dr
        TT = 512
        xt = A.tile([2048], F32)
        junk = A.tile([2048], BF16)
        ss = A.tile([4], F32)
        xnTs = [A.tile([16, TT], BF16) for _ in range(2)]
        h1T = A.tile([64, TT], BF16)
        wps = [A.tile([16, 512], BF16) for _ in range(3)]
        wpks = ["wp0", "wp1", "wp2"]
        rl = [A.tile([512], F32) for _ in range(2)]
        rt = [(A.tile([512], F32), "rt%d" % i) for i in range(3)]
        print("MLP arena words", A.off)
        cnt = [0, 0]
        wu = dr["up%d" % layer]
        wd = dr["dn%d" % layer]
        gi = PV_MLP0 + layer
        ec = 0
        plan = []
        for tt in range(T // TT):
            plan += [(wu, 0, fp * 512) for fp in range(16)]
            plan += [(wd, kp * 2048, n * 512) for n in range(4) for kp in range(4)]
        WS = WStream(C, plan, wps, wpks)

        def norm_tile(tt):
            for s in range(4):
                norm_T(C, dr["h"], tt * TT + s * 128, gi, xnTs[tt % 2], ("xnT", tt % 2), s * 128, xt, "xtM", junk, ss,
                       psbanks=(2, 3))

        norm_tile(0)
        for tt in range(T // TT):
            tok0 = tt * TT
            xnT = xnTs[tt % 2]
            xk = ("xnT", tt % 2)
            for fp in range(16):
                wp_, wpk_ = WS.get()
                for j in range(4):
                    fc = fp * 4 + j
                    bank = fc % 2
                    for k in range(16):
                        P.mm(ps[:, bank, :], wp_[:, k, j * 128:(j + 1) * 128], xnT[:, k, :], start=(k == 0), stop=(k == 15),
                             reads=[wpk_, (xk, k)], writes=pk(bank), sig=(k == 15))
                    ri = ec % 2
                    ec += 1
                    P.act(rl[ri], ps[:, bank, :], AF.Relu, reads=pk(bank), writes=[("rl", ri)])
                    P.tt("vector", h1T[:, fc, :], rl[ri], rl[ri], ALU.mult, reads=[("rl", ri)], writes=[("h1T", fc)])
                WS.done()
            streams = []
            wts = []
            C.P = dn = Deferred()
            proj_tm(C, h1T, "h1T", 64, wd, 4, WS, dr["h"], dr[dst_name], tok0, rt, dst_name, cnt)
            C.P = P
            streams.append(dn.calls)
            wts.append(1)
            if tt + 1 < T // TT:
                C.P = dq = Deferred()
                norm_tile(tt + 1)
                C.P = P
                streams.append(dq.calls)
                wts.append(8)
            replay_interleaved(P, streams, wts)
    return phase


def phase_K(C):
    P, A, ps, dr = C.P, C.A, C.ps, C.dr
    TT = 512
    xt = A.tile([2048], F32)
    junk = A.tile([2048], BF16)
    ss = A.tile([4], F32)
    xnT = A.tile([16, TT], BF16)
    wkv = A.tile([16, 512], BF16)
    kv = [A.tile([512], F32) for _ in range(2)]
    sqk = A.tile([4, 64], F32)
    ssum = A.tile([8], F32)
    kn = A.tile([4, 64], F32)
    ta = A.tile([4, 32], F32)
    tb = A.tile([4, 32], F32)
    kr = [A.tile([4, 64], BF16) for _ in range(2)]
    kTs = [A.tile([2, 128], BF16) for _ in range(2)]
    vsb = [A.tile([256], BF16) for _ in range(2)]
    load_panel(C, wkv, "wkv", dr["w_kv"], 0, 0)
    knb = bc(C.pvb[:, 64:128].unsqueeze(1), [128, 4, 64])
    for tt in range(T // TT):
        tok0 = tt * TT
        for s in range(4):
            norm_T(C, dr["h"], tok0 + s * 128, PV_KVN, xnT, "xnT", s * 128, xt, "xtK", junk, ss, psbanks=(6, 7))
        for s in range(4):
            gt = tt * 4 + s
            i2 = s % 2
            bank = s % 2
            for k in range(16):
                P.mm(ps[:, bank, :], xnT[:, k, s * 128:(s + 1) * 128], wkv[:, k, :], start=(k == 0), stop=(k == 15),
                     reads=["wkv", ("xnT", k)], writes=pk(bank), sig=(k == 15))
            P.cp("scalar", kv[i2], ps[:, bank, :], reads=pk(bank), writes=[("kv", i2)])
            k3 = kv[i2][:, 0:256].rearrange("p (h d) -> p h d", h=4)
            P.act(sqk, k3, AF.Square, reads=[("kv", i2)], writes=["sqk"])
            P.add("vector", lambda e: e.tensor_reduce(ssum[:, 0:4], sqk, AX.X, ALU.add), reads=["sqk"], writes=["ssum0"])
            rsqrt(P, ssum[:, 4:8], ssum[:, 0:4], 1.0 / 64, 1e-6, ["ssum0"], ["ssum1"])
            P.tt("vector", kn, k3, bc(ssum[:, 4:8].unsqueeze(2), [128, 4, 64]), ALU.mult, reads=[("kv", i2), "ssum1"], writes=["kn"])
            P.tt("gpsimd", kn, kn, knb, ALU.mult, reads=["kn", "pvb"], writes=["kn"])
            cosb = bc(C.cs[:, gt, 0:32].unsqueeze(1), [128, 4, 32])
            sinb = bc(C.cs[:, gt, 32:64].unsqueeze(1), [128, 4, 32])
            x1 = kn[:, :, 0:32]
            x2 = kn[:, :, 32:64]
            P.tt("vector", ta, x1, cosb, ALU.mult, reads=["kn", "cs"], writes=["ta"])
            P.tt("gpsimd", tb, x2, sinb, ALU.mult, reads=["kn", "cs"], writes=["tb"])
            P.tt("vector", kr[i2][:, :, 0:32], ta, tb, ALU.subtract, reads=["ta", "tb"], writes=[("kr", i2, 0)])
            P.tt("vector", ta, x2, cosb, ALU.mult, reads=["kn", "cs", ("kr", i2, 0)], writes=["ta"])
            P.tt("gpsimd", tb, x1, sinb, ALU.mult, reads=["kn", "cs", ("kr", i2, 0)], writes=["tb"])
            P.tt("vector", kr[i2][:, :, 32:64], ta, tb, ALU.add, reads=["ta", "tb"], writes=[("kr", i2, 1)])
            pT = ps[:, 2 + i2, 0:128].bitcast(BF16).rearrange("p (j t) -> p j t", j=2)
            for j in range(2):
                P.tr(pT[:, j, :], kr[i2][:, 2 * j:2 * j + 2, :].rearrange("p a b -> p (a b)"), C.identb,
                     reads=[("kr", i2, 0), ("kr", i2, 1), "identb"], writes=pk(2 + i2, 0, 128), sig=(j == 1))
            P.cp("vector", kTs[i2], pT, reads=pk(2 + i2, 0, 128), writes=[("kTs", i2)])
            P.dma("sync", dr["KTs"].rearrange("(j a) d t -> (a d) j t", a=2)[:, :, gt * 128:(gt + 1) * 128], kTs[i2],
                  reads=[("kTs", i2)], writes=[("dram", "KTs", gt)], semkey="oKTs%d" % i2)
            P.cp("scalar", vsb[i2], kv[i2][:, 256:512], reads=[("kv", i2)], writes=[("vsb", i2)])
            P.dma("sync", dr["Vt"][gt * 128:(gt + 1) * 128, :], vsb[i2], reads=[("vsb", i2)],
                  writes=[("dram", "Vt", gt)], semkey="oVt%d" % i2)


def phase_B(C):
    P, A, ps, dr = C.P, C.A, C.ps, C.dr
    TT = 256
    NS = TT // 128
    KTsb = A.tile([4, T], BF16)
    Vsb = A.tile([16, 256], BF16)
    onesb = A.tile([64], BF16)
    xt = A.tile([2048], F32)
    junk = A.tile([2048], BF16)
    ss = A.tile([4], F32)
    xnTs = [A.tile([16, TT], BF16) for _ in range(2)]
    q_tm = [A.tile([32, 64], F32) for _ in range(NS)]
    tmpq = A.tile([32, 64], F32)
    ssum = A.tile([64], F32)
    ta = A.tile([32, 32], F32)
    tb = A.tile([32, 32], F32)
    qr = A.tile([32, 64], BF16)
    QT = A.tile([16, 128], BF16)
    pf = [A.tile([512], F32) for _ in range(2)]
    PT = [A.tile([4, 128], BF16) for _ in range(4)]
    den = A.tile([4, 128], F32)
    oT = A.tile([16, TT], BF16)
    wps = [A.tile([16, 512], BF16) for _ in range(3)]
    wpks = ["wp0", "wp1", "wp2"]
    rt = [(A.tile([512], F32), "rt%d" % i) for i in range(2)]
    print("B arena words", A.off)
    plan = []
    for tt in range(T // TT):
        plan += [(dr["w_q"], 0, n * 512) for n in range(4)]
        plan += [(dr["w_o"], 0, n * 512) for n in range(4)]
    WS = WStream(C, plan, wps, wpks)
    for dup in range(2):
        P.dma("sync", KTsb[dup * 64:(dup + 1) * 64], dr["KTs"].rearrange("g d t -> d g t"), writes=["KTsb%d" % dup], semkey="lKTs%d" % dup)
    P.dma("sync", Vsb, dr["Vt"].rearrange("(b p) f -> p b f", p=128), writes=["Vsb"], semkey="lVsb")
    P.add("vector", lambda e: e.memset(onesb, 1.0), writes=["onesb"])
    qnb = bc(C.pvb[:, 0:64].unsqueeze(1), [128, 32, 64])
    cnt = [0, 0]
    pcnt = 0
    def norm_tile(tt):
        for s in range(NS):
            norm_T(C, dr["h"], tt * TT + s * 128, PV_BN, xnTs[tt % 2], ("xnT", tt % 2), s * 128, xt, "xtB", junk, ss,
                   psbanks=(6, 7))

    norm_tile(0)
    for tt in range(T // TT):
        tok0 = tt * TT
        xnT = xnTs[tt % 2]
        xk = ("xnT", tt % 2)
        for n in range(4):
            wp_, wpk_ = WS.get()
            for s in range(NS):
                bank = s
                for k in range(16):
                    P.mm(ps[:, bank, :], xnT[:, k, s * 128:(s + 1) * 128], wp_[:, k, :], start=(k == 0), stop=(k == 15),
                         reads=[wpk_, (xk, k)], writes=pk(bank), sig=(k == 15))
                dstq = q_tm[s].rearrange("p h d -> p (h d)")[:, n * 512:(n + 1) * 512]
                if (n + s) % 2 == 0:
                    P.cp("scalar", dstq, ps[:, bank, :], reads=pk(bank), writes=[("q_tm", s, n)])
                else:
                    P.cp("vector", dstq, ps[:, bank, :], reads=pk(bank), writes=[("q_tm", s, n)])
            WS.done()
        for s in range(NS):
            nb = tt * NS + s
            qk = [("q_tm", s, n) for n in range(4)]
            q3 = q_tm[s]
            P.act(tmpq, q3, AF.Square, reads=qk, writes=["tmpq"])
            P.add("vector", lambda e: e.tensor_reduce(ssum[:, 0:32], tmpq, AX.X, ALU.add), reads=["tmpq"], writes=["ssum0"])
            rsqrt(P, ssum[:, 32:64], ssum[:, 0:32], 1.0 / 64, 1e-6, ["ssum0"], ["ssum1"])
            P.tt("vector", tmpq, q3, bc(ssum[:, 32:64].unsqueeze(2), [128, 32, 64]), ALU.mult, reads=qk + ["ssum1", "tmpq"], writes=["tmpq"])
            P.tt("gpsimd", tmpq, tmpq, qnb, ALU.mult, reads=["tmpq", "pvb"], writes=["tmpq"])
            cosb = bc(C.cs[:, nb, 0:32].unsqueeze(1), [128, 32, 32])
            sinb = bc(C.cs[:, nb, 32:64].unsqueeze(1), [128, 32, 32])
            x1 = tmpq[:, :, 0:32]
            x2 = tmpq[:, :, 32:64]
            P.tt("vector", ta, x1, cosb, ALU.mult, reads=["tmpq", "cs"], writes=["ta"])
            P.tt("gpsimd", tb, x2, sinb, ALU.mult, reads=["tmpq", "cs"], writes=["tb"])
            P.tt("vector", qr[:, :, 0:32], ta, tb, ALU.subtract, reads=["ta", "tb"], writes=["qr0"])
            P.tt("vector", ta, x2, cosb, ALU.mult, reads=["tmpq", "cs", "qr0"], writes=["ta"])
            P.tt("gpsimd", tb, x1, sinb, ALU.mult, reads=["tmpq", "cs", "qr0"], writes=["tb"])
            P.tt("vector", qr[:, :, 32:64], ta, tb, ALU.add, reads=["ta", "tb"], writes=["qr1"])
            for hb_ in range(2):
                pT = ps[:, 6 + hb_, :].bitcast(BF16).rearrange("p (c t) -> p c t", c=8)
                for cc in range(8):
                    c = hb_ * 8 + cc
                    P.tr(pT[:, cc, :], qr[:, 2 * c:2 * c + 2, :].rearrange("p a b -> p (a b)"), C.identb,
                         reads=["qr0", "qr1", "identb"], writes=pk(6 + hb_), sig=(cc == 7))
                if hb_ == 0:
                    P.cp("scalar", QT[:, 0:8, :], pT, reads=pk(6), writes=[("QT", 0)])
                else:
                    P.cp("vector", QT[:, 8:16, :], pT, reads=pk(7), writes=[("QT", 1)])
            for g in range(4):
                js = [1] if nb == 0 else [0, 1]
                pts = {}
                for hp_ in range(2):
                    hp = slice(hp_ * 64, (hp_ + 1) * 64)
                    for j in js:
                        kb = nb - 1 + j
                        bank = pcnt % 4
                        pcnt += 1
                        P.mm(ps[:, bank, :], KTsb[hp, g, kb * 128:(kb + 1) * 128],
                             QT[hp, 4 * g:4 * g + 4, :].rearrange("p c t -> p (c t)"),
                             reads=["KTsb%d" % hp_, ("QT", g // 2)], writes=pk(bank))
                        pfi = pcnt % 2
                        P.act(pf[pfi], ps[:, bank, :], AF.Exp, reads=pk(bank), writes=[("pf", pfi)], scale=0.125)
                        pti = hp_ * 2 + j
                        msk = C.am[:, 0, :] if j == 1 else C.am[:, 1, :]
                        eng = "vector" if pcnt % 2 == 0 else "gpsimd"
                        P.tt(eng, PT[pti], pf[pfi].rearrange("p (c t) -> p c t", c=4), bc(msk.unsqueeze(1), [128, 4, 128]),
                             ALU.mult, reads=[("pf", pfi), "am"], writes=[("PT", pti)])
                        pts[(hp_, j)] = pti
                psO = ps[:, 4, :].rearrange("p (c t) -> p c t", c=4)
                psD = ps[:, 5, :].rearrange("p (c t) -> p c t", c=4)
                for hp_ in range(2):
                    hp = slice(hp_ * 64, (hp_ + 1) * 64)
                    fo = True
                    for j in js:
                        kb = nb - 1 + j
                        pti = pts[(hp_, j)]
                        for ci in range(4):
                            lastmm = (hp_ == 1 and j == js[-1] and ci == 3)
                            P.mm(psO[hp, ci, :], Vsb[:, kb, g * 64:(g + 1) * 64], PT[pti][:, ci, :], start=fo, stop=True,
                                 reads=["Vsb", ("PT", pti)], writes=pk(4), sig=False)
                            P.mm(psD[hp, ci, :], onesb, PT[pti][:, ci, :], start=fo, stop=True,
                                 reads=["onesb", ("PT", pti)], writes=pk(5), sig=lastmm)
                            fo = False
                P.tt("vector", den, psD, bc(C.esk[:, 4 * g:4 * g + 4].unsqueeze(2), [128, 4, 128]), ALU.add,
                     reads=pk(5) + ["esk"], writes=["den"])
                P.add("vector", lambda e: e.reciprocal(den, den), reads=["den"], writes=["den"])
                P.tt("vector", oT[:, 4 * g:4 * g + 4, s * 128:(s + 1) * 128], psO, den, ALU.mult,
                     reads=pk(4) + ["den"], writes=[("oT", 4 * g + i) for i in range(4)])
        streams = []
        wts = []
        C.P = dn = Deferred()
        proj_tm(C, oT, "oT", 16, dr["w_o"], NS, WS, dr["h"], dr["out"], tok0, rt, "out", cnt)
        C.P = P
        streams.append(dn.calls)
        wts.append(1)
        if tt + 1 < T // TT:
            C.P = dq = Deferred()
            norm_tile(tt + 1)
            C.P = P
            streams.append(dq.calls)
            wts.append(10)
        replay_interleaved(P, streams, wts)


ALL_PHASES = None


def get_phases():
    return [phase_A1, phase_A2, phase_A3, make_mlp(0, "h"), phase_K, phase_B_pre, make_mlp(1, "out")]


def phase_B_pre(C):
    dr = C.dr
    o = dr["out"]
    dr["out"] = dr["h"]
    phase_B(C)
    dr["out"] = o


def host_consts():
    f = np.float32
    ident = np.eye(128, dtype=f)
    bones = np.zeros((128, 128), f)
    bones[:64, :64] = 1
    bones[64:, 64:] = 1
    p = np.arange(128)[:, None] % 64
    t = np.arange(64)[None, :]
    m64 = np.concatenate([(t > p), (t >= p), (t < p)], axis=1).astype(f)
    pos = np.arange(T, dtype=f)
    half = 32
    inv = np.power(f(10000.0), -np.arange(half, dtype=f) / f(half)).astype(f)
    ang = (pos[:, None] * inv[None, :]).astype(f)
    cs = np.concatenate([np.cos(ang), np.sin(ang)], axis=1).astype(f)
    cs = cs.reshape(16, 128, 64).transpose(1, 0, 2).reshape(128, 16 * 64)
    s = np.arange(128)[:, None]
    q = np.arange(128)[None, :]
    am = np.concatenate([(s <= q), (s > q)], axis=1).astype(f)
    return dict(c_ident=ident, c_bones=bones, c_m64=np.ascontiguousarray(m64), c_cs=np.ascontiguousarray(cs), c_am=np.ascontiguousarray(am))


def host_inputs(inp):
    f = np.float32
    g = lambda k: np.asarray(inp[k], dtype=f)
    vecs = [g("a_norm")[0]] + [g("a_mix")[0, i] for i in range(6)] + [g("a_w0")[0], g("a_a0")[0], g("a_k_k")[0], g("a_k_a")[0],
            g("a_r_k")[0].reshape(-1), g("a_ln_x_w")[0], g("a_ln_x_b")[0], g("mlp_norm")[0], g("mlp_norm")[1], g("kv_norm"),
            g("b_norm")[0]]
    pv = np.zeros((128, NV, 16), f)
    for i, v in enumerate(vecs):
        pv[:, i, :] = v.reshape(16, 128).T
    sk = g("b_sinks")[0]
    for c in range(16):
        pv[:64, PV_SK, c] = sk[2 * c]
        pv[64:, PV_SK, c] = sk[2 * c + 1]
    pvb = np.zeros((128, 128), f)
    pvb[:, 0:64] = g("b_q_norm")[0][None, :]
    pvb[:, 64:128] = g("k_norm")[None, :]
    wr = g("a_w_rkv")[0]
    shared = dict(
        w_r=wr[0], w_k=wr[1], w_v=wr[2], w_out=g("a_w_out")[0], w_q=g("b_w_q")[0], w_o=g("b_w_o")[0],
        w1=g("a_w1")[0], w2=g("a_w2")[0], a1=g("a_a1")[0], a2=g("a_a2")[0], g1=g("a_g1")[0], g2=g("a_g2")[0],
        up0=g("mlp_w_up")[0], up1=g("mlp_w_up")[1], dn0=g("mlp_w_down")[0], dn1=g("mlp_w_down")[1],
        w_kv=g("w_kv"), pv=pv.reshape(128, NV * 16), pvb=pvb)
    shared.update(host_consts())
    shared = {k: np.ascontiguousarray(v) for k, v in shared.items()}
    return shared


_CACHE = {}


def kernel(**inputs):
    shared = host_inputs(inputs)
    x = np.asarray(inputs["x"], dtype=np.float32)
    if "nc" not in _CACHE:
        _CACHE["nc"] = build(get_phases())
    nc, C = _CACHE["nc"]
    in_maps = []
    for b in range(NCORES):
        m = dict(shared)
        m["x"] = np.ascontiguousarray(x[b])
        in_maps.append(m)
    res = run_bass_kernel_spmd(nc, in_maps, core_ids=list(range(NCORES)))
    return np.stack([res.results[b]["out"] for b in range(NCORES)], axis=0).astype(np.float32)
```

```python
import math
import os
from contextlib import ExitStack
import numpy as np
import ml_dtypes
import concourse.bass as bass
import concourse.mybir as mybir
from concourse.bass_utils import run_bass_kernel_spmd

F32 = mybir.dt.float32
BF16 = mybir.dt.bfloat16
AF = mybir.ActivationFunctionType
ALU = mybir.AluOpType
AX = mybir.AxisListType

D = 2048
T = 2048
NCH = 16
FF = 8192
NCORES = 8
ENGS = ["tensor", "vector", "scalar", "gpsimd", "sync"]
ARENA_WORDS = 45056


class Op:
    __slots__ = ("eng", "fn", "reads", "writes", "sig", "dma", "semkey", "csem", "cval", "waits", "preads")

    def __init__(self, eng, fn, reads, writes, sig, dma, semkey):
        self.eng = eng
        self.fn = fn
        self.reads = reads
        self.writes = writes
        self.sig = sig
        self.dma = dma
        self.semkey = semkey
        self.csem = None
        self.cval = None
        self.waits = []


class Prog:
    def __init__(self, nc):
        self.nc = nc
        self.ops = []
        self.psq = {}

    def add(self, eng, fn, reads=(), writes=(), sig=True, dma=False, semkey=None):
        ispk = lambda b: isinstance(b, tuple) and len(b) == 2 and b[0] == "ps"
        pr = frozenset(b for b in reads if ispk(b) and b not in writes)
        rd = tuple(b for b in reads if not ispk(b))
        wr = tuple(writes) + tuple(pr)
        op = Op(eng, fn, rd, wr, sig, dma, semkey)
        op.preads = pr
        self.ops.append(op)
        return op

    def barrier(self):
        self.ops.append(None)

    def mm(self, out, lhsT, rhs, start=True, stop=True, reads=(), writes=(), sig=True):
        return self.add("tensor", lambda e: e.matmul(out, lhsT, rhs, start=start, stop=stop,
                                                     skip_group_check=True),
                        reads, writes, sig=sig)

    def tr(self, out, in_, ident, reads=(), writes=(), sig=True):
        return self.add("tensor", lambda e: e.transpose(out, in_, ident), reads, writes, sig=sig)

    def act(self, out, in_, func, reads=(), writes=(), **kw):
        return self.add("scalar", lambda e: e.activation(out, in_, func, **kw), reads, writes)

    def dma(self, eng, out, in_, reads=(), writes=(), semkey=None):
        assert semkey is not None
        return self.add(eng, lambda e: e.dma_start(out, in_), reads, writes, dma=True, semkey=semkey)

    def tt(self, eng, out, a, b, op, reads=(), writes=()):
        return self.add(eng, lambda e: e.tensor_tensor(out, a, b, op), reads, writes)

    def ts(self, eng, out, a, s1, s2, op0, op1=None, reads=(), writes=()):
        if op1 is None:
            return self.add(eng, lambda e: e.tensor_scalar(out, a, s1, None, op0), reads, writes)
        return self.add(eng, lambda e: e.tensor_scalar(out, a, s1, s2, op0, op1), reads, writes)

    def stt(self, eng, out, a, s, b, op0, op1, reads=(), writes=()):
        eng = "vector"
        return self.add(eng, lambda e: e.scalar_tensor_tensor(out, a, s, b, op0, op1), reads, writes)

    def cp(self, eng, out, a, reads=(), writes=()):
        if eng == "scalar":
            return self.act(out, a, AF.Copy, reads, writes)
        return self.add(eng, lambda e: e.tensor_copy(out, a), reads, writes)

    def emit(self, stack):
        nc = self.nc
        ops = self.ops
        eng_cnt = {e: 0 for e in ENGS}
        dma_cnt = {}
        pending = {e: [] for e in ENGS}
        lastop = {}
        for op in ops:
            if op is not None:
                lastop[op.eng] = op
        for op in lastop.values():
            op.sig = True
        for op in ops:
            if op is None:
                continue
            if op.dma:
                n = dma_cnt.get(op.semkey, 0) + 1
                dma_cnt[op.semkey] = n
                op.csem = ("d", op.semkey)
                op.cval = 16 * n
            elif op.sig:
                eng_cnt[op.eng] += 1
                op.csem = ("e", op.eng)
                op.cval = eng_cnt[op.eng]
                for q in pending[op.eng]:
                    q.csem = op.csem
                    q.cval = op.cval
                pending[op.eng] = []
            else:
                pending[op.eng].append(op)
        for e in ENGS:
            assert not pending[e], e
        last_w = {}
        readers = {}
        waited = {e: {} for e in ENGS}
        all_prev = {}
        barrier_req = {e: None for e in ENGS}
        for op in ops:
            if op is None:
                snap = dict(all_prev)
                for e in ENGS:
                    barrier_req[e] = snap
                continue
            need = {}
            if barrier_req[op.eng] is not None:
                for s, v in barrier_req[op.eng].items():
                    need[s] = max(need.get(s, 0), v)
                barrier_req[op.eng] = None
            for b in op.reads:
                o = last_w.get(b)
                if o is not None and o is not op:
                    if not (o.eng == "tensor" and op.eng == "tensor"):
                        assert o.cval is not None
                        need[o.csem] = max(need.get(o.csem, 0), o.cval)
            for b in op.writes:
                o = last_w.get(b)
                lst = list(readers.get(b, ()))
                if o is not None:
                    lst.append(o)
                for o in lst:
                    if o is op:
                        continue
                    if o.eng == "tensor" and op.eng == "tensor":
                        continue
                    if o.eng == op.eng and b in op.preads and b in o.preads:
                        continue
                    need[o.csem] = max(need.get(o.csem, 0), o.cval)
            for b in op.reads:
                readers.setdefault(b, []).append(op)
            for b in op.writes:
                last_w[b] = op
                readers[b] = []
            w = waited[op.eng]
            for s, v in need.items():
                if s == op.csem and v >= op.cval:
                    raise RuntimeError("self-wait deadlock")
                if w.get(s, 0) < v:
                    w[s] = v
                    op.waits.append((s, v))
            all_prev[op.csem] = max(all_prev.get(op.csem, 0), op.cval)
        final = dict(all_prev)
        sems = {}
        for op in ops:
            if op is None:
                continue
            if op.csem not in sems:
                sems[op.csem] = stack.enter_context(nc.semaphore("s%d" % len(sems)))
        self.n_sems = len(sems)
        block = stack.enter_context(nc.Block())
        per_eng = {e: [o for o in ops if o is not None and o.eng == e] for e in ENGS}

        def body(engname):
            def f(eng):
                for op in per_eng[engname]:
                    for s, v in op.waits:
                        eng.wait_ge(sems[s], v)
                    ins = op.fn(eng)
                    if op.dma:
                        ins.then_inc(sems[op.csem], 16)
                    elif op.sig:
                        ins.then_inc(sems[op.csem], 1)
                if engname == "sync":
                    for s, v in final.items():
                        eng.wait_ge(sems[s], v)
            return f

        block.tensor(body("tensor"))
        block.vector(body("vector"))
        block.scalar(body("scalar"))
        block.gpsimd(body("gpsimd"))
        block.sync(body("sync"))
        return {e: len(per_eng[e]) for e in ENGS}, dict(eng_cnt)


class Arena:
    def __init__(self, ap):
        self.ap = ap
        self.off = 0
        self.cap = ap.shape[1]

    def tile(self, shape, dt):
        n = int(np.prod(shape))
        nb = n * (4 if dt == F32 else 2)
        nw = (nb + 3) // 4
        nw = (nw + 7) // 8 * 8
        a = self.ap[:, self.off:self.off + nw]
        self.off += nw
        assert self.off <= self.cap, ("arena overflow", self.off, self.cap)
        if dt != F32:
            a = a.bitcast(dt)
        a = a[:, 0:n]
        if len(shape) == 2:
            a = a.rearrange("p (a b) -> p a b", a=shape[0])
        elif len(shape) == 3:
            a = a.rearrange("p (a b c) -> p a b c", a=shape[0], b=shape[1])
        elif len(shape) == 4:
            a = a.rearrange("p (a b c d) -> p a b c d", a=shape[0], b=shape[1], c=shape[2])
        return a


def bc(ap, shape):
    return ap.to_broadcast(list(shape))


PV_ANORM, PV_MIX, PV_W0, PV_A0, PV_KK, PV_KA, PV_RK, PV_LNW, PV_LNB = 0, 1, 7, 8, 9, 10, 11, 12, 13
PV_MLP0, PV_MLP1, PV_KVN, PV_BN, PV_SK = 14, 15, 16, 17, 18
NV = 19
EXPM05 = math.exp(-0.5)


class Ctx:
    pass


class _Stop(Exception):
    pass


def chk(n):
    if os.environ.get('KSTOP') == str(n):
        raise _Stop()


def build(phases, debug_outs=()):
    nc = bass.Bass("TRN2", target_bir_lowering=False)
    C = Ctx()
    C.nc = nc
    dr = {}

    def din(name, shape, dt=F32):
        dr[name] = nc.dram_tensor(name, list(shape), dt, kind="ExternalInput").ap()

    def dscr(name, shape, dt=F32):
        kind = "ExternalOutput" if name in debug_outs else "Internal"
        dr[name] = nc.dram_tensor(name, list(shape), dt, kind=kind).ap()

    din("x", [T, D])
    for w in ["w_r", "w_k", "w_v", "w_out", "w_q", "w_o"]:
        din(w, [D, D])
    din("w1", [D, 96]); din("w2", [96, D]); din("a1", [D, 96]); din("a2", [96, D])
    din("g1", [D, 256]); din("g2", [256, D])
    din("up0", [D, FF]); din("up1", [D, FF]); din("dn0", [FF, D]); din("dn1", [FF, D])
    din("w_kv", [D, 512])
    din("pv", [128, NV * 16])
    din("pvb", [128, 128])
    din("c_ident", [128, 128]); din("c_bones", [128, 128])
    din("c_m64", [128, 3 * 64])
    din("c_cs", [128, 16 * 64])
    din("c_am", [128, 256])
    dr["out"] = nc.dram_tensor("out", [T, D], F32, kind="ExternalOutput").ap()
    dscr("h", [T, D])
    dscr("ARt", [D, 32, 2, 64], BF16); dscr("BTt", [D, T], BF16); dscr("KTt", [D, T], BF16)
    dscr("Bs", [16, 128, 32, 64], BF16); dscr("Ks", [16, 128, 32, 64], BF16); dscr("Vs", [16, 128, 32, 64], BF16)
    dscr("GCt", [D, 32]); dscr("BON", [D, T]); dscr("Gt", [D, T], BF16)
    dscr("ygT", [D, T], BF16)
    dscr("KTs", [4, 64, T], BF16); dscr("Vt", [T, 256], BF16)
    C.dr = dr

    with ExitStack() as st:
        arena_ap = st.enter_context(nc.sbuf_tensor("arena", [128, ARENA_WORDS], F32))
        ps = st.enter_context(nc.psum_tensor("ps", [128, 8, 512], F32))
        C.ps = ps
        P = Prog(nc)
        C.P = P
        A = Arena(arena_ap)
        C.A = A
        C.ident = A.tile([128], F32)
        C.identb = A.tile([128], BF16)
        C.bones = A.tile([128], F32)
        C.m64 = A.tile([3, 64], F32)
        C.pv = A.tile([NV, 16], F32)
        C.pvb = A.tile([128], F32)
        C.cs = A.tile([16, 64], F32)
        C.am = A.tile([2, 128], F32)
        C.esk = A.tile([16], F32)
        P.dma("sync", C.ident, dr["c_ident"], writes=["ident"], semkey="c0")
        P.dma("sync", C.bones, dr["c_bones"], writes=["bones"], semkey="c1")
        P.dma("sync", C.m64, dr["c_m64"].rearrange("p (a b) -> p a b", a=3), writes=["m64"], semkey="c2")
        P.dma("sync", C.pv, dr["pv"].rearrange("p (a b) -> p a b", a=NV), writes=["pv"], semkey="c3")
        P.dma("sync", C.pvb, dr["pvb"], writes=["pvb"], semkey="c4")
        P.dma("sync", C.cs, dr["c_cs"].rearrange("p (a b) -> p a b", a=16), writes=["cs"], semkey="c5")
        P.dma("sync", C.am, dr["c_am"].rearrange("p (a b) -> p a b", a=2), writes=["am"], semkey="c6")
        P.cp("vector", C.identb, C.ident, reads=["ident"], writes=["identb"])
        P.act(C.esk, C.pv[:, PV_SK, :], AF.Exp, reads=["pv"], writes=["esk"])
        C.base = A.off
        for ph in phases:
            A.off = C.base
            P.barrier()
            try:
                ph(C)
            except _Stop:
                break
        P.barrier()
        info = P.emit(st)
        C.info = info
    return nc, C


def pk(bank, lo=0, hi=512):
    return [("ps", bank)]


def rsqrt(P, out, in_, scale, eps, rk, wk):
    P.ts("vector", out, in_, scale, eps, ALU.mult, ALU.add, reads=rk, writes=wk)
    P.act(out, out, AF.Sqrt, reads=wk, writes=wk)
    P.add("vector", lambda e: e.reciprocal(out, out), reads=wk, writes=wk)


def norm_T(C, src, tok0, gain_idx, dst, dst_key, col0, xt, xtk, junk, ss, psbanks=(6, 7), junk_keys=("junk",)):
    P = C.P
    ps = C.ps
    P.dma("sync", xt, src[tok0:tok0 + 128, :], writes=[xtk], semkey=xtk)
    chk('n0')
    P.act(junk, xt, AF.Square, reads=[xtk], writes=list(junk_keys) + ["ss0"], accum_out=ss[:, 0:1])
    chk('n1')
    rsqrt(P, ss[:, 2:3], ss[:, 0:1], 1.0 / D, 1e-6, ["ss0"], ["ss2"])
    chk('n2')
    P.ts("vector", xt, xt, ss[:, 2:3], None, ALU.mult, reads=[xtk, "ss2"], writes=[xtk])
    chk('n3')
    for gq in range(NCH // 4):
        b = psbanks[gq % 2]
        k_ = pk(b)
        for q in range(4):
            c = gq * 4 + q
            P.tr(ps[:, b, q * 128:(q + 1) * 128], xt[:, c * 128:(c + 1) * 128], C.ident, reads=[xtk, "ident"], writes=k_,
                 sig=(q == 3))
        for q in range(4):
            c = gq * 4 + q
            o = dst[:, c, col0:col0 + 128]
            g = C.pv[:, gain_idx, c:c + 1]
            if gq % 2 == 0:
                P.ts("vector", o, ps[:, b, q * 128:(q + 1) * 128], g, None, ALU.mult, reads=k_ + ["pv"], writes=[(dst_key, c)])
            else:
                P.act(o, ps[:, b, q * 128:(q + 1) * 128], AF.Copy, reads=k_ + ["pv"], writes=[(dst_key, c)], scale=g)


def load_panel(C, wp, wpk, w, r0, c0, rows=2048, cols=512):
    kc = rows // 128
    C.P.dma("gpsimd", wp[:, 0:kc, 0:cols], w[r0:r0 + rows, c0:c0 + cols].rearrange("(k p) n -> p k n", p=128),
            writes=[wpk], semkey=wpk)


class Deferred:
    def __init__(self):
        self.calls = []

    def __getattr__(self, name):
        def f(*a, **k):
            self.calls.append((name, a, k))
        return f


def replay_interleaved(P, lists, weights=None):
    ptr = [0] * len(lists)
    if weights is None:
        weights = [1] * len(lists)
    while any(ptr[i] < len(l) for i, l in enumerate(lists)):
        for i, l in enumerate(lists):
            for _ in range(weights[i]):
                while ptr[i] < len(l):
                    name, a, k = l[ptr[i]]
                    ptr[i] += 1
                    getattr(P, name)(*a, **k)
                    if k.get("sig", True):
                        break


class WStream:
    def __init__(self, C, plan, tiles, keys):
        self.C, self.plan, self.tiles, self.keys = C, plan, tiles, keys
        self.n = len(tiles)
        self.issued = 0
        self.ndone = 0
        self.nget = 0

    def _pump(self):
        while self.issued < len(self.plan) and self.issued - self.n < self.ndone:
            j = self.issued
            w, r0, c0 = self.plan[j]
            load_panel(self.C, self.tiles[j % self.n], self.keys[j % self.n], w, r0, c0)
            self.issued += 1

    def get(self):
        self._pump()
        j = self.nget
        assert j < self.issued
        self.nget += 1
        return self.tiles[j % self.n], self.keys[j % self.n]

    def done(self):
        self.ndone += 1
        self._pump()


def proj_tm(C, lhs, lhs_key, KC, w, nsub, WS, res_src, res_dst, tok0, rtiles, name, cnt):
    P = C.P
    ps = C.ps
    npan = KC // 16
    L_ = len(rtiles)
    for n in range(4):
        ris = [(cnt[1] + s) % L_ for s in range(nsub)]
        cnt[1] += nsub
        npre = min(nsub, L_)
        for s in range(npre):
            rt, rk = rtiles[ris[s]]
            t0 = tok0 + s * 128
            P.dma("sync", rt, res_src[t0:t0 + 128, n * 512:(n + 1) * 512], writes=[rk], semkey=rk)
        for kp in range(npan):
            wp_, wpk_ = WS.get()
            for k in range(16):
                kk = kp * 16 + k
                for s in range(nsub):
                    bank = 4 + s
                    P.mm(ps[:, bank, :], lhs[:, kk, s * 128:(s + 1) * 128], wp_[:, k, :],
                         start=(kk == 0), stop=(kk == KC - 1),
                         reads=[wpk_, (lhs_key, kk)], writes=pk(bank), sig=(k == 15 and s == nsub - 1))
            WS.done()
        for s in range(nsub):
            bank = 4 + s
            rt, rk = rtiles[ris[s]]
            t0 = tok0 + s * 128
            if s >= npre:
                P.dma("sync", rt, res_src[t0:t0 + 128, n * 512:(n + 1) * 512], writes=[rk], semkey=rk)
            P.tt("vector", rt, rt, ps[:, bank, :], ALU.add, reads=[rk] + pk(bank), writes=[rk])
            P.dma("sync", res_dst[t0:t0 + 128, n * 512:(n + 1) * 512], rt, reads=[rk],
                  writes=[("dram", name, t0 // 128, n)], semkey=rk + "o")


def phase_A1(C):
    P, A, ps, dr, pv = C.P, C.A, C.ps, C.dr, C.pv
    TT = 256
    NQ = TT // 64
    xt = A.tile([2048], F32)
    junk_raw = A.tile([1024], F32)
    junk = junk_raw.bitcast(BF16)
    ss = A.tile([4], F32)
    xnT = A.tile([16, TT], BF16)
    xxT = A.tile([16, TT], BF16)
    xlast = A.tile([16, 1], BF16)
    xl = [A.tile([16, TT], BF16) for _ in range(3)]
    xs_raw = A.tile([2048], F32)
    xs = xs_raw.bitcast(BF16).rearrange("p (k n) -> p k n", k=16)
    stage32 = A.tile([4096], F32)
    wps = [stage32.bitcast(BF16).rearrange("p (k n) -> p k n", k=16), A.tile([16, 512], BF16)]
    wpks = ["wpA0", "wpA1"]
    w1b = A.tile([16, 96], BF16)
    a1b = A.tile([16, 96], BF16)
    g1b = A.tile([16, 256], BF16)
    w2b = A.tile([2048], BF16)
    a2b = A.tile([2048], BF16)
    g2b = A.tile([2, 2048], BF16)
    twT = A.tile([TT], BF16)
    taT = A.tile([TT], BF16)
    sgT = A.tile([2, TT], BF16)

    def ld(dst, src_ap, view, key):
        P.dma("sync", view, src_ap, writes=["wpA0"], semkey="ldsmall")
        P.cp("vector", dst, view, reads=["wpA0"], writes=[key])

    ld(w1b, dr["w1"].rearrange("(k p) n -> p k n", p=128), stage32[:, 0:16 * 96].rearrange("p (k n) -> p k n", k=16), "w1b")
    ld(a1b, dr["a1"].rearrange("(k p) n -> p k n", p=128), stage32[:, 0:16 * 96].rearrange("p (k n) -> p k n", k=16), "a1b")
    ld(g1b, dr["g1"].rearrange("(k p) n -> p k n", p=128), stage32[:, 0:16 * 256].rearrange("p (k n) -> p k n", k=16), "g1b")
    ld(w2b[0:96, :], dr["w2"], stage32[0:96, 0:2048], "w2b")
    ld(a2b[0:96, :], dr["a2"], stage32[0:96, 0:2048], "a2b")
    ld(g2b, dr["g2"].rearrange("(k p) n -> p k n", p=128), stage32[:, 0:4096].rearrange("p (k n) -> p k n", k=2), "g2b")
    P.add("vector", lambda e: e.memset(xlast, 0.0), writes=["xlast"])

    def stg(n, dt=F32, shape=(TT,)):
        return [A.tile(list(shape), dt) for _ in range(n)]

    r_s, k_s, a_s, d_s = stg(4), stg(4), stg(4), stg(4)
    v_s = stg(2)
    g_sb = stg(2, BF16)
    negka = A.tile([16], F32)
    P.ts("vector", negka, pv[:, PV_KA, :], -1.0, None, ALU.mult, reads=["pv"], writes=["negka"])
    TN = ["kk0", "sq", "nkk", "bb", "km", "t1", "rk", "e1", "e2"]
    TMP = {}
    TKY = {}
    for j, nm in enumerate(TN):
        if j < 8:
            alt = xs_raw[:, j * 256:(j + 1) * 256]
            altk = [("xs", 2 * j), ("xs", 2 * j + 1)]
        else:
            alt = stg(1)[0]
            altk = [(nm, 1)]
        TMP[nm] = [stg(1)[0], alt]
        TKY[nm] = [[(nm, 0)], altk]
    cA = [stg(1, F32, (NQ, 96))[0], junk_raw[:, 0:384].rearrange("p (a b) -> p a b", a=NQ)]
    cB = [stg(1, F32, (NQ, 96))[0], junk_raw[:, 384:768].rearrange("p (a b) -> p a b", a=NQ)]
    bon = [stg(1)[0], junk_raw[:, 768:1024]]
    CK = [[("cAB", 0)], ["junk"]]
    BK = [[("bon", 0)], ["junk"]]
    P.add("vector", lambda e: e.memset(cA[0], 0.0), writes=CK[0])
    P.add("vector", lambda e: e.memset(cB[0], 0.0), writes=CK[0])
    gc = stg(2, F32, (NQ,))
    AR = stg(2, BF16, (NQ, 2, 64))
    btil, ktil = stg(2, BF16), stg(2, BF16)
    bhat = stg(2, BF16, (NQ, 64))
    khat = stg(2, BF16, (NQ, 64))
    vb = stg(2, BF16, (NQ, 64))
    stk = stg(2, BF16, (3, NQ, 64))
    print("A1 arena words", A.off)
    v3 = lambda a: a.rearrange("p (a b) -> p a b", a=NQ)

    def post(P, c, tt, tok0):
        s4 = c % 4
        si = c % 2
        hb = (c % 2) * 3
        K4 = lambda nm: (nm, s4)
        K2 = lambda nm: (nm, si)
        rs, ks, as_, ds, vs = r_s[s4], k_s[s4], a_s[s4], d_s[s4], v_s[si]
        kk0, sq, nkk, bb, km, t1, rk, e1, e2 = [TMP[n][si] for n in TN]
        kk0k, sqk, nkkk, bbk, kmk, t1k, rkk, e1k, e2k = [TKY[n][si] for n in TN]
        ck = CK[si]
        bk = BK[si]
        P.act(kk0, ks, AF.Copy, reads=[K4("k_s"), "pv"], writes=kk0k, scale=pv[:, PV_KK, c:c + 1])
        P.act(sq, kk0, AF.Square, reads=kk0k, writes=sqk)
        pA = ps[:, hb + 2, 256:256 + TT]
        pAk = pk(hb + 2)
        P.mm(pA, C.bones, sq, reads=["bones"] + sqk, writes=pAk)
        P.act(sq, pA, AF.Sqrt, reads=pAk, writes=sqk)
        P.ts("vector", sq, sq, 1e-12, None, ALU.max, reads=sqk, writes=sqk)
        P.add("vector", lambda e: e.reciprocal(sq, sq), reads=sqk, writes=sqk)
        P.stt("vector", nkk, kk0, -1.0, sq, ALU.mult, ALU.mult, reads=kk0k + sqk, writes=nkkk)
        P.stt("vector", bb, nkk, -1.0, as_, ALU.mult, ALU.mult, reads=nkkk + [K4("a_s")], writes=bbk)
        P.act(t1, as_, AF.Identity, reads=[K4("a_s"), "pv", "negka"], writes=t1k, scale=pv[:, PV_KA, c:c + 1], bias=negka[:, c:c + 1])
        P.stt("vector", km, t1, 1.0, ks, ALU.add, ALU.mult, reads=t1k + [K4("k_s")], writes=kmk)
        P.stt("vector", rk, rs, pv[:, PV_RK, c:c + 1], km, ALU.mult, ALU.mult, reads=[K4("r_s"), "pv"] + kmk, writes=rkk)
        pB = ps[:, hb + 2, 0:TT]
        P.mm(pB, C.bones, rk, reads=["bones"] + rkk, writes=pAk)
        P.tt("vector", bon[si], pB, vs, ALU.mult, reads=pAk + [K2("v_s")], writes=bk)
        P.dma("sync", dr["BON"][c * 128:(c + 1) * 128, tok0:tok0 + TT], bon[si], reads=bk,
              writes=[("dram", "BON", c, tt)], semkey="oBON%d" % si)
        ca, cb = cA[si], cB[si]
        P.cp("vector", ca[:, :, 32:96], v3(ds), reads=[K4("d_s")], writes=ck)
        src, dst = ca, cb
        for sh in [1, 2, 4, 8, 16, 32]:
            P.tt("vector", dst[:, :, 32:96], src[:, :, 32:96], src[:, :, 32 - sh:96 - sh], ALU.add, reads=ck, writes=ck)
            src, dst = dst, src
        Lp = src[:, :, 32:96]
        P.act(v3(e1), Lp, AF.Exp, reads=ck, writes=e1k, scale=-EXPM05)
        P.act(v3(e2), Lp, AF.Exp, reads=ck, writes=e2k, scale=EXPM05)
        P.tt("gpsimd", v3(t1), Lp, v3(ds), ALU.subtract, reads=ck + [K4("d_s")], writes=t1k)
        P.act(t1, t1, AF.Exp, reads=t1k, writes=t1k, scale=-EXPM05)
        P.tt("gpsimd", v3(kk0), bc(src[:, :, 95:96], [128, NQ, 64]), Lp, ALU.subtract, reads=ck, writes=kk0k)
        P.act(kk0, kk0, AF.Exp, reads=kk0k, writes=kk0k, scale=-EXPM05)
        P.cp("gpsimd", gc[si], v3(e1)[:, :, 63], reads=e1k, writes=[K2("gc")])
        P.dma("sync", dr["GCt"][c * 128:(c + 1) * 128, tt * NQ:(tt + 1) * NQ], gc[si], reads=[K2("gc")],
              writes=[("dram", "GCt", c, tt)], semkey="oGC%d" % si)
        ar = AR[si]
        P.tt("vector", ar[:, :, 0, :], v3(nkk), v3(t1), ALU.mult, reads=nkkk + t1k, writes=[K2("AR0")])
        P.tt("gpsimd", ar[:, :, 1, :], v3(rs), v3(e1), ALU.mult, reads=[K4("r_s")] + e1k, writes=[K2("AR1")])
        P.dma("sync", dr["ARt"][c * 128:(c + 1) * 128, tt * NQ:(tt + 1) * NQ, :, :], ar, reads=[K2("AR0"), K2("AR1")],
              writes=[("dram", "ARt", c, tt)], semkey="oAR%d" % si)
        P.tt("vector", btil[si], bb, e2, ALU.mult, reads=bbk + e2k, writes=[K2("btil")])
        P.dma("sync", dr["BTt"][c * 128:(c + 1) * 128, tok0:tok0 + TT], btil[si], reads=[K2("btil")],
              writes=[("dram", "BTt", c, tt)], semkey="oBT%d" % si)
        P.tt("gpsimd", ktil[si], km, e2, ALU.mult, reads=kmk + e2k, writes=[K2("ktil")])
        P.dma("sync", dr["KTt"][c * 128:(c + 1) * 128, tok0:tok0 + TT], ktil[si], reads=[K2("ktil")],
              writes=[("dram", "KTt", c, tt)], semkey="oKT%d" % si)
        P.tt("vector", bhat[si], v3(bb), v3(kk0), ALU.mult, reads=bbk + kk0k, writes=[K2("bhat")])
        P.tt("gpsimd", khat[si], v3(km), v3(kk0), ALU.mult, reads=kmk + kk0k, writes=[K2("khat")])
        pst = ps[:, hb, 256:512].bitcast(BF16).rearrange("p (a b c) -> p a b c", a=2, b=NQ)
        pst2 = ps[:, hb + 1, 256:384].bitcast(BF16).rearrange("p (b c) -> p b c", b=NQ)
        for oi, (srct, sk_) in enumerate([(bhat[si], "bhat"), (khat[si], "khat"), (vb[si], "vb")]):
            for q in range(NQ):
                for h in range(2):
                    hp = slice(h * 64, (h + 1) * 64)
                    dstp = pst[hp, oi, q, :] if oi < 2 else pst2[hp, q, :]
                    dk_ = pk(hb) if oi < 2 else pk(hb + 1)
                    last = (q == NQ - 1 and h == 1)
                    P.tr(dstp, srct[hp, q, :], C.identb[hp, hp], reads=[K2(sk_), "identb"], writes=dk_, sig=last)
        P.cp("scalar", stk[si][:, 0:2, :, :], pst, reads=pk(hb), writes=[K2("stk01")])
        P.cp("scalar", stk[si][:, 2, :, :], pst2, reads=pk(hb + 1), writes=[K2("stk2")])
        for oi, nm in enumerate(["Bs", "Ks", "Vs"]):
            P.dma("sync", dr[nm][c, :, tt * NQ:(tt + 1) * NQ, :], stk[si][:, oi, :, :],
                  reads=[K2("stk01") if oi < 2 else K2("stk2")], writes=[("dram", nm, c, tt)], semkey="o%s%d" % (nm, si))

    chk('a')
    xx_junk = xxT.rearrange("p a b -> p (a b)")[:, 0:2048]
    nxt = {"calls": [], "ptr": 0}

    def drain(n):
        L = nxt["calls"]
        done = 0
        while nxt["ptr"] < len(L) and done < n:
            name, a, k = L[nxt["ptr"]]
            nxt["ptr"] += 1
            getattr(P, name)(*a, **k)
            if k.get("sig", True):
                done += 1

    for tt in range(T // TT):
        tok0 = tt * TT
        if tt == 0:
            for s in range(TT // 128):
                norm_T(C, dr["x"], tok0 + s * 128, PV_ANORM, xnT, "xnT", s * 128, xt, "xtA", junk, ss)
        chk(1)
        xn_all = [("xnT", c) for c in range(NCH)]
        P.tt("vector", xxT[:, :, 1:TT], xnT[:, :, 0:TT - 1], xnT[:, :, 1:TT], ALU.subtract, reads=xn_all, writes=["xxT_a"])
        P.tt("vector", xxT[:, :, 0:1], xlast, xnT[:, :, 0:1], ALU.subtract, reads=xn_all + ["xlast"], writes=["xxT_b"])
        P.cp("vector", xlast, xnT[:, :, TT - 1:TT], reads=xn_all + ["xxT_b"], writes=["xlast"])
        xx_all = ["xxT_a", "xxT_b"]

        def lerp(m, dst, dkey, eng0):
            for c in range(NCH):
                eng = "vector" if (c + eng0) % 2 == 0 else "gpsimd"
                P.stt(eng, dst[:, c, :], xxT[:, c, :], pv[:, PV_MIX + m, c:c + 1], xnT[:, c, :], ALU.mult, ALU.add,
                      reads=xn_all + xx_all + ["pv"], writes=[(dkey, c)])

        lerp(3, xs, "xs", 0)
        for k in range(NCH):
            P.mm(ps[0:96, 0, 0:TT], w1b[:, k, :], xs[:, k, :], start=(k == 0), stop=(k == 15),
                 reads=["w1b", ("xs", k)], writes=pk(0, 0, 256), sig=(k == 15))
        P.act(twT[0:96, :], ps[0:96, 0, 0:TT], AF.Tanh, reads=pk(0, 0, 256), writes=["twT"])
        lerp(4, xs, "xs", 1)
        for k in range(NCH):
            P.mm(ps[0:96, 1, 0:TT], a1b[:, k, :], xs[:, k, :], start=(k == 0), stop=(k == 15),
                 reads=["a1b", ("xs", k)], writes=pk(1, 0, 256), sig=(k == 15))
        P.cp("vector", taT[0:96, :], ps[0:96, 1, 0:TT], reads=pk(1, 0, 256), writes=["taT"])
        lerp(5, xs, "xs", 0)
        for j in range(2):
            for k in range(NCH):
                P.mm(ps[:, 2 + j, 0:TT], g1b[:, k, j * 128:(j + 1) * 128], xs[:, k, :], start=(k == 0), stop=(k == 15),
                     reads=["g1b", ("xs", k)], writes=pk(2 + j, 0, 256), sig=(k == 15))
            P.act(sgT[:, j, :], ps[:, 2 + j, 0:TT], AF.Sigmoid, reads=pk(2 + j, 0, 256), writes=[("sgT", j)])
        chk(2)
        for m in range(3):
            lerp(m, xl[m], "xl%d" % m, m)
        chk(3)
        P.add("vector", lambda e: e.memset(cA[1], 0.0), writes=CK[1])
        P.add("vector", lambda e: e.memset(cB[1], 0.0), writes=CK[1])
        nxt["calls"], nxt["ptr"] = [], 0
        if tt + 1 < T // TT:
            C.P = dq = Deferred()
            for s in range(TT // 128):
                norm_T(C, dr["x"], tok0 + TT + s * 128, PV_ANORM, xnT, "xnT", s * 128, xt, "xtA", xx_junk, ss,
                       junk_keys=("xxT_a", "xxT_b"))
            C.P = P
            nxt["calls"] = dq.calls
        for pn in range(4):
            if tt == 0 and pn == 0:
                load_panel(C, wps[0], wpks[0], dr["w_r"], 0, pn * 512)
                load_panel(C, wps[1], wpks[1], dr["w_k"], 0, pn * 512)
            is_last = (tt == T // TT - 1 and pn == 3)
            npn = (pn + 1) % 4
            for c4 in range(4):
                c = pn * 4 + c4
                s4 = c % 4
                si = c % 2
                hb = (c % 2) * 3

                def half(i):
                    b_ = hb + i // 2
                    lo = (i % 2) * 256
                    return ps[:, b_, lo:lo + TT], pk(b_, lo, lo + 256)

                for m in range(2):
                    o, ok = half(m)
                    for k in range(NCH):
                        P.mm(o, wps[m][:, k, c4 * 128:(c4 + 1) * 128], xl[m][:, k, :], start=(k == 0), stop=(k == 15),
                             reads=[wpks[m], ("xl%d" % m, k)], writes=ok, sig=(k == 15))
                o, ok = half(3)
                P.mm(o, a2b[0:96, c * 128:(c + 1) * 128], taT[0:96, :], reads=["a2b", "taT"], writes=ok)
                o, ok = half(4)
                P.mm(o, w2b[0:96, c * 128:(c + 1) * 128], twT[0:96, :], reads=["w2b", "twT"], writes=ok)
                o, ok = half(5)
                for j in range(2):
                    P.mm(o, g2b[:, j, c * 128:(c + 1) * 128], sgT[:, j, :], start=(j == 0), stop=(j == 1),
                         reads=["g2b", ("sgT", 0), ("sgT", 1)], writes=ok, sig=(j == 1))
                o, ok = half(0)
                P.cp("scalar", r_s[s4], o, reads=ok, writes=[("r_s", s4)])
                o, ok = half(1)
                P.cp("vector", k_s[s4], o, reads=ok, writes=[("k_s", s4)])
                o, ok = half(3)
                P.act(a_s[s4], o, AF.Sigmoid, reads=ok + ["pv"], writes=[("a_s", s4)], bias=pv[:, PV_A0, c:c + 1])
                o, ok = half(4)
                P.act(d_s[s4], o, AF.Sigmoid, reads=ok + ["pv"], writes=[("d_s", s4)], bias=pv[:, PV_W0, c:c + 1])
                o, ok = half(5)
                P.cp("vector", g_sb[si], o, reads=ok, writes=[("g_sb", si)])
                P.dma("sync", dr["Gt"][c * 128:(c + 1) * 128, tok0:tok0 + TT], g_sb[si], reads=[("g_sb", si)],
                      writes=[("dram", "Gt", c, tt)], semkey="oG%d" % si)
            chk(4)
            load_panel(C, wps[0], wpks[0], dr["w_v"], 0, pn * 512)
            if not is_last:
                load_panel(C, wps[1], wpks[1], dr["w_k"], 0, npn * 512)
            for c2 in range(2):
                dfs = []
                for c4 in (2 * c2, 2 * c2 + 1):
                    c = pn * 4 + c4
                    si = c % 2
                    hb = (c % 2) * 3
                    o = ps[:, hb + 1, 0:TT]
                    ok = pk(hb + 1)
                    for k in range(NCH):
                        P.mm(o, wps[0][:, k, c4 * 128:(c4 + 1) * 128], xl[2][:, k, :], start=(k == 0), stop=(k == 15),
                             reads=[wpks[0], ("xl2", k)], writes=ok, sig=(k == 15))
                    P.cp("scalar", v_s[si], o, reads=ok, writes=[("v_s", si)])
                    P.cp("scalar", vb[si].rearrange("p a b -> p (a b)"), o, reads=ok, writes=[("vb", si)])
                if c2 == 1 and not is_last:
                    load_panel(C, wps[0], wpks[0], dr["w_r"], 0, npn * 512)
                for c4 in (2 * c2, 2 * c2 + 1):
                    c = pn * 4 + c4
                    df = Deferred()
                    post(df, c, tt, tok0)
                    dfs.append(df.calls)
                replay_interleaved(P, dfs)
                drain(8)
        drain(10 ** 6)


def phase_A2(C):
    P, A, ps, dr, pv = C.P, C.A, C.ps, C.dr, C.pv
    G = 4
    AR = A.tile([G, 32, 128], BF16)
    BT = A.tile([G, T], BF16)
    KT = A.tile([G, T], BF16)
    Bs = A.tile([G, 32, 64], BF16)
    Ks = A.tile([G, 32, 64], BF16)
    Vs = A.tile([G, 32, 64], BF16)
    GC = A.tile([G, 32], F32)
    NTbs = [A.tile([G, 128], BF16) for _ in range(2)]
    Nbs = [A.tile([G, 128], BF16) for _ in range(2)]
    AKb2 = [A.tile([G, 128], BF16) for _ in range(3)]
    XTF = [A.tile([G, 128], BF16) for _ in range(3)]
    Pms = [[A.tile([G, 128], BF16) for _ in range(2)] for _ in range(2)]
    PTms = [[A.tile([G, 128], BF16) for _ in range(2)] for _ in range(2)]
    XTs = [[A.tile([G, 128], BF16) for _ in range(2)] for _ in range(2)]
    ARB2 = [A.tile([G, 64], BF16) for _ in range(3)]
    ARK2 = [A.tile([G, 64], BF16) for _ in range(3)]
    W1s = A.tile([G, 64], BF16)
    Us = A.tile([G, 64], BF16)
    Sf = A.tile([G, 64], F32)
    Stmp = A.tile([G, 64], F32)
    Sb = A.tile([G, 64], BF16)
    yb = [A.tile([G, 256], F32)] * 2
    ysq = A.tile([G, 256], F32)
    mean = A.tile([G, 256], F32)
    m2 = A.tile([G, 256], F32)
    var = ysq
    bonb = A.tile([G, 256], F32)
    gb = A.tile([G, 256], BF16)
    ygb = [A.tile([G, 256], BF16)] * 2
    print("A2 arena words", A.off)
    m_uts = C.m64[:, 0, :]
    m_uti = C.m64[:, 1, :]
    m_lts = C.m64[:, 2, :]
    for t_, k_ in [(NTbs[0], ("NTb", 0)), (NTbs[1], ("NTb", 1)), (Nbs[0], ("Nb", 0)), (Nbs[1], ("Nb", 1)),
                   (AKb2[0], ("AKb", 0)), (AKb2[1], ("AKb", 1)), (AKb2[2], ("AKb", 2))]:
        P.add("vector", lambda e, t_=t_: e.memset(t_, 0.0), writes=[k_])
    identG = bc(C.identb.unsqueeze(1), [128, G, 128])
    for grp in range(16 // G):
        p0 = grp * G
        f0 = p0 * 128
        fs = slice(f0, f0 + G * 128)
        P.dma("sync", AR, dr["ARt"][fs].rearrange("(g p) c a t -> p g c (a t)", p=128), writes=["AR"], semkey="lAR")
        P.dma("sync", BT, dr["BTt"][fs].rearrange("(g p) t -> p g t", p=128), writes=["BT"], semkey="lBT")
        P.dma("sync", KT, dr["KTt"][fs].rearrange("(g p) t -> p g t", p=128), writes=["KT"], semkey="lKT")
        P.dma("sync", Bs, dr["Bs"][p0:p0 + G].rearrange("g p c k -> p g c k"), writes=["Bs"], semkey="lBs")
        P.dma("sync", Ks, dr["Ks"][p0:p0 + G].rearrange("g p c k -> p g c k"), writes=["Ks"], semkey="lKs")
        P.dma("sync", Vs, dr["Vs"][p0:p0 + G].rearrange("g p c k -> p g c k"), writes=["Vs"], semkey="lVs")
        P.dma("sync", GC, dr["GCt"][fs].rearrange("(g p) c -> p g c", p=128), writes=["GC"], semkey="lGC")
        P.add("vector", lambda e: e.memset(Sf, 0.0), writes=["Sf"])
        P.add("vector", lambda e: e.memset(Sb, 0.0), writes=["Sb"])
        def front(P, c):
            st = c % 2
            par = c % 3
            b0, b1, b2 = 3 * st, 3 * st + 1, 3 * st + 2
            NTb, Nb, Pm, PTm, XT = NTbs[st], Nbs[st], Pms[st], PTms[st], XTs[st]
            kNT, kN = ("NTb", st), ("Nb", st)
            ts_ = slice(c * 64, (c + 1) * 64)
            for g in range(G):
                for h in range(2):
                    hp = slice(h * 64, (h + 1) * 64)
                    st_ = (g == 0)
                    P.mm(ps[hp, b0, g * 128:(g + 1) * 128], BT[hp, g, ts_], AR[hp, g, c, :], start=st_, stop=True,
                         reads=["BT", "AR"], writes=pk(b0), sig=False)
                    P.mm(ps[hp, b1, g * 128:(g + 1) * 128], KT[hp, g, ts_], AR[hp, g, c, :], start=st_, stop=True,
                         reads=["KT", "AR"], writes=pk(b1), sig=False)
                    P.mm(ps[hp, b2, g * 64:(g + 1) * 64], AR[hp, g, c, 0:64], BT[hp, g, ts_], start=st_, stop=True,
                         reads=["BT", "AR"], writes=pk(b2), sig=(g == G - 1 and h == 1))
            psAB = ps[:, b0, :].rearrange("p (g n) -> p g n", g=G)
            psAK = ps[:, b1, :].rearrange("p (g n) -> p g n", g=G)
            psN = ps[:, b2, 0:256].rearrange("p (g n) -> p g n", g=G)
            for h in range(2):
                hp = slice(h * 64, (h + 1) * 64)
                cs_ = slice(h * 64, (h + 1) * 64)
                P.tt("vector", NTb[hp, :, cs_], psAB[hp, :, 0:64], bc(m_uts[hp].unsqueeze(1), [64, G, 64]), ALU.mult,
                     reads=pk(b0) + ["m64"], writes=[kNT])
                P.tt("vector", AKb2[par][hp, :, cs_], psAK[hp, :, 0:64], bc(m_uts[hp].unsqueeze(1), [64, G, 64]), ALU.mult,
                     reads=pk(b1) + ["m64"], writes=[("AKb", par)])
                P.tt("vector", Nb[hp, :, cs_], psN[hp, :, :], bc(m_lts[hp].unsqueeze(1), [64, G, 64]), ALU.mult,
                     reads=pk(b2) + ["m64"], writes=[kN])
            P.tt("vector", ARB2[par], psAB[:, :, 64:128], bc(m_uti.unsqueeze(1), [128, G, 64]), ALU.mult,
                 reads=pk(b0) + ["m64"], writes=[("ARB", par)])
            P.tt("vector", ARK2[par], psAK[:, :, 64:128], bc(m_uti.unsqueeze(1), [128, G, 64]), ALU.mult,
                 reads=pk(b1) + ["m64"], writes=[("ARK", par)])
            xi = 0
            P.tt("gpsimd", XT[0], NTb, identG, ALU.add, reads=[kNT, "identb"], writes=[("XT", st, 0)])
            Pc, PTc, Pk, PTk = Nb, NTb, kN, kNT
            for j in range(1, 6):
                pi = j % 2
                for g in range(G):
                    P.mm(ps[:, b0, g * 128:(g + 1) * 128], PTc[:, g, :], Pc[:, g, :], start=(g == 0), stop=True,
                         reads=[Pk, PTk], writes=pk(b0), sig=(g == G - 1))
                P.cp("scalar", Pm[pi].rearrange("p g n -> p (g n)"), ps[:, b0, :], reads=pk(b0), writes=[("Pm", st, pi)])
                if j < 5:
                    for g in range(G):
                        P.mm(ps[:, b1, g * 128:(g + 1) * 128], Pc[:, g, :], PTc[:, g, :], start=(g == 0), stop=True,
                             reads=[Pk, PTk], writes=pk(b1), sig=(g == G - 1))
                    P.cp("scalar", PTm[pi].rearrange("p g n -> p (g n)"), ps[:, b1, :], reads=pk(b1), writes=[("PTm", st, pi)])
                for g in range(G):
                    P.mm(ps[:, b2, g * 128:(g + 1) * 128], Pm[pi][:, g, :], XT[xi][:, g, :], start=(g == 0), stop=True,
                         reads=[("Pm", st, pi), ("XT", st, xi)], writes=pk(b2), sig=(g == G - 1))
                xdst, xdk = (XTF[par], ("XTF", par)) if j == 5 else (XT[1 - xi], ("XT", st, 1 - xi))
                P.tt("vector", xdst.rearrange("p g n -> p (g n)"), XT[xi].rearrange("p g n -> p (g n)"), ps[:, b2, :],
                     ALU.add, reads=pk(b2) + [("XT", st, xi)], writes=[xdk])
                xi = 1 - xi
                Pc, PTc, Pk, PTk = Pm[pi], PTm[pi], ("Pm", st, pi), ("PTm", st, pi)

        def back(P, c):
            par = c % 3
            ts_ = slice(c * 64, (c + 1) * 64)
            XTf, XTk = XTF[par], ("XTF", par)
            psW = ps[:, 6, 0:256].rearrange("p (g n) -> p g n", g=G)
            psU = ps[:, 6, 256:512].rearrange("p (g n) -> p g n", g=G)
            psY = ps[:, 7, 0:256].rearrange("p (g n) -> p g n", g=G)
            psS = ps[:, 7, 256:512].rearrange("p (g n) -> p g n", g=G)
            kW, kU, kY, kS = pk(6, 0, 256), pk(6, 256, 512), pk(7, 0, 256), pk(7, 256, 512)
            fl = [True, True]
            for g in range(G):
                for h in range(2):
                    hp = slice(h * 64, (h + 1) * 64)
                    P.mm(psW[hp, g, :], AR[hp, g, c, 0:64], Sb[hp, g, :], start=fl[h], stop=True,
                         reads=["AR", "Sb"], writes=kW, sig=False)
                    fl[h] = False
            for g in range(G):
                P.mm(psW[:, g, :], AKb2[par][:, g, :], Vs[:, g, c, :], start=False, stop=True,
                     reads=[("AKb", par), "Vs"], writes=kW, sig=(g == G - 1))
            P.cp("scalar", W1s, psW, reads=kW, writes=["W1s"])
            for g in range(G):
                P.mm(psU[:, g, :], XTf[:, g, :], W1s[:, g, :], start=False, stop=True,
                     reads=[XTk, "W1s"], writes=kU, sig=(g == G - 1))
            P.cp("vector", Us, psU, reads=kU, writes=["Us"])
            fl = [True, True]
            for g in range(G):
                for h in range(2):
                    hp = slice(h * 64, (h + 1) * 64)
                    P.mm(psY[hp, g, :], Sb[hp, g, :], AR[hp, g, c, 64:128], start=fl[h], stop=False,
                         reads=["AR", "Sb"], writes=kY, sig=False)
                    fl[h] = False
                    P.mm(psY[hp, g, :], Us[hp, g, :], ARB2[par][hp, g, :], start=False, stop=False,
                         reads=["Us", ("ARB", par)], writes=kY, sig=False)
                    P.mm(psY[hp, g, :], Vs[hp, g, c, :], ARK2[par][hp, g, :], start=False, stop=True,
                         reads=["Vs", ("ARK", par)], writes=kY, sig=(g == G - 1 and h == 1))
            ybi = (c // 4) % 2
            q = c % 4
            P.cp("scalar", yb[ybi][:, :, q * 64:(q + 1) * 64], psY, reads=kY, writes=[("yb", 0, q)])
            for g in range(G):
                for h in range(2):
                    hp = slice(h * 64, (h + 1) * 64)
                    P.mm(psS[hp, g, :], Bs[hp, g, c, :], Us[hp, g, :], start=False, stop=False,
                         reads=["Bs", "Us"], writes=kS, sig=False)
                    P.mm(psS[hp, g, :], Ks[hp, g, c, :], Vs[hp, g, c, :], start=False, stop=True,
                         reads=["Ks", "Vs"], writes=kS, sig=(g == G - 1 and h == 1))
            P.tt("gpsimd", Stmp, Sf, bc(GC[:, :, c:c + 1], [128, G, 64]), ALU.mult, reads=["Sf", "GC"], writes=["Stmp"])
            P.tt("vector", Sf, Stmp, psS, ALU.add, reads=["Stmp"] + kS, writes=["Sf"])
            P.cp("scalar", Sb, Sf, reads=["Sf"], writes=["Sb"])
            if q == 3:
                blk = c // 4
                tk0 = blk * 256
                ybk = [("yb", 0, qq) for qq in range(4)]
                y = yb[ybi]
                P.dma("sync", bonb, dr["BON"][fs, tk0:tk0 + 256].rearrange("(g p) t -> p g t", p=128), writes=["bonb"], semkey="lbon")
                P.dma("sync", gb, dr["Gt"][fs, tk0:tk0 + 256].rearrange("(g p) t -> p g t", p=128), writes=["gb"], semkey="lgb")
                P.act(ysq, y, AF.Square, reads=ybk, writes=["ysq", ("var", 0), ("var", 1)])
                for hh in range(2):
                    gs = slice(hh * 2, hh * 2 + 2)
                    for g in (2 * hh, 2 * hh + 1):
                        lo = (g % 2) * 256
                        P.mm(ps[:, 6, lo:lo + 256], C.bones, y[:, g, :], reads=ybk + ["bones"], writes=pk(6))
                        P.mm(ps[:, 7, lo:lo + 256], C.bones, ysq[:, g, :], reads=["ysq", "bones"], writes=pk(7))
                    pm_ = ps[:, 6, :].rearrange("p (g n) -> p g n", g=2)
                    pq_ = ps[:, 7, :].rearrange("p (g n) -> p g n", g=2)
                    P.act(mean[:, gs, :], pm_, AF.Copy, reads=pk(6), writes=[("mean", hh)], scale=1.0 / 64)
                    P.tt("gpsimd", m2[:, gs, :], mean[:, gs, :], mean[:, gs, :], ALU.mult, reads=[("mean", hh)], writes=[("m2", hh)])
                    P.stt("vector", var[:, gs, :], pq_, 1.0 / 64, m2[:, gs, :], ALU.mult, ALU.subtract,
                          reads=pk(7) + [("m2", hh)], writes=[("var", hh), "ysq"])
                    rsqrt(P, var[:, gs, :], var[:, gs, :], 1.0, 64e-5, [("var", hh)], [("var", hh)])
                    P.tt("gpsimd", mean[:, gs, :], y[:, gs, :], mean[:, gs, :], ALU.subtract, reads=ybk + [("mean", hh)], writes=[("mean", hh)])
                    P.tt("vector", mean[:, gs, :], mean[:, gs, :], var[:, gs, :], ALU.mult, reads=[("mean", hh), ("var", hh)], writes=[("mean", hh)])
                for g in range(G):
                    pc = p0 + g
                    P.act(m2[:, g, :], mean[:, g, :], AF.Identity, reads=[("mean", g // 2), "pv"], writes=[("m2", g // 2)],
                          scale=pv[:, PV_LNW, pc:pc + 1], bias=pv[:, PV_LNB, pc:pc + 1])
                P.tt("vector", m2, m2, bonb, ALU.add, reads=[("m2", 0), ("m2", 1), "bonb"], writes=[("m2", 0), ("m2", 1)])
                yi = blk % 2
                P.tt("gpsimd", ygb[yi], m2, gb, ALU.mult, reads=[("m2", 0), ("m2", 1), "gb"], writes=[("ygb", 0)])
                P.dma("sync", dr["ygT"][fs, tk0:tk0 + 256].rearrange("(g p) t -> p g t", p=128), ygb[yi], reads=[("ygb", 0)],
                      writes=[("dram", "ygT", grp, blk)], semkey="oyg")

        def halves(c):
            d = Deferred()
            front(d, c)
            L = d.calls
            m = len(L) // 2
            while m < len(L) and not L[m - 1][2].get("sig", True):
                m += 1
            return L[:m], L[m:]

        fr = {}
        fr[0] = halves(0)
        fr[1] = halves(1)
        replay_interleaved(P, [fr[0][0]])
        replay_interleaved(P, [fr[0][1], fr[1][0]])
        for c in range(32):
            lists = []
            db = Deferred()
            back(db, c)
            lists.append(db.calls)
            if c + 1 < 32:
                lists.append(fr[c + 1][1])
            if c + 2 < 32:
                fr[c + 2] = halves(c + 2)
                lists.append(fr[c + 2][0])
            replay_interleaved(P, lists)


def phase_A3(C):
    P, A, dr = C.P, C.A, C.dr
    TT = 512
    lhs = [A.tile([16, TT], BF16) for _ in range(2)]
    wps = [A.tile([16, 512], BF16) for _ in range(3)]
    wpks = ["wp0", "wp1", "wp2"]
    rt = [(A.tile([512], F32), "rt%d" % i) for i in range(3)]
    cnt = [0, 0]
    plan = [(dr["w_out"], 0, n * 512) for tt in range(T // TT) for n in range(4)]
    WS = WStream(C, plan, wps, wpks)
    for tt in range(T // TT):
        li = tt % 2
        P.dma("sync", lhs[li], dr["ygT"][:, tt * TT:(tt + 1) * TT].rearrange("(c p) t -> p c t", p=128),
              writes=[("lhsA3%d" % li, k) for k in range(16)], semkey="lyg%d" % li)
        proj_tm(C, lhs[li], "lhsA3%d" % li, 16, dr["w_out"], 4, WS, dr["x"], dr["h"], tt * TT, rt, "h", cnt)


def make_mlp(layer, dst_name):
    def phase(C):
        P, A, ps, dr = C.P, C.A, C.ps, C.dr
        TT = 512
        xt = A.tile([2048], F32)
        junk = A.tile([2048], BF16)
        ss = A.tile([4], F32)
        xnTs = [A.tile([16, TT], BF16) for _ in range(2)]
        h1T = A.tile([64, TT], BF16)
        wps = [A.tile([16, 512], BF16) for _ in range(3)]
        wpks = ["wp0", "wp1", "wp2"]
        rl = [A.tile([512], F32) for _ in range(2)]
        rt = [(A.tile([512], F32), "rt%d" % i) for i in range(3)]
        print("MLP arena words", A.off)
        cnt = [0, 0]
        wu = dr["up%d" % layer]
        wd = dr["dn%d" % layer]
        gi = PV_MLP0 + layer
        ec = 0
        plan = []
        for tt in range(T // TT):
            plan += [(wu, 0, fp * 512) for fp in range(16)]
            plan += [(wd, kp * 2048, n * 512) for n in range(4) for kp in range(4)]
        WS = WStream(C, plan, wps, wpks)

        def norm_tile(tt):
            for s in range(4):
                norm_T(C, dr["h"], tt * TT + s * 128, gi, xnTs[tt % 2], ("xnT", tt % 2), s * 128, xt, "xtM", junk, ss,
                       psbanks=(2, 3))

        norm_tile(0)
        for tt in range(T // TT):
            tok0 = tt * TT
            xnT = xnTs[tt % 2]
            xk = ("xnT", tt % 2)
            for fp in range(16):
                wp_, wpk_ = WS.get()
                for j in range(4):
                    fc = fp * 4 + j
                    bank = fc % 2
                    for k in range(16):
                        P.mm(ps[:, bank, :], wp_[:, k, j * 128:(j + 1) * 128], xnT[:, k, :], start=(k == 0), stop=(k == 15),
                             reads=[wpk_, (xk, k)], writes=pk(bank), sig=(k == 15))
                    ri = ec % 2
                    ec += 1
                    P.act(rl[ri], ps[:, bank, :], AF.Relu, reads=pk(bank), writes=[("rl", ri)])
                    P.tt("vector", h1T[:, fc, :], rl[ri], rl[ri], ALU.mult, reads=[("rl", ri)], writes=[("h1T", fc)])
                WS.done()
            streams = []
            wts = []
            C.P = dn = Deferred()
            proj_tm(C, h1T, "h1T", 64, wd, 4, WS, dr["h"], dr[dst_name], tok0, rt, dst_name, cnt)
            C.P = P
            streams.append(dn.calls)
            wts.append(1)
            if tt + 1 < T // TT:
                C.P = dq = Deferred()
                norm_tile(tt + 1)
                C.P = P
                streams.append(dq.calls)
                wts.append(8)
            replay_interleaved(P, streams, wts)
    return phase


def phase_K(C):
    P, A, ps, dr = C.P, C.A, C.ps, C.dr
    TT = 512
    xt = A.tile([2048], F32)
    junk = A.tile([2048], BF16)
    ss = A.tile([4], F32)
    xnT = A.tile([16, TT], BF16)
    wkv = A.tile([16, 512], BF16)
    kv = [A.tile([512], F32) for _ in range(2)]
    sqk = A.tile([4, 64], F32)
    ssum = A.tile([8], F32)
    kn = A.tile([4, 64], F32)
    ta = A.tile([4, 32], F32)
    tb = A.tile([4, 32], F32)
    kr = [A.tile([4, 64], BF16) for _ in range(2)]
    kTs = [A.tile([2, 128], BF16) for _ in range(2)]
    vsb = [A.tile([256], BF16) for _ in range(2)]
    load_panel(C, wkv, "wkv", dr["w_kv"], 0, 0)
    knb = bc(C.pvb[:, 64:128].unsqueeze(1), [128, 4, 64])
    for tt in range(T // TT):
        tok0 = tt * TT
        for s in range(4):
            norm_T(C, dr["h"], tok0 + s * 128, PV_KVN, xnT, "xnT", s * 128, xt, "xtK", junk, ss, psbanks=(6, 7))
        for s in range(4):
            gt = tt * 4 + s
            i2 = s % 2
            bank = s % 2
            for k in range(16):
                P.mm(ps[:, bank, :], xnT[:, k, s * 128:(s + 1) * 128], wkv[:, k, :], start=(k == 0), stop=(k == 15),
                     reads=["wkv", ("xnT", k)], writes=pk(bank), sig=(k == 15))
            P.cp("scalar", kv[i2], ps[:, bank, :], reads=pk(bank), writes=[("kv", i2)])
            k3 = kv[i2][:, 0:256].rearrange("p (h d) -> p h d", h=4)
            P.act(sqk, k3, AF.Square, reads=[("kv", i2)], writes=["sqk"])
            P.add("vector", lambda e: e.tensor_reduce(ssum[:, 0:4], sqk, AX.X, ALU.add), reads=["sqk"], writes=["ssum0"])
            rsqrt(P, ssum[:, 4:8], ssum[:, 0:4], 1.0 / 64, 1e-6, ["ssum0"], ["ssum1"])
            P.tt("vector", kn, k3, bc(ssum[:, 4:8].unsqueeze(2), [128, 4, 64]), ALU.mult, reads=[("kv", i2), "ssum1"], writes=["kn"])
            P.tt("gpsimd", kn, kn, knb, ALU.mult, reads=["kn", "pvb"], writes=["kn"])
            cosb = bc(C.cs[:, gt, 0:32].unsqueeze(1), [128, 4, 32])
            sinb = bc(C.cs[:, gt, 32:64].unsqueeze(1), [128, 4, 32])
            x1 = kn[:, :, 0:32]
            x2 = kn[:, :, 32:64]
            P.tt("vector", ta, x1, cosb, ALU.mult, reads=["kn", "cs"], writes=["ta"])
            P.tt("gpsimd", tb, x2, sinb, ALU.mult, reads=["kn", "cs"], writes=["tb"])
            P.tt("vector", kr[i2][:, :, 0:32], ta, tb, ALU.subtract, reads=["ta", "tb"], writes=[("kr", i2, 0)])
            P.tt("vector", ta, x2, cosb, ALU.mult, reads=["kn", "cs", ("kr", i2, 0)], writes=["ta"])
            P.tt("gpsimd", tb, x1, sinb, ALU.mult, reads=["kn", "cs", ("kr", i2, 0)], writes=["tb"])
            P.tt("vector", kr[i2][:, :, 32:64], ta, tb, ALU.add, reads=["ta", "tb"], writes=[("kr", i2, 1)])
            pT = ps[:, 2 + i2, 0:128].bitcast(BF16).rearrange("p (j t) -> p j t", j=2)
            for j in range(2):
                P.tr(pT[:, j, :], kr[i2][:, 2 * j:2 * j + 2, :].rearrange("p a b -> p (a b)"), C.identb,
                     reads=[("kr", i2, 0), ("kr", i2, 1), "identb"], writes=pk(2 + i2, 0, 128), sig=(j == 1))
            P.cp("vector", kTs[i2], pT, reads=pk(2 + i2, 0, 128), writes=[("kTs", i2)])
            P.dma("sync", dr["KTs"].rearrange("(j a) d t -> (a d) j t", a=2)[:, :, gt * 128:(gt + 1) * 128], kTs[i2],
                  reads=[("kTs", i2)], writes=[("dram", "KTs", gt)], semkey="oKTs%d" % i2)
            P.cp("scalar", vsb[i2], kv[i2][:, 256:512], reads=[("kv", i2)], writes=[("vsb", i2)])
            P.dma("sync", dr["Vt"][gt * 128:(gt + 1) * 128, :], vsb[i2], reads=[("vsb", i2)],
                  writes=[("dram", "Vt", gt)], semkey="oVt%d" % i2)


def phase_B(C):
    P, A, ps, dr = C.P, C.A, C.ps, C.dr
    TT = 256
    NS = TT // 128
    KTsb = A.tile([4, T], BF16)
    Vsb = A.tile([16, 256], BF16)
    onesb = A.tile([64], BF16)
    xt = A.tile([2048], F32)
    junk = A.tile([2048], BF16)
    ss = A.tile([4], F32)
    xnTs = [A.tile([16, TT], BF16) for _ in range(2)]
    q_tm = [A.tile([32, 64], F32) for _ in range(NS)]
    tmpq = A.tile([32, 64], F32)
    ssum = A.tile([64], F32)
    ta = A.tile([32, 32], F32)
    tb = A.tile([32, 32], F32)
    qr = A.tile([32, 64], BF16)
    QT = A.tile([16, 128], BF16)
    pf = [A.tile([512], F32) for _ in range(2)]
    PT = [A.tile([4, 128], BF16) for _ in range(4)]
    den = A.tile([4, 128], F32)
    oT = A.tile([16, TT], BF16)
    wps = [A.tile([16, 512], BF16) for _ in range(3)]
    wpks = ["wp0", "wp1", "wp2"]
    rt = [(A.tile([512], F32), "rt%d" % i) for i in range(2)]
    print("B arena words", A.off)
    plan = []
    for tt in range(T // TT):
        plan += [(dr["w_q"], 0, n * 512) for n in range(4)]
        plan += [(dr["w_o"], 0, n * 512) for n in range(4)]
    WS = WStream(C, plan, wps, wpks)
    for dup in range(2):
        P.dma("sync", KTsb[dup * 64:(dup + 1) * 64], dr["KTs"].rearrange("g d t -> d g t"), writes=["KTsb%d" % dup], semkey="lKTs%d" % dup)
    P.dma("sync", Vsb, dr["Vt"].rearrange("(b p) f -> p b f", p=128), writes=["Vsb"], semkey="lVsb")
    P.add("vector", lambda e: e.memset(onesb, 1.0), writes=["onesb"])
    qnb = bc(C.pvb[:, 0:64].unsqueeze(1), [128, 32, 64])
    cnt = [0, 0]
    pcnt = 0
    def norm_tile(tt):
        for s in range(NS):
            norm_T(C, dr["h"], tt * TT + s * 128, PV_BN, xnTs[tt % 2], ("xnT", tt % 2), s * 128, xt, "xtB", junk, ss,
                   psbanks=(6, 7))

    norm_tile(0)
    for tt in range(T // TT):
        tok0 = tt * TT
        xnT = xnTs[tt % 2]
        xk = ("xnT", tt % 2)
        for n in range(4):
            wp_, wpk_ = WS.get()
            for s in range(NS):
                bank = s
                for k in range(16):
                    P.mm(ps[:, bank, :], xnT[:, k, s * 128:(s + 1) * 128], wp_[:, k, :], start=(k == 0), stop=(k == 15),
                         reads=[wpk_, (xk, k)], writes=pk(bank), sig=(k == 15))
                dstq = q_tm[s].rearrange("p h d -> p (h d)")[:, n * 512:(n + 1) * 512]
                if (n + s) % 2 == 0:
                    P.cp("scalar", dstq, ps[:, bank, :], reads=pk(bank), writes=[("q_tm", s, n)])
                else:
                    P.cp("vector", dstq, ps[:, bank, :], reads=pk(bank), writes=[("q_tm", s, n)])
            WS.done()
        for s in range(NS):
            nb = tt * NS + s
            qk = [("q_tm", s, n) for n in range(4)]
            q3 = q_tm[s]
            P.act(tmpq, q3, AF.Square, reads=qk, writes=["tmpq"])
            P.add("vector", lambda e: e.tensor_reduce(ssum[:, 0:32], tmpq, AX.X, ALU.add), reads=["tmpq"], writes=["ssum0"])
            rsqrt(P, ssum[:, 32:64], ssum[:, 0:32], 1.0 / 64, 1e-6, ["ssum0"], ["ssum1"])
            P.tt("vector", tmpq, q3, bc(ssum[:, 32:64].unsqueeze(2), [128, 32, 64]), ALU.mult, reads=qk + ["ssum1", "tmpq"], writes=["tmpq"])
            P.tt("gpsimd", tmpq, tmpq, qnb, ALU.mult, reads=["tmpq", "pvb"], writes=["tmpq"])
            cosb = bc(C.cs[:, nb, 0:32].unsqueeze(1), [128, 32, 32])
            sinb = bc(C.cs[:, nb, 32:64].unsqueeze(1), [128, 32, 32])
            x1 = tmpq[:, :, 0:32]
            x2 = tmpq[:, :, 32:64]
            P.tt("vector", ta, x1, cosb, ALU.mult, reads=["tmpq", "cs"], writes=["ta"])
            P.tt("gpsimd", tb, x2, sinb, ALU.mult, reads=["tmpq", "cs"], writes=["tb"])
            P.tt("vector", qr[:, :, 0:32], ta, tb, ALU.subtract, reads=["ta", "tb"], writes=["qr0"])
            P.tt("vector", ta, x2, cosb, ALU.mult, reads=["tmpq", "cs", "qr0"], writes=["ta"])
            P.tt("gpsimd", tb, x1, sinb, ALU.mult, reads=["tmpq", "cs", "qr0"], writes=["tb"])
            P.tt("vector", qr[:, :, 32:64], ta, tb, ALU.add, reads=["ta", "tb"], writes=["qr1"])
            for hb_ in range(2):
                pT = ps[:, 6 + hb_, :].bitcast(BF16).rearrange("p (c t) -> p c t", c=8)
                for cc in range(8):
                    c = hb_ * 8 + cc
                    P.tr(pT[:, cc, :], qr[:, 2 * c:2 * c + 2, :].rearrange("p a b -> p (a b)"), C.identb,
                         reads=["qr0", "qr1", "identb"], writes=pk(6 + hb_), sig=(cc == 7))
                if hb_ == 0:
                    P.cp("scalar", QT[:, 0:8, :], pT, reads=pk(6), writes=[("QT", 0)])
                else:
                    P.cp("vector", QT[:, 8:16, :], pT, reads=pk(7), writes=[("QT", 1)])
            for g in range(4):
                js = [1] if nb == 0 else [0, 1]
                pts = {}
                for hp_ in range(2):
                    hp = slice(hp_ * 64, (hp_ + 1) * 64)
                    for j in js:
                        kb = nb - 1 + j
                        bank = pcnt % 4
                        pcnt += 1
                        P.mm(ps[:, bank, :], KTsb[hp, g, kb * 128:(kb + 1) * 128],
                             QT[hp, 4 * g:4 * g + 4, :].rearrange("p c t -> p (c t)"),
                             reads=["KTsb%d" % hp_, ("QT", g // 2)], writes=pk(bank))
                        pfi = pcnt % 2
                        P.act(pf[pfi], ps[:, bank, :], AF.Exp, reads=pk(bank), writes=[("pf", pfi)], scale=0.125)
                        pti = hp_ * 2 + j
                        msk = C.am[:, 0, :] if j == 1 else C.am[:, 1, :]
                        eng = "vector" if pcnt % 2 == 0 else "gpsimd"
                        P.tt(eng, PT[pti], pf[pfi].rearrange("p (c t) -> p c t", c=4), bc(msk.unsqueeze(1), [128, 4, 128]),
                             ALU.mult, reads=[("pf", pfi), "am"], writes=[("PT", pti)])
                        pts[(hp_, j)] = pti
                psO = ps[:, 4, :].rearrange("p (c t) -> p c t", c=4)
                psD = ps[:, 5, :].rearrange("p (c t) -> p c t", c=4)
                for hp_ in range(2):
                    hp = slice(hp_ * 64, (hp_ + 1) * 64)
                    fo = True
                    for j in js:
                        kb = nb - 1 + j
                        pti = pts[(hp_, j)]
                        for ci in range(4):
                            lastmm = (hp_ == 1 and j == js[-1] and ci == 3)
                            P.mm(psO[hp, ci, :], Vsb[:, kb, g * 64:(g + 1) * 64], PT[pti][:, ci, :], start=fo, stop=True,
                                 reads=["Vsb", ("PT", pti)], writes=pk(4), sig=False)
                            P.mm(psD[hp, ci, :], onesb, PT[pti][:, ci, :], start=fo, stop=True,
                                 reads=["onesb", ("PT", pti)], writes=pk(5), sig=lastmm)
                            fo = False
                P.tt("vector", den, psD, bc(C.esk[:, 4 * g:4 * g + 4].unsqueeze(2), [128, 4, 128]), ALU.add,
                     reads=pk(5) + ["esk"], writes=["den"])
                P.add("vector", lambda e: e.reciprocal(den, den), reads=["den"], writes=["den"])
                P.tt("vector", oT[:, 4 * g:4 * g + 4, s * 128:(s + 1) * 128], psO, den, ALU.mult,
                     reads=pk(4) + ["den"], writes=[("oT", 4 * g + i) for i in range(4)])
        streams = []
        wts = []
        C.P = dn = Deferred()
        proj_tm(C, oT, "oT", 16, dr["w_o"], NS, WS, dr["h"], dr["out"], tok0, rt, "out", cnt)
        C.P = P
        streams.append(dn.calls)
        wts.append(1)
        if tt + 1 < T // TT:
            C.P = dq = Deferred()
            norm_tile(tt + 1)
            C.P = P
            streams.append(dq.calls)
            wts.append(10)
        replay_interleaved(P, streams, wts)


ALL_PHASES = None


def get_phases():
    return [phase_A1, phase_A2, phase_A3, make_mlp(0, "h"), phase_K, phase_B_pre, make_mlp(1, "out")]


def phase_B_pre(C):
    dr = C.dr
    o = dr["out"]
    dr["out"] = dr["h"]
    phase_B(C)
    dr["out"] = o


def host_consts():
    f = np.float32
    ident = np.eye(128, dtype=f)
    bones = np.zeros((128, 128), f)
    bones[:64, :64] = 1
    bones[64:, 64:] = 1
    p = np.arange(128)[:, None] % 64
    t = np.arange(64)[None, :]
    m64 = np.concatenate([(t > p), (t >= p), (t < p)], axis=1).astype(f)
    pos = np.arange(T, dtype=f)
    half = 32
    inv = np.power(f(10000.0), -np.arange(half, dtype=f) / f(half)).astype(f)
    ang = (pos[:, None] * inv[None, :]).astype(f)
    cs = np.concatenate([np.cos(ang), np.sin(ang)], axis=1).astype(f)
    cs = cs.reshape(16, 128, 64).transpose(1, 0, 2).reshape(128, 16 * 64)
    s = np.arange(128)[:, None]
    q = np.arange(128)[None, :]
    am = np.concatenate([(s <= q), (s > q)], axis=1).astype(f)
    return dict(c_ident=ident, c_bones=bones, c_m64=np.ascontiguousarray(m64), c_cs=np.ascontiguousarray(cs), c_am=np.ascontiguousarray(am))


def host_inputs(inp):
    f = np.float32
    g = lambda k: np.asarray(inp[k], dtype=f)
    vecs = [g("a_norm")[0]] + [g("a_mix")[0, i] for i in range(6)] + [g("a_w0")[0], g("a_a0")[0], g("a_k_k")[0], g("a_k_a")[0],
            g("a_r_k")[0].reshape(-1), g("a_ln_x_w")[0], g("a_ln_x_b")[0], g("mlp_norm")[0], g("mlp_norm")[1], g("kv_norm"),
            g("b_norm")[0]]
    pv = np.zeros((128, NV, 16), f)
    for i, v in enumerate(vecs):
        pv[:, i, :] = v.reshape(16, 128).T
    sk = g("b_sinks")[0]
    for c in range(16):
        pv[:64, PV_SK, c] = sk[2 * c]
        pv[64:, PV_SK, c] = sk[2 * c + 1]
    pvb = np.zeros((128, 128), f)
    pvb[:, 0:64] = g("b_q_norm")[0][None, :]
    pvb[:, 64:128] = g("k_norm")[None, :]
    wr = g("a_w_rkv")[0]
    shared = dict(
        w_r=wr[0], w_k=wr[1], w_v=wr[2], w_out=g("a_w_out")[0], w_q=g("b_w_q")[0], w_o=g("b_w_o")[0],
        w1=g("a_w1")[0], w2=g("a_w2")[0], a1=g("a_a1")[0], a2=g("a_a2")[0], g1=g("a_g1")[0], g2=g("a_g2")[0],
        up0=g("mlp_w_up")[0], up1=g("mlp_w_up")[1], dn0=g("mlp_w_down")[0], dn1=g("mlp_w_down")[1],
        w_kv=g("w_kv"), pv=pv.reshape(128, NV * 16), pvb=pvb)
    shared.update(host_consts())
    shared = {k: np.ascontiguousarray(v) for k, v in shared.items()}
    return shared


_CACHE = {}


def kernel(**inputs):
    shared = host_inputs(inputs)
    x = np.asarray(inputs["x"], dtype=np.float32)
    if "nc" not in _CACHE:
        _CACHE["nc"] = build(get_phases())
    nc, C = _CACHE["nc"]
    in_maps = []
    for b in range(NCORES):
        m = dict(shared)
        m["x"] = np.ascontiguousarray(x[b])
        in_maps.append(m)
    res = run_bass_kernel_spmd(nc, in_maps, core_ids=list(range(NCORES)))
    return np.stack([res.results[b]["out"] for b in range(NCORES)], axis=0).astype(np.float32)
```
